# Optimizing a Trainium2 kernel written in Bass

```python
import jax, jax.numpy as jnp
from jax import lax
import numpy as np

D_MODEL = 2048
BATCH = 1
SEQ = 16384
DEPTH = 1
DEC_BATCH = 8
DEC_SEQ = 2048
PAST_LEN = 128

N_META = 16
GRID_W = 64
EPS = 1e-6
HEAD_DIM = 64
ATT_WIDTH = D_MODEL // 2
N_Q_HEADS = ATT_WIDTH // HEAD_DIM
N_KV_HEADS = 4
Q_PER_KV = N_Q_HEADS // N_KV_HEADS
Q_BLOCK = 128
ROPE_THETA = 10000.0
SSM_WIDTH = D_MODEL - ATT_WIDTH
SSM_HEAD_DIM = 64
N_SSM_HEADS = SSM_WIDTH // SSM_HEAD_DIM
N_SSM_GROUPS = 2
D_STATE = 128
D_CONV = 5
CHUNK = 128
CONV_DIM = SSM_WIDTH + 2 * N_SSM_GROUPS * D_STATE
MIX_WIDTH = ATT_WIDTH + SSM_WIDTH
D_FF = 5632
KV_WIDTH = N_KV_HEADS * HEAD_DIM
IN_PROJ = ATT_WIDTH + 2 * KV_WIDTH + SSM_WIDTH + CONV_DIM + 2 * N_SSM_HEADS
SPLIT_POINTS = [ATT_WIDTH, ATT_WIDTH + KV_WIDTH, ATT_WIDTH + 2 * KV_WIDTH,
                ATT_WIDTH + 2 * KV_WIDTH + SSM_WIDTH,
                ATT_WIDTH + 2 * KV_WIDTH + SSM_WIDTH + CONV_DIM]

kernel_name = "hymba_bidir_attn_ssd_macaron_encoder"


def rms_norm(x, g):
    xf = x.astype(jnp.float32)
    y = xf * lax.rsqrt(jnp.mean(xf * xf, axis=-1, keepdims=True) + EPS)
    return (y * g.astype(jnp.float32)).astype(x.dtype)


def swiglu(u, w_gate, w_up, w_down):
    return (jax.nn.silu(u @ w_gate) * (u @ w_up)) @ w_down


def axial_rope_tables(n_tok):
    rows = n_tok // GRID_W
    row = jnp.repeat(jnp.arange(rows), GRID_W).astype(jnp.float32)
    col = jnp.tile(jnp.arange(GRID_W), rows).astype(jnp.float32)
    n_freq = HEAD_DIM // 4
    inv_freq = ROPE_THETA ** (-jnp.arange(n_freq, dtype=jnp.float32) / n_freq)
    ang = jnp.stack([row[:, None] * inv_freq, col[:, None] * inv_freq], axis=1)
    ang = jnp.broadcast_to(ang[:, :, None, :], (n_tok, 2, 2, n_freq)).reshape(n_tok, HEAD_DIM)
    ang = jnp.concatenate([jnp.zeros((N_META, HEAD_DIM), jnp.float32), ang], axis=0)
    return jnp.cos(ang), jnp.sin(ang)


def apply_axial_rope(x, cos, sin):
    xs = x.reshape(x.shape[:-1] + (2, 2, HEAD_DIM // 4))
    rot = jnp.stack([-xs[..., 1, :], xs[..., 0, :]], axis=-2).reshape(x.shape)
    return x * cos[None, :, None, :] + rot * sin[None, :, None, :]


def attention_group(q, k, v, cos, sin, q_g, k_g):
    b, L = q.shape[:2]
    q = apply_axial_rope(rms_norm(q, q_g).astype(jnp.float32), cos, sin) * (HEAD_DIM ** -0.5)
    k = apply_axial_rope(rms_norm(k, k_g).astype(jnp.float32), cos, sin).astype(v.dtype)
    q = q.astype(v.dtype).reshape(b, L, N_KV_HEADS, Q_PER_KV, HEAD_DIM)

    def attend(qb):
        s = jnp.einsum('bqkgd,bskd->bkgqs', qb, k, preferred_element_type=jnp.float32)
        p = jax.nn.softmax(s, axis=-1).astype(v.dtype)
        return jnp.einsum('bkgqs,bskd->bqkgd', p, v)

    o_meta = attend(q[:, :N_META])
    n_blk = (L - N_META) // Q_BLOCK
    qb = q[:, N_META:].reshape(b, n_blk, Q_BLOCK, N_KV_HEADS, Q_PER_KV, HEAD_DIM).swapaxes(0, 1)
    o = lax.map(attend, qb).swapaxes(0, 1).reshape(b, L - N_META, N_KV_HEADS, Q_PER_KV, HEAD_DIM)
    return jnp.concatenate([o_meta, o], axis=1).reshape(b, L, ATT_WIDTH)


def centred_dwconv(u, w, bias):
    pad = D_CONV // 2
    L = u.shape[1]
    up = jnp.pad(u, ((0, 0), (pad, pad), (0, 0)))
    out = bias
    for j in range(D_CONV):
        out = out + up[:, j:j + L] * w[j]
    return out


def ssd_scan(x, dt, A, Bm, Cm):
    b, l, h, p = x.shape
    g, n = Bm.shape[2], Bm.shape[3]
    r = h // g
    c = l // CHUNK
    xr = (x * dt[..., None]).reshape(b, c, CHUNK, g, r, p)
    a = (dt * A).reshape(b, c, CHUNK, g, r).transpose(0, 3, 4, 1, 2)
    a_cum = jnp.cumsum(a, axis=-1)
    Bc = Bm.reshape(b, c, CHUNK, g, n)
    Cc = Cm.reshape(b, c, CHUNK, g, n)
    causal = jnp.tril(jnp.ones((CHUNK, CHUNK), dtype=bool))
    seg = a_cum[..., :, None] - a_cum[..., None, :]
    Lmat = jnp.exp(jnp.where(causal, seg, -jnp.inf))
    CB = jnp.einsum('bclgn,bcsgn->bcgls', Cc, Bc)
    y_diag = jnp.einsum('bcgls,bgrcls,bcsgrp->bclgrp', CB, Lmat, xr)
    decay_states = jnp.exp(a_cum[..., -1:] - a_cum)
    states = jnp.einsum('bcsgn,bgrcs,bcsgrp->bcgrpn', Bc, decay_states, xr)
    chunk_decay = jnp.exp(a_cum[..., -1])

    def step(carry, inp):
        s_c, d_c = inp
        return carry * d_c[..., None, None] + s_c, carry

    init = jnp.zeros((b, g, r, p, n), jnp.float32)
    _, prev = lax.scan(step, init, (jnp.moveaxis(states, 1, 0), jnp.moveaxis(chunk_decay, -1, 0)))
    prev = jnp.moveaxis(prev, 0, 1)
    y_off = jnp.einsum('bclgn,bcgrpn,bgrcl->bclgrp', Cc, prev, jnp.exp(a_cum))
    return (y_diag + y_off).reshape(b, l, h, p)


def ssd_group(z, xbc, dt_raw, conv_w, conv_b, a_log, dt_bias, d_skip, norm_g):
    b, L = z.shape[:2]
    xbc = jax.nn.silu(centred_dwconv(xbc, conv_w, conv_b)).astype(jnp.float32)
    xs, Bm, Cm = jnp.split(xbc, [SSM_WIDTH, SSM_WIDTH + N_SSM_GROUPS * D_STATE], axis=-1)
    xs = xs.reshape(b, L, N_SSM_HEADS, SSM_HEAD_DIM)
    Bm = Bm.reshape(b, L, N_SSM_GROUPS, D_STATE)
    Cm = Cm.reshape(b, L, N_SSM_GROUPS, D_STATE)
    dt = jax.nn.softplus(dt_raw.astype(jnp.float32).reshape(b, L, 2, N_SSM_HEADS) + dt_bias.astype(jnp.float32))
    A = -jnp.exp(a_log.astype(jnp.float32))
    lead = CHUNK - N_META
    padf = lambda t: jnp.pad(t, ((0, 0), (lead, 0)) + ((0, 0),) * (t.ndim - 2))
    xp, Bp, Cp, dtp = padf(xs), padf(Bm), padf(Cm), padf(dt)
    flip = lambda t: jnp.flip(t, axis=1)
    y_f = ssd_scan(xp, dtp[:, :, 0], A[0], Bp, Cp)
    y_b = flip(ssd_scan(flip(xp), flip(dtp[:, :, 1]), A[1], flip(Bp), flip(Cp)))
    y = (y_f + y_b)[:, lead:] + d_skip.astype(jnp.float32)[:, None] * xs
    y = y.reshape(b, L, SSM_WIDTH) * jax.nn.silu(z.astype(jnp.float32))
    y = rms_norm(y.reshape(b, L, N_SSM_GROUPS, SSM_WIDTH // N_SSM_GROUPS),
                 norm_g.reshape(N_SSM_GROUPS, SSM_WIDTH // N_SSM_GROUPS))
    return y.reshape(b, L, SSM_WIDTH).astype(z.dtype)


def encoder_layer(h, cos, sin, p):
    b, L, _ = h.shape
    u = rms_norm(h, p['ff1_norm_pre'])
    h = h + 0.5 * rms_norm(swiglu(u, p['ff1_w_gate'], p['ff1_w_up'], p['ff1_w_down']), p['ff1_norm_post'])
    u = rms_norm(h, p['mix_norm_pre'])
    q, k, v, z, xbc, dt_raw = jnp.split(u @ p['w_in'], SPLIT_POINTS, axis=-1)
    q = q.reshape(b, L, N_Q_HEADS, HEAD_DIM)
    k = k.reshape(b, L, N_KV_HEADS, HEAD_DIM)
    v = v.reshape(b, L, N_KV_HEADS, HEAD_DIM)
    o_att = attention_group(q, k, v, cos, sin, p['q_norm'], p['k_norm'])
    o_ssm = ssd_group(z, xbc, dt_raw, p['conv_w'], p['conv_b'], p['a_log'], p['dt_bias'], p['d_skip'], p['ssm_norm'])
    mix = jnp.concatenate([o_att, o_ssm], axis=-1) @ p['w_out']
    h = h + rms_norm(mix, p['mix_norm_post'])
    u = rms_norm(h, p['ff2_norm_pre'])
    h = h + 0.5 * rms_norm(swiglu(u, p['ff2_w_gate'], p['ff2_w_up'], p['ff2_w_down']), p['ff2_norm_post'])
    return h


def run_trunk(x, meta_tokens, layers):
    b, n_tok, _ = x.shape
    cos, sin = axial_rope_tables(n_tok)
    meta = jnp.broadcast_to(meta_tokens.astype(x.dtype)[None], (b, N_META, D_MODEL))
    h = jnp.concatenate([meta, x], axis=1)
    for i in range(DEPTH):
        h = encoder_layer(h, cos, sin, {name: w[i] for name, w in layers.items()})
    return h[:, N_META:]


def setup_inputs(seed: int = 0) -> dict:
    key = jax.random.key(seed)
    ks = jax.random.split(key, 32)
    f32 = jnp.float32
    nrm = lambda k, shape, scale: jax.random.normal(k, shape, f32) * scale
    gain = lambda k, shape: 1.0 + 0.05 * jax.random.normal(k, shape, f32)
    dt0 = jnp.exp(jax.random.uniform(ks[12], (DEPTH, 2, N_SSM_HEADS), f32, np.log(1e-3), np.log(1e-1)))
    return {
        'x_prompt': jax.random.normal(ks[0], (BATCH, SEQ, D_MODEL), f32),
        'x_sample': jax.random.normal(ks[1], (DEC_BATCH, DEC_SEQ, D_MODEL), f32),
        'meta_tokens': nrm(ks[2], (N_META, D_MODEL), 1.0),
        'ff1_norm_pre': gain(ks[3], (DEPTH, D_MODEL)),
        'ff1_w_gate': nrm(ks[4], (DEPTH, D_MODEL, D_FF), D_MODEL ** -0.5),
        'ff1_w_up': nrm(ks[5], (DEPTH, D_MODEL, D_FF), D_MODEL ** -0.5),
        'ff1_w_down': nrm(ks[6], (DEPTH, D_FF, D_MODEL), D_FF ** -0.5),
        'ff1_norm_post': gain(ks[7], (DEPTH, D_MODEL)),
        'mix_norm_pre': gain(ks[8], (DEPTH, D_MODEL)),
        'w_in': nrm(ks[9], (DEPTH, D_MODEL, IN_PROJ), D_MODEL ** -0.5),
        'conv_w': nrm(ks[10], (DEPTH, D_CONV, CONV_DIM), D_CONV ** -0.5),
        'conv_b': nrm(ks[11], (DEPTH, CONV_DIM), 0.02),
        'a_log': jnp.log(jax.random.uniform(ks[13], (DEPTH, 2, N_SSM_HEADS), f32, 1.0, 16.0)),
        'dt_bias': dt0 + jnp.log(-jnp.expm1(-dt0)),
        'd_skip': gain(ks[14], (DEPTH, N_SSM_HEADS)),
        'q_norm': gain(ks[15], (DEPTH, HEAD_DIM)),
        'k_norm': gain(ks[16], (DEPTH, HEAD_DIM)),
        'ssm_norm': gain(ks[17], (DEPTH, SSM_WIDTH)),
        'w_out': nrm(ks[18], (DEPTH, MIX_WIDTH, D_MODEL), MIX_WIDTH ** -0.5),
        'mix_norm_post': gain(ks[19], (DEPTH, D_MODEL)),
        'ff2_norm_pre': gain(ks[20], (DEPTH, D_MODEL)),
        'ff2_w_gate': nrm(ks[21], (DEPTH, D_MODEL, D_FF), D_MODEL ** -0.5),
        'ff2_w_up': nrm(ks[22], (DEPTH, D_MODEL, D_FF), D_MODEL ** -0.5),
        'ff2_w_down': nrm(ks[23], (DEPTH, D_FF, D_MODEL), D_FF ** -0.5),
        'ff2_norm_post': gain(ks[24], (DEPTH, D_MODEL)),
    }


def reference(x_prompt, x_sample, meta_tokens, ff1_norm_pre, ff1_w_gate, ff1_w_up, ff1_w_down, ff1_norm_post,
              mix_norm_pre, w_in, conv_w, conv_b, a_log, dt_bias, d_skip, q_norm, k_norm, ssm_norm, w_out,
              mix_norm_post, ff2_norm_pre, ff2_w_gate, ff2_w_up, ff2_w_down, ff2_norm_post):
    layers = {
        'ff1_norm_pre': ff1_norm_pre, 'ff1_w_gate': ff1_w_gate, 'ff1_w_up': ff1_w_up,
        'ff1_w_down': ff1_w_down, 'ff1_norm_post': ff1_norm_post,
        'mix_norm_pre': mix_norm_pre, 'w_in': w_in, 'conv_w': conv_w, 'conv_b': conv_b,
        'a_log': a_log, 'dt_bias': dt_bias, 'd_skip': d_skip, 'q_norm': q_norm, 'k_norm': k_norm,
        'ssm_norm': ssm_norm, 'w_out': w_out, 'mix_norm_post': mix_norm_post,
        'ff2_norm_pre': ff2_norm_pre, 'ff2_w_gate': ff2_w_gate, 'ff2_w_up': ff2_w_up,
        'ff2_w_down': ff2_w_down, 'ff2_norm_post': ff2_norm_post,
    }
    y_prompt = run_trunk(x_prompt, meta_tokens, layers)
    y_sample = run_trunk(x_sample, meta_tokens, layers)
    return (y_prompt, y_sample)
```

```python
import contextlib
import numpy as np
import ml_dtypes
import concourse.bass as bass
import concourse.mybir as mybir
from concourse.bass_utils import run_bass_kernel_spmd

import concourse.bass as bass
import concourse.mybir as mybir

F32 = mybir.dt.float32
BF16 = mybir.dt.bfloat16
AF = mybir.ActivationFunctionType
ALU = mybir.AluOpType
AX = mybir.AxisListType

N_DMA_SEMS = 56
N_SP_SEMS = 36


class Op:
    __slots__ = ("eng", "fn", "deps", "needs_inc", "sem", "semval", "ndma", "idx")


class Prog:
    ENGS = ("pe", "act", "dve", "pool", "sp")

    def __init__(self, nc):
        self.nc = nc
        self.ops = {e: [] for e in self.ENGS}
        self.last_w = {}
        self.readers = {}
        self.dma_rr = 0
        self.dma_rr2 = 0
        self.dma_sem_last = [None] * N_DMA_SEMS
        self.dma_sem_cnt = [0] * N_DMA_SEMS
        self.out_dmas = []

    def op(self, eng, fn, reads=(), writes=(), ndma=0, is_output=False):
        o = Op()
        o.eng = eng
        o.fn = fn
        o.deps = []
        o.needs_inc = False
        o.sem = None
        o.semval = 0
        o.ndma = ndma
        deps = {}
        for k in reads:
            w = self.last_w.get(k)
            if w is not None:
                deps[id(w)] = w
        for k in writes:
            w = self.last_w.get(k)
            if w is not None:
                deps[id(w)] = w
            for r in self.readers.get(k, {}).values():
                if isinstance(r, list):
                    for rr in r:
                        deps[id(rr)] = rr
                else:
                    deps[id(r)] = r
        if ndma:
            if eng == "sp":
                i = self.dma_rr % N_SP_SEMS
                self.dma_rr += 1
            else:
                i = N_SP_SEMS + self.dma_rr2 % (N_DMA_SEMS - N_SP_SEMS)
                self.dma_rr2 += 1
            prev = self.dma_sem_last[i]
            if prev is not None:
                deps[id(prev)] = prev
            self.dma_sem_cnt[i] += 16 * ndma
            o.sem = ("dma", i)
            o.semval = self.dma_sem_cnt[i]
            self.dma_sem_last[i] = o
            o.needs_inc = True
        for d in deps.values():
            if d is o:
                continue
            if d.eng == "pe" and eng == "pe":
                continue
            o.deps.append(d)
            d.needs_inc = True
        for k in reads:
            rd = self.readers.setdefault(k, {})
            if ndma:
                rd.setdefault("dma", []).append(o)
            else:
                rd[eng] = o
        for k in writes:
            self.last_w[k] = o
            self.readers[k] = {}
        self.ops[eng].append(o)
        if is_output:
            self.out_dmas.append(o)
        return o

    def barrier(self):
        lasts = []
        for e in ("pe", "act", "dve", "pool"):
            for o in reversed(self.ops[e]):
                if o.fn is not None and not o.ndma:
                    lasts.append(o)
                    break
        dl = [o for o in self.dma_sem_last if o is not None]
        for e in self.ENGS:
            b = Op()
            b.eng = e; b.fn = None; b.needs_inc = False; b.sem = None; b.semval = 0; b.ndma = 0
            b.deps = [o for o in lasts if o.eng != e] + dl
            for d in b.deps:
                d.needs_inc = True
            self.ops[e].append(b)
        self.last_w = {}
        self.readers = {}

    def dma(self, out, in_, reads=(), writes=(), is_output=False, **kw):
        return self.op("sp", lambda e: [e.dma_start(out=out, in_=in_, **kw)], reads, writes,
                       ndma=1, is_output=is_output)

    def emit(self):
        nc = self.nc
        fin = Op()
        fin.eng = "sp"; fin.fn = None; fin.deps = list(self.out_dmas); fin.needs_inc = False
        fin.sem = None; fin.semval = 0; fin.ndma = 0
        self.ops["sp"].append(fin)
        for e in self.ENGS:
            c = 0
            for o in self.ops[e]:
                if o.ndma:
                    continue
                if o.needs_inc:
                    c += 1
                    o.sem = ("eng", e)
                    o.semval = c
        import contextlib
        with contextlib.ExitStack() as st:
            sems = {}
            for e in self.ENGS:
                sems[("eng", e)] = st.enter_context(nc.semaphore("s_" + e))
            for i in range(N_DMA_SEMS):
                sems[("dma", i)] = st.enter_context(nc.semaphore("s_dma%d" % i))
            block = st.enter_context(nc.Block())

            def run(ename):
                def body(engine):
                    known = {}
                    for o in self.ops[ename]:
                        for d in o.deps:
                            if known.get(d.sem, 0) < d.semval:
                                engine.wait_ge(sems[d.sem], d.semval)
                                known[d.sem] = d.semval
                        if o.fn is None:
                            continue
                        r = o.fn(engine)
                        if o.ndma:
                            for ins in r:
                                ins.then_inc(sems[o.sem], 16)
                        elif o.needs_inc:
                            ins = r[-1] if isinstance(r, (list, tuple)) else r
                            ins.then_inc(sems[o.sem], 1)
                return body

            block.tensor(run("pe"))
            block.scalar(run("act"))
            block.vector(run("dve"))
            block.gpsimd(run("pool"))
            block.sync(run("sp"))


class Rot:
    def __init__(self, name, aps):
        self.name = name; self.aps = aps; self.i = 0

    def next(self):
        k = self.i % len(self.aps)
        self.i += 1
        return "%s%d" % (self.name, k), self.aps[k]


D = 2048
KC = 16
EPS = 1e-6
N_META = 16
ATT_W = 1024
SSM_W = 1024
CONV_DIM = 1536
NTM = 2592
IN_PROJ = 4128


class Cfg:
    def __init__(self, dff=5632, ls=2048, ncores=8, do_w=True, do_a=True, do_b=True, do_c=True, debug=False):
        self.dff = dff
        self.ls = ls
        self.ncores = ncores
        self.nts = ls // 512
        self.rp = ncores * ls + 128
        self.rs0 = self.rp
        self.rtot = self.rp + ls + 128
        self.do_w, self.do_a, self.do_b, self.do_c = do_w, do_a, do_b, do_c
        self.debug = debug
        self.b_phases = ("b0", "b1", "b2", "b3")


class Builder:
    def __init__(self, cfg):
        self.cfg = cfg
        self.nc = bass.Bass("TRN2", target_bir_lowering=False)
        self.P = Prog(self.nc)

    def salloc(self, name, shape, dtype):
        assert shape[0] == 128
        n = 1
        for d in shape[1:]:
            n *= d
        esz = 4 if dtype == F32 else 2
        words = (n * esz + 3) // 4
        words = (words + 7) // 8 * 8
        off = self.sb_off
        self.sb_off += words
        assert self.sb_off <= self.sb_words, (name, self.sb_off, self.sb_words)
        v = self.SB[:, off:off + words]
        if dtype != F32:
            v = v.bitcast(dtype)
        v = v[:, 0:n]
        if len(shape) == 3:
            v = v.rearrange("p (a b) -> p a b", a=shape[1])
        return v

    def declare(self):
        nc, cfg = self.nc, self.cfg
        dff = cfg.dff
        I = lambda n, s, d=F32: nc.dram_tensor(n, s, d, kind="ExternalInput").ap()
        S = lambda n, s, d: nc.dram_tensor(n, s, d, kind=("ExternalOutput" if cfg.debug else "Internal")).ap()
        self.xin = I("xin", [cfg.rtot, D])
        self.wf = {
            "g1": I("ff1_wg", [D, dff]), "u1": I("ff1_wu", [D, dff]), "d1": I("ff1_wd", [dff, D]),
            "win": I("w_in_p", [D, IN_PROJ]), "wout": I("w_out_p", [D, D]),
            "g2": I("ff2_wg", [D, dff]), "u2": I("ff2_wu", [D, dff]), "d2": I("ff2_wd", [dff, D]),
        }
        self.gains = {k: I(k, [D]) for k in ("ff1_pre", "ff1_post", "mix_pre", "mix_post", "ff2_pre", "ff2_post")}
        self.ident_in = I("ident", [128, 128])
        self.yout = nc.dram_tensor("yout", [2 * cfg.ls, D], F32, kind="ExternalOutput").ap()
        self.own_rows = [(0, 0), (cfg.rs0, cfg.ls)]
        self.wb = {k: S("wb_" + k, list(v.shape), BF16) for k, v in self.wf.items()}
        self.h1 = S("h1", [cfg.rtot, D], F32)
        self.proj = S("proj", [cfg.rtot, NTM], F32)
        self.xbcT = S("xbcT", [CONV_DIM, cfg.rtot], F32)
        if cfg.do_b:
            self.oT = S("oT", [D, 2 * cfg.ls], BF16)
        else:
            self.oT = I("oT", [D, 2 * cfg.ls], BF16)
        self.declare_b()

    def phase_w(self, st):
        nc, P = self.nc, self.P
        sb = self.salloc
        stg = Rot("wstg", [sb("wstg%d" % i, [128, 4096], F32) for i in range(3)])
        cvt = Rot("wcvt", [sb("wcvt%d" % i, [128, 4096], BF16) for i in range(3)])
        engs = ["pool", "dve", "act"]
        n = 0
        for name, w in self.wf.items():
            K, N = w.shape
            per = K * N // 128
            src = w.rearrange("(p r) n -> p (r n)", p=128)
            dst = self.wb[name].rearrange("(p r) n -> p (r n)", p=128)
            for c0 in range(0, per, 4096):
                sz = min(4096, per - c0)
                sk, sap = stg.next()
                ck, cap = cvt.next()
                P.dma(sap[:, 0:sz], src[:, c0:c0 + sz], writes=[sk])
                e = engs[n % 3]
                n += 1
                if e == "act":
                    P.op("act", lambda en, o=cap[:, 0:sz], i=sap[:, 0:sz]: en.activation(out=o, in_=i, func=AF.Copy),
                         reads=[sk], writes=[ck])
                else:
                    P.op(e, lambda en, o=cap[:, 0:sz], i=sap[:, 0:sz]: en.tensor_copy(out=o, in_=i),
                         reads=[sk], writes=[ck])
                P.op("pool", lambda en, o=dst[:, c0:c0 + sz], i=cap[:, 0:sz]: [en.dma_start(out=o, in_=i)],
                     reads=[ck], writes=[("wbd", name, c0)], ndma=1)

    def alloc_ac(self, st, with_stage=True):
        nc = self.nc
        sb = self.salloc
        self.xt = Rot("xt", [sb("xt%d" % i, [128, D], F32) for i in range(5)])
        self.ub = [sb("ub%d" % i, [128, KC, 512], BF16) for i in range(2)]
        self.hmid = sb("hmid", [128, 44, 512], BF16)
        self.wbuf = Rot("wbuf", [sb("wbuf%d" % i, [128, 4096], BF16) for i in range(6)])
        self.un = Rot("un", [sb("un%d" % i, [128, D], BF16) for i in range(4)])
        self.junk = sb("junk", [128, D], BF16)
        self.sg = Rot("sg", [sb("sg%d" % i, [128, 512], F32) for i in range(2)])
        self.small = Rot("small", [sb("small%d" % i, [128, 4], F32) for i in range(4)])
        self.ident = sb("identb", [128, 128], BF16)
        self.identf = sb("identf", [128, 128], F32)
        self.nh = sb("nh", [128, 1], F32)
        self.gcol = {}
        self.gph = {}
        if with_stage:
            self.stage = Rot("stage", [sb("stage%d" % i, [128, 512], F32) for i in range(2)])

    def init_consts(self, pre_names, post_names):
        P, nc = self.P, self.nc
        P.dma(self.identf[:], self.ident_in, writes=["identf"])
        P.op("dve", lambda e: e.tensor_copy(out=self.ident[:], in_=self.identf[:]), reads=["identf"], writes=["ident"])
        P.op("dve", lambda e: e.memset(self.nh[:], -0.5), writes=["nh"])
        for nm in pre_names:
            t = self.gcol[nm]
            P.op("sp", lambda e, t=t, nm=nm: [e.dma_start(out=t[:], in_=self.gains[nm].rearrange("(c p) -> p c", p=128),
                                                       allow_slow_non_contiguous=True)], writes=["gcol_" + nm], ndma=1)
        for nm in post_names:
            t = self.gph[nm]
            P.dma(t[:], self.gains[nm].partition_broadcast(128), writes=["gph_" + nm])
            P.op("dve", lambda e, t=t: e.tensor_scalar(out=t[:], in0=t[:], scalar1=0.5, scalar2=None, op0=ALU.mult),
                 reads=["gph_" + nm], writes=["gph_" + nm])

    def rstd_from(self, src_key, src_ap, junk_ap, junk_key):
        P = self.P
        sk, sm = self.small.next()
        P.op("act", lambda e: e.activation(out=junk_ap, in_=src_ap, func=AF.Square, accum_out=sm[:, 0:1]),
             reads=[src_key], writes=[junk_key, sk])
        P.op("dve", lambda e: e.tensor_scalar(out=sm[:, 1:2], in0=sm[:, 0:1], scalar1=1.0 / D, scalar2=EPS,
                                              op0=ALU.mult, op1=ALU.add), reads=[sk], writes=[sk])
        P.op("pool", lambda e: e.tensor_tensor(out=sm[:, 2:3], in0=sm[:, 1:2], in1=self.nh[:], op=ALU.pow),
             reads=[sk, "nh"], writes=[sk])
        return sk, sm[:, 2:3]

    def prenorm_a(self, xk, xap):
        P = self.P
        uk, un = self.un.next()
        rk, rstd = self.rstd_from(xk, xap[:], self.junk[:], "junk")
        P.op("dve", lambda e: e.tensor_scalar(out=un[:], in0=xap[:], scalar1=rstd, scalar2=None, op0=ALU.mult),
             reads=[xk, rk], writes=[uk])
        return uk, un

    def prenorm(self, xk, xap, gname, ub_i, s):
        uk, un = self.prenorm_a(xk, xap)
        self.prenorm_b(uk, un, gname, ub_i, s)

    def prenorm_b(self, uk, un, gname, ub_i, s):
        P = self.P
        pst = self.psum[:, 6:8, :].bitcast(BF16)
        for c in range(KC):
            P.op("pe", lambda e, c=c: e.transpose(out=pst[:, c // 8, (c % 8) * 128:(c % 8 + 1) * 128],
                                                  in_=un[:, c * 128:(c + 1) * 128], identity=self.ident[:]),
                 reads=[uk, "ident"], writes=["ps6" if c < 8 else "ps7"])
        gc = self.gcol[gname]
        ub = self.ub[ub_i]
        for hf in range(2):
            P.op("dve", lambda e, hf=hf: e.tensor_tensor(
                out=ub[:, hf * 8:(hf + 1) * 8, s * 128:(s + 1) * 128],
                in0=pst[:, hf, :].rearrange("p (c t) -> p c t", t=128),
                in1=gc[:, hf * 8:(hf + 1) * 8].unsqueeze(2).broadcast_to([128, 8, 128]), op=ALU.mult),
                reads=["ps%d" % (6 + hf), "gcol_" + gname], writes=[("ub", ub_i, s)])

    def load_cunit(self, name, c0, ncols):
        k, buf = self.wbuf.next()
        v = buf[:, 0:KC * ncols].rearrange("p (kc f) -> p kc f", kc=KC)
        src = self.wb[name].rearrange("(kc p) f -> p kc f", p=128)[:, :, c0:c0 + ncols]
        self.P.dma(v, src, writes=[k])
        return k, v

    def load_runit(self, name, kc0, nkc=2):
        k, buf = self.wbuf.next()
        v = buf[:, 0:nkc * D].rearrange("p (kc f) -> p kc f", kc=nkc)
        src = self.wb[name].rearrange("(kc p) f -> p kc f", p=128)[:, kc0:kc0 + nkc, :]
        self.P.dma(v, src, writes=[k])
        return k, v

    def gate_up(self, ub_i, nsub, gname, uname):
        P, cfg = self.P, self.cfg
        T = nsub * 128
        ub = self.ub[ub_i]
        ubkeys = [("ub", ub_i, s) for s in range(nsub)]
        nstep = cfg.dff // 256
        for step in range(nstep):
            gk, gv = self.load_cunit(gname, step * 256, 256)
            uk, uv = self.load_cunit(uname, step * 256, 256)
            b0 = (step % 2) * 4
            for jj in range(2):
                j = step * 2 + jj
                bg, bu = b0 + jj, b0 + 2 + jj
                for kc in range(KC):
                    P.op("pe", lambda e, kc=kc, jj=jj, bg=bg, gv=gv: e.matmul(
                        self.psum[:, bg, 0:T], lhsT=gv[:, kc, jj * 128:(jj + 1) * 128], rhs=ub[:, kc, 0:T],
                        start=(kc == 0), stop=(kc == KC - 1)), reads=[gk] + ubkeys, writes=["ps%d" % bg])
                for kc in range(KC):
                    P.op("pe", lambda e, kc=kc, jj=jj, bu=bu, uv=uv: e.matmul(
                        self.psum[:, bu, 0:T], lhsT=uv[:, kc, jj * 128:(jj + 1) * 128], rhs=ub[:, kc, 0:T],
                        start=(kc == 0), stop=(kc == KC - 1)), reads=[uk] + ubkeys, writes=["ps%d" % bu])
                sk, sg = self.sg.next()
                P.op("act", lambda e, bg=bg, sg=sg: e.activation(out=sg[:, 0:T], in_=self.psum[:, bg, 0:T], func=AF.Silu),
                     reads=["ps%d" % bg], writes=[sk])
                P.op("dve", lambda e, bu=bu, sg=sg, j=j: e.tensor_tensor(
                    out=self.hmid[:, j, 0:T], in0=sg[:, 0:T], in1=self.psum[:, bu, 0:T], op=ALU.mult),
                    reads=[sk, "ps%d" % bu], writes=[("hm", j)])

    def proj_rows(self, wname, nkc, lhs_fn, lhs_keys_fn, nsub, epilogue):
        P = self.P
        for sp0 in range(0, nsub, 2):
            ss = list(range(sp0, min(sp0 + 2, nsub)))
            for kc0 in range(0, nkc, 2):
                wk, wv = self.load_runit(wname, kc0, 2)
                for kk in range(2):
                    kc = kc0 + kk
                    for s in ss:
                        for blk in range(4):
                            b = (s % 2) * 4 + blk
                            P.op("pe", lambda e, kc=kc, kk=kk, s=s, blk=blk, b=b, wv=wv: e.matmul(
                                self.psum[:, b, :], lhsT=lhs_fn(kc, s), rhs=wv[:, kk, blk * 512:(blk + 1) * 512],
                                start=(kc == 0), stop=(kc == nkc - 1)),
                                reads=[wk] + lhs_keys_fn(kc, s), writes=["ps%d" % b])
            for s in ss:
                b0 = (s % 2) * 4
                epilogue(s, self.psum[:, b0:b0 + 4, :].rearrange("p b f -> p (b f)"), ["ps%d" % (b0 + i) for i in range(4)])

    def post_residual(self, yps, ykeys, xk, xap, gpost):
        P = self.P
        rk, rstd = self.rstd_from_multi(ykeys, yps)
        gph = self.gph[gpost]
        P.op("dve", lambda e: e.scalar_tensor_tensor(out=yps, in0=yps, scalar=rstd, in1=gph[:], op0=ALU.mult, op1=ALU.mult),
             reads=ykeys + [rk, "gph_" + gpost], writes=ykeys)
        P.op("dve", lambda e: e.tensor_tensor(out=xap[:], in0=yps, in1=xap[:], op=ALU.add),
             reads=ykeys + [xk], writes=[xk])

    def rstd_from_multi(self, keys, src_ap):
        P = self.P
        sk, sm = self.small.next()
        P.op("act", lambda e: e.activation(out=self.junk[:], in_=src_ap, func=AF.Square, accum_out=sm[:, 0:1]),
             reads=list(keys), writes=["junk", sk])
        P.op("dve", lambda e: e.tensor_scalar(out=sm[:, 1:2], in0=sm[:, 0:1], scalar1=1.0 / D, scalar2=EPS,
                                              op0=ALU.mult, op1=ALU.add), reads=[sk], writes=[sk])
        P.op("pool", lambda e: e.tensor_tensor(out=sm[:, 2:3], in0=sm[:, 1:2], in1=self.nh[:], op=ALU.pow),
             reads=[sk, "nh"], writes=[sk])
        return sk, sm[:, 2:3]

    def ffn(self, xts, ub_i, gn, un_, dn, gpre, gpost, pre=None, after_post=None):
        nsub = len(xts)
        for s, (xk, xap) in enumerate(xts):
            if pre is not None:
                self.prenorm_b(pre[s][0], pre[s][1], gpre, ub_i, s)
            else:
                self.prenorm(xk, xap, gpre, ub_i, s)
        self.gate_up(ub_i, nsub, gn, un_)
        nfc = self.cfg.dff // 128

        def epi(s, yps, ykeys):
            self.post_residual(yps, ykeys, xts[s][0], xts[s][1], gpost)
            if after_post is not None:
                after_post(s)
        self.proj_rows(dn, nfc, lambda kc, s: self.hmid[:, kc, s * 128:(s + 1) * 128],
                       lambda kc, s: [("hm", kc)], nsub, epi)

    def phase_a(self, st):
        nc, P, cfg = self.nc, self.P, self.cfg
        self.alloc_ac(st)
        sb = self.salloc
        for nm in ("ff1_pre", "mix_pre"):
            self.gcol[nm] = sb("gcol_" + nm, [128, KC], F32)
        self.gph["ff1_post"] = sb("gph_ff1_post", [128, D], F32)
        self.init_consts(("ff1_pre", "mix_pre"), ("ff1_post",))
        tiles = []
        for (b0, nrows) in ((0, cfg.rp), (cfg.rs0, cfg.ls + 128)):
            r = 0
            while r < nrows:
                n = min(512, nrows - r)
                tiles.append((b0 + r, n // 128, r < cfg.ls))
                r += n
        projv = self.hmid[:, 0:41, :].rearrange("p a b -> p (a b)").bitcast(F32)
        for (r0, nsub, own) in tiles:
            T = nsub * 128
            xts = []
            for s in range(nsub):
                xk, xap = self.xt.next()
                P.dma(xap[:], self.xin[r0 + s * 128:r0 + (s + 1) * 128, :], writes=[xk])
                xts.append((xk, xap))
            pend = [None] * nsub

            def after_post(s, xts=xts, pend=pend, own=own, r0=r0):
                xk, xap = xts[s]
                if own:
                    P.op("pool", lambda e: [e.dma_start(out=self.h1[r0 + s * 128:r0 + (s + 1) * 128, :], in_=xap[:])],
                         reads=[xk], writes=[("h1", r0, s)], ndma=1)
                pend[s] = self.prenorm_a(xk, xap)
            self.ffn(xts, 0, "g1", "u1", "d1", "ff1_pre", "ff1_post", after_post=after_post)
            for s in range(nsub):
                self.prenorm_b(pend[s][0], pend[s][1], "mix_pre", 1, s)
            ub = self.ub[1]
            hmkeys = [("hm", j) for j in range(44)]
            c0 = 0
            u = 0
            first = True
            while c0 < NTM:
                ncols = min(256, NTM - c0)
                if not own and not (1024 <= c0 < 1536 or c0 >= 2560):
                    c0 += ncols
                    u += 1
                    continue
                wk, wv = self.load_cunit("win", c0, ncols)
                for s in range(nsub):
                    b = (u % 2) * 4 + s
                    for kc in range(KC):
                        P.op("pe", lambda e, kc=kc, s=s, b=b, wv=wv, ncols=ncols: e.matmul(
                            self.psum[:, b, 0:ncols], lhsT=ub[:, kc, s * 128:(s + 1) * 128], rhs=wv[:, kc, :],
                            start=(kc == 0), stop=(kc == KC - 1)), reads=[wk, ("ub", 1, s)], writes=["ps%d" % b])
                    eng = "act" if (u + s) % 2 == 0 else "dve"
                    dstv = projv[:, s * NTM + c0:s * NTM + c0 + ncols]
                    if eng == "act":
                        P.op("act", lambda e, b=b, dstv=dstv, ncols=ncols: e.activation(out=dstv, in_=self.psum[:, b, 0:ncols], func=AF.Copy),
                             reads=["ps%d" % b] + (hmkeys if first else []), writes=[("pj", s, u)] + (hmkeys if first else []))
                    else:
                        P.op("dve", lambda e, b=b, dstv=dstv, ncols=ncols: e.tensor_copy(out=dstv, in_=self.psum[:, b, 0:ncols]),
                             reads=["ps%d" % b] + (hmkeys if first else []), writes=[("pj", s, u)] + (hmkeys if first else []))
                    first = False
                c0 += ncols
                u += 1
            nu = u
            P.op("pool", lambda e, r0=r0, nsub=nsub: [e.dma_start(out=self.proj[r0 + s * 128:r0 + (s + 1) * 128, :],
                                                                 in_=projv[:, s * NTM:(s + 1) * NTM]) for s in range(nsub)],
                 reads=[("pj", s, uu) for uu in range(nu) for s in range(nsub)], writes=[("projd", r0)] + hmkeys, ndma=nsub)
            for cu in range(CONV_DIM // 256):
                wk, wv = self.load_cunit("win", NTM + cu * 256, 256)
                for jj in range(2):
                    j = cu * 2 + jj
                    b = j % 8
                    for kc in range(KC):
                        P.op("pe", lambda e, kc=kc, jj=jj, b=b, wv=wv, T=T: e.matmul(
                            self.psum[:, b, 0:T], lhsT=wv[:, kc, jj * 128:(jj + 1) * 128], rhs=ub[:, kc, 0:T],
                            start=(kc == 0), stop=(kc == KC - 1)),
                            reads=[wk] + [("ub", 1, s) for s in range(nsub)], writes=["ps%d" % b])
                    sk, sg = self.stage.next()
                    if j % 2 == 0:
                        P.op("act", lambda e, b=b, sg=sg, T=T: e.activation(out=sg[:, 0:T], in_=self.psum[:, b, 0:T], func=AF.Copy),
                             reads=["ps%d" % b], writes=[sk])
                    else:
                        P.op("dve", lambda e, b=b, sg=sg, T=T: e.tensor_copy(out=sg[:, 0:T], in_=self.psum[:, b, 0:T]),
                             reads=["ps%d" % b], writes=[sk])
                    P.op("pool", lambda e, j=j, sg=sg, r0=r0, T=T: [e.dma_start(out=self.xbcT[j * 128:(j + 1) * 128, r0:r0 + T], in_=sg[:, 0:T])],
                         reads=[sk], writes=[("xbcd", j, r0)], ndma=1)

    def phase_c(self, st):
        nc, P, cfg = self.nc, self.P, self.cfg
        self.alloc_ac(st, with_stage=False)
        sb = self.salloc
        self.gcol["ff2_pre"] = sb("gcol_ff2_pre", [128, KC], F32)
        self.gph["mix_post"] = sb("gph_mix_post", [128, D], F32)
        self.gph["ff2_post"] = sb("gph_ff2_post", [128, D], F32)
        self.init_consts(("ff2_pre",), ("mix_post", "ff2_post"))
        P.op("dve", lambda e: e.tensor_scalar(out=self.gph["mix_post"][:], in0=self.gph["mix_post"][:], scalar1=2.0,
                                              scalar2=None, op0=ALU.mult), reads=["gph_mix_post"], writes=["gph_mix_post"])
        ctiles = [(sr0 + t * 512, or0 + t * 512) for (sr0, or0) in self.own_rows for t in range(cfg.nts)]
        for (hr0, r0) in ctiles:
            xts = []
            for s in range(4):
                xk, xap = self.xt.next()
                P.dma(xap[:], self.h1[hr0 + s * 128:hr0 + (s + 1) * 128, :], writes=[xk])
                xts.append((xk, xap))
            ub = self.ub[1]
            P.dma(ub[:], self.oT.rearrange("(kc p) t -> p kc t", p=128)[:, :, r0:r0 + 512], writes=[("ub", 1, s) for s in range(4)])
            pend = [None] * 4

            def epi_c(s, yps, ykeys, xts=xts, pend=pend):
                self.post_residual(yps, ykeys, xts[s][0], xts[s][1], "mix_post")
                pend[s] = self.prenorm_a(xts[s][0], xts[s][1])
            self.proj_rows("wout", KC, lambda kc, s: ub[:, kc, s * 128:(s + 1) * 128],
                           lambda kc, s: [("ub", 1, s)], 4, epi_c)

            def store(s, xts=xts, r0=r0):
                xk, xap = xts[s]
                P.op("pool", lambda e: [e.dma_start(out=self.yout[r0 + s * 128:r0 + (s + 1) * 128, :], in_=xap[:])],
                     reads=[xk], ndma=1, is_output=True)
            self.ffn(xts, 0, "g2", "u2", "d2", "ff2_pre", "ff2_post", pre=pend, after_post=store)

    def declare_b(self):
        nc, cfg = self.nc, self.cfg
        I = lambda n, s, d=F32: nc.dram_tensor(n, s, d, kind="ExternalInput").ap()
        S = lambda n, s, d: nc.dram_tensor(n, s, d, kind=("ExternalOutput" if cfg.debug else "Internal")).ap()
        nch = cfg.rtot // 128
        self.cst = I("cst", [128, 5, 128])
        self.ropet = I("ropet", [cfg.rtot, 128])
        self.qkg = I("qkg", [4, 64])
        self.kbias = I("kbias", [128, nch])
        self.dtmask = I("dtmask", [128, nch])
        self.keep = I("keep", [128, 2, nch])
        self.conv_w = I("conv_w", [5, CONV_DIM])
        self.conv_b = I("conv_b", [CONV_DIM])
        self.a_log = I("a_log", [32])
        self.dt_bias = I("dt_bias", [32])
        self.dskip = I("dskip", [SSM_W])
        self.ssm_norm = I("ssm_norm", [SSM_W])
        self.xsB = S("xsB", [cfg.rtot, 1280], BF16)
        self.bcT = S("bcT", [512, cfg.rtot], BF16)
        self.qT = S("qT", [1024, 2 * cfg.ls], BF16)
        self.kT = S("kT", [256, cfg.rtot], BF16)
        self.vx = S("vx", [cfg.rtot, 640], BF16)
        self.blocks = [(0, cfg.rp, 0), (cfg.rs0, cfg.ls + 128, cfg.ls)]

    def phase_b0(self):
        nc, P, cfg = self.nc, self.P, self.cfg
        sb = self.salloc
        identf = sb("identf", [128, 128], F32)
        ident = sb("ident", [128, 128], BF16)
        wcol = sb("wcol", [128, 5, 12], F32)
        bcol = sb("bcol", [128, 12], F32)
        xin = Rot("cxin", [sb("cxin%d" % i, [128, 1028], F32) for i in range(2)])
        acc = Rot("cacc", [sb("cacc%d" % i, [128, 1024], F32) for i in range(2)])
        cvo = [sb("cvo%d" % i, [128, 1024], BF16) for i in range(12)]
        stg = Rot("cstg", [sb("cstg%d" % i, [128, 1280], BF16) for i in range(2)])
        P.dma(identf[:], self.cst[:, 0, :], writes=["identf"])
        P.op("dve", lambda e: e.tensor_copy(out=ident[:], in_=identf[:]), reads=["identf"], writes=["ident"])
        P.op("sp", lambda e: [e.dma_start(out=wcol[:, j, :], in_=self.conv_w[j, :].rearrange("(c p) -> p c", p=128), allow_slow_non_contiguous=True)
                              for j in range(5)], writes=["wcol"], ndma=5)
        P.op("sp", lambda e: [e.dma_start(out=bcol[:], in_=self.conv_b.rearrange("(c p) -> p c", p=128), allow_slow_non_contiguous=True)],
             writes=["bcol"], ndma=1)
        pst = self.psum[:, 0:4, :].bitcast(BF16)
        tcount = 0
        for (b0, nrows, _) in self.blocks:
            c = 0
            while c < nrows:
                w = min(1024, nrows - c)
                c0 = b0 + c
                lh = b0 + (c - 2) % nrows
                rh = b0 + (c + w) % nrows
                for cc in range(12):
                    xk, xa = xin.next()
                    rows = slice(cc * 128, (cc + 1) * 128)
                    P.op("sp", lambda e, xa=xa, rows=rows, c0=c0, w=w, lh=lh, rh=rh: [
                        e.dma_start(out=xa[:, 2:2 + w], in_=self.xbcT[rows, c0:c0 + w]),
                        e.dma_start(out=xa[:, 0:2], in_=self.xbcT[rows, lh:lh + 2]),
                        e.dma_start(out=xa[:, 2 + w:4 + w], in_=self.xbcT[rows, rh:rh + 2])], writes=[xk], ndma=3)
                    ak, aa = acc.next()
                    P.op("act", lambda e, xa=xa, aa=aa, cc=cc, w=w: e.activation(
                        out=aa[:, 0:w], in_=xa[:, 0:w], func=AF.Identity, scale=wcol[:, 0, cc:cc + 1], bias=bcol[:, cc:cc + 1]),
                        reads=[xk, "wcol", "bcol"], writes=[ak])
                    for j in range(1, 5):
                        P.op("dve", lambda e, xa=xa, aa=aa, cc=cc, w=w, j=j: e.scalar_tensor_tensor(
                            out=aa[:, 0:w], in0=xa[:, j:j + w], scalar=wcol[:, j, cc:cc + 1], in1=aa[:, 0:w],
                            op0=ALU.mult, op1=ALU.add), reads=[xk, ak, "wcol"], writes=[ak])
                    P.op("act", lambda e, aa=aa, cc=cc, w=w: e.activation(out=cvo[cc][:, 0:w], in_=aa[:, 0:w], func=AF.Silu),
                         reads=[ak], writes=[("cvo", cc)])
                for cc in range(8, 12):
                    P.op("pool", lambda e, cc=cc, c0=c0, w=w: [e.dma_start(out=self.bcT[(cc - 8) * 128:(cc - 7) * 128, c0:c0 + w], in_=cvo[cc][:, 0:w])],
                         reads=[("cvo", cc)], writes=[("bcTd", cc, c0)], ndma=1)
                for s in range(w // 128):
                    pb = (tcount % 2) * 2
                    tcount += 1
                    for cc in range(10):
                        P.op("pe", lambda e, cc=cc, s=s, pb=pb: e.transpose(
                            out=pst[:, pb + cc // 8, (cc % 8) * 128:(cc % 8 + 1) * 128], in_=cvo[cc][:, s * 128:(s + 1) * 128], identity=ident[:]),
                            reads=[("cvo", cc), "ident"], writes=["ps%d" % (pb + cc // 8)])
                    sk, sa = stg.next()
                    P.op("act", lambda e, sa=sa, pb=pb: e.activation(out=sa[:, 0:1024], in_=pst[:, pb, :], func=AF.Copy),
                         reads=["ps%d" % pb], writes=[sk])
                    P.op("dve", lambda e, sa=sa, pb=pb: e.tensor_copy(out=sa[:, 1024:1280], in_=pst[:, pb + 1, 0:256]),
                         reads=["ps%d" % (pb + 1), sk], writes=[sk])
                    P.op("pool", lambda e, sa=sa, r=c0 + s * 128: [e.dma_start(out=self.xsB[r:r + 128, :], in_=sa[:])],
                         reads=[sk], writes=[("xsBd", c0, s)], ndma=1)
                c += w

    def phase_b1(self):
        nc, P, cfg = self.nc, self.P, self.cfg
        sb = self.salloc
        identf = sb("identf", [128, 128], F32)
        ident = sb("ident", [128, 128], BF16)
        nh = sb("nh", [128, 1], F32)
        gqq = sb("gqq", [128, 128], F32)
        gkk = sb("gkk", [128, 128], F32)
        pj = Rot("pj", [sb("pj%d" % i, [128, 1536], F32) for i in range(2)])
        rp = Rot("rp", [sb("rp%d" % i, [128, 128], F32) for i in range(2)])
        sq = sb("sq", [128, 1280], F32)
        xn = sb("xn", [128, 1280], F32)
        t1 = sb("t1", [128, 1280], F32)
        t2 = sb("t2", [128, 1280], F32)
        tq = sb("tq", [128, 128], F32)
        tk = sb("tk", [128, 128], F32)
        sm = Rot("b1sm", [sb("b1sm%d" % i, [128, 64], F32) for i in range(2)])
        yb = Rot("yb", [sb("yb%d" % i, [128, 1280], BF16) for i in range(2)])
        vxt = Rot("vxt", [sb("vxt%d" % i, [128, 640], BF16) for i in range(2)])
        qst = Rot("qst", [sb("qst%d" % i, [128, 8, 512], BF16) for i in range(2)])
        kst = Rot("kst", [sb("kst%d" % i, [128, 2, 512], BF16) for i in range(2)])
        pj4 = Rot("pj4", [sb("pj4_%d" % i, [128, 4, 512], F32) for i in range(2)])
        rp4 = Rot("rp4", [sb("rp4_%d" % i, [128, 4, 128], F32) for i in range(2)])
        sq4 = sb("sq4", [128, 4, 256], F32)
        xn4 = sb("xn4", [128, 4, 256], F32)
        xw4 = sb("xw4", [128, 4, 256], F32)
        t14 = sb("t14", [128, 4, 256], F32)
        t24 = sb("t24", [128, 4, 256], F32)
        tk4 = sb("tk4", [128, 4, 128], F32)
        sm4 = Rot("sm4", [sb("sm4_%d" % i, [128, 64], F32) for i in range(2)])
        yb4 = Rot("yb4", [sb("yb4_%d" % i, [128, 4, 256], BF16) for i in range(2)])
        vx4 = Rot("vx4", [sb("vx4_%d" % i, [128, 4, 640], BF16) for i in range(2)])
        for i in range(2):
            P.op("pool", lambda e, i=i: e.memset(vx4.aps[i][:].rearrange("p s w -> p (s w)"), 1.0), writes=["vx4_%d" % i])
        P.dma(identf[:], self.cst[:, 0, :], writes=["identf"])
        P.op("dve", lambda e: e.tensor_copy(out=ident[:], in_=identf[:]), reads=["identf"], writes=["ident"])
        P.op("dve", lambda e: e.memset(nh[:], -0.5), writes=["nh"])
        P.dma(gqq[:], self.qkg[0:2, :].rearrange("a d -> (a d)").partition_broadcast(128), writes=["gqq"])
        P.dma(gkk[:], self.qkg[2:4, :].rearrange("a d -> (a d)").partition_broadcast(128), writes=["gkk"])
        P.op("dve", lambda e: e.tensor_scalar(out=gqq[:], in0=gqq[:], scalar1=0.125, scalar2=None, op0=ALU.mult), reads=["gqq"], writes=["gqq"])
        for i in range(2):
            P.op("pool", lambda e, i=i: e.memset(vxt.aps[i][:], 1.0), writes=["vxt%d" % i])
        pst = self.psum[:, 0:4, :].bitcast(BF16)
        tcount = 0
        for bi, (b0, nrows, oc0) in enumerate(self.blocks):
            r = 0
            while r < nrows:
                n = min(512, nrows - r)
                own = r < cfg.ls
                if not own:
                    ns = n // 128
                    r0 = b0 + r
                    pk, pa_ = pj4.next()
                    rk, ra_ = rp4.next()
                    pa = pa_[:, 0:ns, :]
                    ra = ra_[:, 0:ns, :]
                    P.dma(pa, self.proj[r0:r0 + n, 1024:1536].rearrange("(s p) f -> p s f", p=128), writes=[pk])
                    P.dma(ra, self.ropet[r0:r0 + n, :].rearrange("(s p) f -> p s f", p=128), writes=[rk])
                    sk, sa = sm4.next()
                    nh4 = ns * 4
                    kv = lambda t: t[:, 0:ns, :]
                    h3 = lambda ap: ap.rearrange("p s (h d) -> p (s h) d", d=64)
                    P.op("act", lambda e, pa=pa, ns=ns: e.activation(out=sq4[:, 0:ns, :], in_=pa[:, :, 0:256], func=AF.Square), reads=[pk], writes=["sq4"])
                    P.op("dve", lambda e, sa=sa, ns=ns, nh4=nh4: e.tensor_reduce(out=sa[:, 0:nh4], in_=sq4[:, 0:ns, :].rearrange("p s (h d) -> p (s h) d", d=64),
                                                                             axis=AX.X, op=ALU.add), reads=["sq4"], writes=[sk])
                    P.op("dve", lambda e, sa=sa, nh4=nh4: e.tensor_scalar(out=sa[:, 16:16 + nh4], in0=sa[:, 0:nh4], scalar1=1.0 / 64, scalar2=EPS,
                                                                      op0=ALU.mult, op1=ALU.add), reads=[sk], writes=[sk])
                    P.op("pool", lambda e, sa=sa, nh4=nh4: e.tensor_tensor(out=sa[:, 32:32 + nh4], in0=sa[:, 16:16 + nh4],
                                                                       in1=nh[:, 0:1].broadcast_to([128, nh4]), op=ALU.pow), reads=[sk, "nh"], writes=[sk])
                    P.op("dve", lambda e, pa=pa, sa=sa, ns=ns, nh4=nh4: e.tensor_tensor(
                        out=xn4[:, 0:ns, :].rearrange("p s (h d) -> p s h d", d=64), in0=pa[:, :, 0:256].rearrange("p s (h d) -> p s h d", d=64),
                        in1=sa[:, 32:32 + nh4].rearrange("p (s h) -> p s h", h=4).unsqueeze(3).broadcast_to([128, ns, 4, 64]), op=ALU.mult),
                        reads=[pk, sk], writes=["xn4"])
                    P.op("pool", lambda e, ra=ra, ns=ns: e.tensor_tensor(out=tk4[:, 0:ns, :], in0=ra, in1=gkk[:].unsqueeze(1).broadcast_to([128, ns, 128]), op=ALU.mult),
                         reads=[rk, "gkk"], writes=["tk4"])
                    for b in range(2):
                        x5 = xn4[:, 0:ns, :].rearrange("p s (h a b i) -> p (s h) a b i", a=2, b=2, i=16)
                        o5 = xw4[:, 0:ns, :].rearrange("p s (h a b i) -> p (s h) a b i", a=2, b=2, i=16)
                        P.op("pool", lambda e, x5=x5, o5=o5, b=b: e.tensor_copy(out=o5[:, :, :, b, :], in_=x5[:, :, :, 1 - b, :]), reads=["xn4"], writes=[("xw4", b)])
                    q4 = lambda t, ns=ns: t[:, 0:ns, :].rearrange("p s (h d) -> p s h d", d=64)
                    P.op("dve", lambda e, ns=ns, q4=q4: e.tensor_tensor(out=q4(t14), in0=q4(xn4), in1=tk4[:, 0:ns, 0:64].unsqueeze(2).broadcast_to([128, ns, 4, 64]),
                                                                    op=ALU.mult), reads=["xn4", "tk4"], writes=["t14"])
                    P.op("dve", lambda e, ns=ns, q4=q4: e.tensor_tensor(out=q4(t24), in0=q4(xw4), in1=tk4[:, 0:ns, 64:128].unsqueeze(2).broadcast_to([128, ns, 4, 64]),
                                                                    op=ALU.mult), reads=[("xw4", 0), ("xw4", 1), "tk4"], writes=["t24"])
                    yk, ya_ = yb4.next()
                    ya = ya_[:, 0:ns, :]
                    P.op("dve", lambda e, ya=ya, ns=ns: e.tensor_tensor(out=ya, in0=t14[:, 0:ns, :], in1=t24[:, 0:ns, :], op=ALU.add), reads=["t14", "t24"], writes=[yk])
                    vk, va_ = vx4.next()
                    va = va_[:, 0:ns, :]
                    for kp in range(2):
                        P.op("act", lambda e, va=va, pa=pa, kp=kp: e.activation(
                            out=va[:, :, kp * 320 + 64:kp * 320 + 320].rearrange("p s (j w) -> p s j w", j=2)[:, :, :, 0:64],
                            in_=pa[:, :, 256 + kp * 128:384 + kp * 128].rearrange("p s (j d) -> p s j d", j=2), func=AF.Copy), reads=[pk], writes=[(vk, kp)])
                    P.op("pool", lambda e, va=va, r0=r0, n=n: [e.dma_start(out=self.vx[r0:r0 + n, :].rearrange("(s p) w -> p s w", p=128), in_=va)],
                         reads=[(vk, 0), (vk, 1)], writes=[("vxd", r0), (vk, 0), (vk, 1)], ndma=1)
                    pb = (tcount % 2) * 2
                    tcount += 1
                    for s_ in range(ns):
                        for p_ in range(2):
                            P.op("pe", lambda e, p_=p_, s_=s_, ya=ya, pb=pb: e.transpose(out=pst[:, pb, p_ * 512 + s_ * 128:p_ * 512 + (s_ + 1) * 128],
                                                                                   in_=ya[:, s_, p_ * 128:(p_ + 1) * 128], identity=ident[:]),
                                 reads=[yk, "ident"], writes=["ps%d" % pb])
                    kk_, ka = kst.next()
                    P.op("dve", lambda e, ka=ka, pb=pb, n=n: e.tensor_copy(out=ka[:, :, 0:n], in_=pst[:, pb, :].rearrange("p (c t) -> p c t", c=2)[:, :, 0:n]),
                         reads=["ps%d" % pb], writes=[(kk_, s_) for s_ in range(4)])
                    P.op("pool", lambda e, ka=ka, rr=r0, n=n: [e.dma_start(out=self.kT.rearrange("(c p) t -> p c t", p=128)[:, :, rr:rr + n], in_=ka[:, :, 0:n])],
                         reads=[(kk_, s_) for s_ in range(4)], writes=[("kTd", r0)] + [(kk_, s_) for s_ in range(4)], ndma=1)
                    r += n
                    continue
                qk_, qa = qst.next()
                kk_, ka = kst.next()
                for s in range(n // 128):
                    r0 = b0 + r + s * 128
                    c_lo = 0 if own else 1024
                    h_lo = 0 if own else 16
                    nh_ = 20 - h_lo
                    pk, pa = pj.next()
                    rk, ra = rp.next()
                    P.dma(pa[:, c_lo:1536], self.proj[r0:r0 + 128, c_lo:1536], writes=[pk])
                    P.dma(ra[:], self.ropet[r0:r0 + 128, :], writes=[rk])
                    sk, sa = sm.next()
                    P.op("act", lambda e, pa=pa, c_lo=c_lo: e.activation(out=sq[:, c_lo:1280], in_=pa[:, c_lo:1280], func=AF.Square),
                         reads=[pk], writes=["sq"])
                    P.op("dve", lambda e, sa=sa, c_lo=c_lo, h_lo=h_lo: e.tensor_reduce(
                        out=sa[:, h_lo:20], in_=sq[:, c_lo:1280].rearrange("p (h d) -> p h d", d=64), axis=AX.X, op=ALU.add),
                        reads=["sq"], writes=[sk])
                    P.op("dve", lambda e, sa=sa, h_lo=h_lo: e.tensor_scalar(out=sa[:, 20 + h_lo:40], in0=sa[:, h_lo:20], scalar1=1.0 / 64, scalar2=EPS,
                                                                       op0=ALU.mult, op1=ALU.add), reads=[sk], writes=[sk])
                    P.op("pool", lambda e, sa=sa, h_lo=h_lo, nh_=nh_: e.tensor_tensor(out=sa[:, 40 + h_lo:60], in0=sa[:, 20 + h_lo:40],
                                                                                in1=nh[:, 0:1].broadcast_to([128, nh_]), op=ALU.pow),
                         reads=[sk, "nh"], writes=[sk])
                    P.op("dve", lambda e, pa=pa, sa=sa, c_lo=c_lo, h_lo=h_lo, nh_=nh_: e.tensor_tensor(
                        out=xn[:, c_lo:1280].rearrange("p (h d) -> p h d", d=64), in0=pa[:, c_lo:1280].rearrange("p (h d) -> p h d", d=64),
                        in1=sa[:, 40 + h_lo:60].unsqueeze(2).broadcast_to([128, nh_, 64]), op=ALU.mult), reads=[pk, sk], writes=["xn"])
                    if own:
                        P.op("pool", lambda e, ra=ra: e.tensor_tensor(out=tq[:], in0=ra[:], in1=gqq[:], op=ALU.mult), reads=[rk, "gqq"], writes=["tq"])
                    P.op("pool", lambda e, ra=ra: e.tensor_tensor(out=tk[:], in0=ra[:], in1=gkk[:], op=ALU.mult), reads=[rk, "gkk"], writes=["tk"])
                    groups = ([(0, 16, tq, "tq")] if own else []) + [(16, 4, tk, "tk")]
                    for (h0, hn, tt, tkey) in groups:
                        xv = xn[:, h0 * 64:(h0 + hn) * 64]
                        P.op("dve", lambda e, xv=xv, tt=tt, h0=h0, hn=hn: e.tensor_tensor(
                            out=t1[:, h0 * 64:(h0 + hn) * 64].rearrange("p (h d) -> p h d", d=64), in0=xv.rearrange("p (h d) -> p h d", d=64),
                            in1=tt[:, 0:64].unsqueeze(1).broadcast_to([128, hn, 64]), op=ALU.mult), reads=["xn", tkey], writes=[("t1", h0)])
                        for b in range(2):
                            x5 = xv.rearrange("p (h a b i) -> p h a b i", a=2, b=2, i=16)
                            o5 = t2[:, h0 * 64:(h0 + hn) * 64].rearrange("p (h a b i) -> p h a b i", a=2, b=2, i=16)
                            s4 = tt[:, 64:128].rearrange("p (a b i) -> p a b i", a=2, b=2, i=16)
                            P.op("pool", lambda e, x5=x5, o5=o5, s4=s4, b=b, hn=hn: e.tensor_tensor(
                                out=o5[:, :, :, b, :], in0=x5[:, :, :, 1 - b, :],
                                in1=s4[:, :, b, :].unsqueeze(1).broadcast_to([128, hn, 2, 16]), op=ALU.mult),
                                reads=["xn", tkey], writes=[("t2", h0, b)])
                    yk, ya = yb.next()
                    P.op("dve", lambda e, ya=ya, c_lo=c_lo: e.tensor_tensor(out=ya[:, c_lo:1280], in0=t1[:, c_lo:1280], in1=t2[:, c_lo:1280], op=ALU.add),
                         reads=[("t1", 0), ("t1", 16), ("t2", 0, 0), ("t2", 0, 1), ("t2", 16, 0), ("t2", 16, 1)], writes=[yk])
                    vk, va = vxt.next()
                    P.op("act", lambda e, va=va, pa=pa: e.activation(
                        out=va[:].rearrange("p (kp w) -> p kp w", kp=2)[:, :, 64:320].rearrange("p kp (j w) -> p kp j w", j=2)[:, :, :, 0:64],
                        in_=pa[:, 1280:1536].rearrange("p (kp j d) -> p kp j d", kp=2, j=2), func=AF.Copy), reads=[pk], writes=[vk])
                    P.op("pool", lambda e, va=va, r0=r0: [e.dma_start(out=self.vx[r0:r0 + 128, :], in_=va[:])], reads=[vk], writes=[("vxd", r0)], ndma=1)
                    pb = (tcount % 2) * 2
                    tcount += 1
                    if own:
                        for p_ in range(8):
                            P.op("pe", lambda e, p_=p_, ya=ya, pb=pb: e.transpose(out=pst[:, pb, p_ * 128:(p_ + 1) * 128], in_=ya[:, p_ * 128:(p_ + 1) * 128],
                                                                              identity=ident[:]), reads=[yk, "ident"], writes=["ps%d" % pb])
                        P.op("act", lambda e, qa=qa, pb=pb, s=s: e.activation(out=qa[:, :, s * 128:(s + 1) * 128],
                                                                          in_=pst[:, pb, :].rearrange("p (c t) -> p c t", t=128), func=AF.Copy),
                             reads=["ps%d" % pb], writes=[(qk_, s)])
                    for p_ in range(2):
                        P.op("pe", lambda e, p_=p_, ya=ya, pb=pb: e.transpose(out=pst[:, pb + 1, p_ * 128:(p_ + 1) * 128],
                                                                          in_=ya[:, 1024 + p_ * 128:1024 + (p_ + 1) * 128], identity=ident[:]),
                             reads=[yk, "ident"], writes=["ps%d" % (pb + 1)])
                    P.op("dve", lambda e, ka=ka, pb=pb, s=s: e.tensor_copy(out=ka[:, :, s * 128:(s + 1) * 128],
                                                                       in_=pst[:, pb + 1, 0:256].rearrange("p (c t) -> p c t", t=128)),
                         reads=["ps%d" % (pb + 1)], writes=[(kk_, s)])
                ns = n // 128
                if own:
                    P.op("pool", lambda e, qa=qa, oc=oc0 + r, n=n: [e.dma_start(out=self.qT.rearrange("(c p) t -> p c t", p=128)[:, :, oc:oc + n], in_=qa[:, :, 0:n])],
                         reads=[(qk_, s) for s in range(ns)], writes=[("qTd", oc0 + r)] + [(qk_, s) for s in range(ns)], ndma=1)
                P.op("pool", lambda e, ka=ka, rr=b0 + r, n=n: [e.dma_start(out=self.kT.rearrange("(c p) t -> p c t", p=128)[:, :, rr:rr + n], in_=ka[:, :, 0:n])],
                     reads=[(kk_, s) for s in range(ns)], writes=[("kTd", b0 + r)] + [(kk_, s) for s in range(ns)], ndma=1)
                r += n

    def phase_b2(self):
        nc, P, cfg = self.nc, self.P, self.cfg
        sb = self.salloc
        nchmax = cfg.rp // 128
        nch_tot = cfg.rtot // 128
        KT = sb("KT", [128, cfg.rp], BF16)
        VX = sb("VX", [128, nchmax, 320], BF16)
        QT = Rot("QT", [sb("QT%d" % i, [128, cfg.ls], BF16) for i in range(4)])
        PT = Rot("PT", [sb("PT%d" % i, [128, 1024], BF16) for i in range(3)])
        OT = Rot("OT", [sb("OT%d" % i, [128, 512], BF16) for i in range(2)])
        BC = Rot("BC", [sb("BC%d" % i, [128, 512], F32) for i in range(2)])
        RL = Rot("RL", [sb("RL%d" % i, [128, 512], F32) for i in range(2)])
        kb = sb("kb", [128, nch_tot], F32)
        sel = sb("sel", [128, 128], F32)
        gq = sb("gq", [128, 256], F32)
        gsm = sb("gsm", [128, 8], F32)
        P.dma(kb[:], self.kbias, writes=["kb"])
        P.dma(gq[:], self.qkg.rearrange("a d -> (a d)").partition_broadcast(128), writes=["gq"])
        P.op("dve", lambda e: e.tensor_reduce(out=gsm[:, 0:4], in_=gq[:].rearrange("p (a d) -> p a d", a=4), axis=AX.X, op=ALU.max,
                                              apply_absolute_value=True), reads=["gq"], writes=["gsm"])
        P.op("dve", lambda e: e.tensor_tensor(out=gsm[:, 4:5], in0=gsm[:, 0:1], in1=gsm[:, 2:3], op=ALU.mult), reads=["gsm"], writes=["gsm"])
        P.op("dve", lambda e: e.tensor_scalar(out=gsm[:, 5:6], in0=gsm[:, 4:5], scalar1=-8.0, scalar2=None, op0=ALU.mult), reads=["gsm"], writes=["gsm"])
        P.op("dve", lambda e: e.tensor_scalar(out=kb[:], in0=kb[:], scalar1=gsm[:, 5:6], scalar2=None, op0=ALU.add), reads=["gsm", "kb"], writes=["kb"])
        P.op("pool", lambda e: e.memset(sel[:], 1.0), writes=["sel"])
        for bi, (b0, nrows, oc0) in enumerate(self.blocks):
            nch = nrows // 128
            for kp in range(2):
                P.dma(KT[:, 0:nrows], self.kT[kp * 128:(kp + 1) * 128, b0:b0 + nrows], writes=["KT"])
                P.dma(VX[:, 0:nch, :], self.vx[b0:b0 + nrows, kp * 320:(kp + 1) * 320].rearrange("(c p) w -> p c w", p=128), writes=["VX"])
                steps = []
                for i in range(4):
                    p_ = kp * 4 + i
                    qk_, qa = QT.next()
                    P.dma(qa[:], self.qT[p_ * 128:(p_ + 1) * 128, oc0:oc0 + cfg.ls], writes=[qk_])
                    for qb in range(cfg.ls // 512):
                        for ch in range(nch):
                            steps.append((p_, qk_, qa, qb, ch))

                def emit_s(idx, st):
                    p_, qk_, qa, qb, ch = st
                    sbk = (idx % 2) * 2
                    for hf in range(2):
                        lo = hf * 64
                        P.op("pe", lambda e, lo=lo, ch=ch, qa=qa, qb=qb, b=sbk + hf: e.matmul(
                            self.psum[:, b, :], lhsT=KT[lo:lo + 64, ch * 128:(ch + 1) * 128], rhs=qa[lo:lo + 64, qb * 512:(qb + 1) * 512],
                            start=True, stop=True), reads=["KT", qk_], writes=["ps%d" % (sbk + hf)])

                emit_s(0, steps[0])
                for idx, st in enumerate(steps):
                    p_, qk_, qa, qb, ch = st
                    if idx + 1 < len(steps):
                        emit_s(idx + 1, steps[idx + 1])
                    sbk = (idx % 2) * 2
                    pk_, pa = PT.next()
                    gch = b0 // 128 + ch
                    P.op("act", lambda e, pa=pa, sbk=sbk, gch=gch: e.activation(
                        out=pa[:], in_=self.psum[:, sbk:sbk + 2, :].rearrange("p b f -> p (b f)"), func=AF.Exp, bias=kb[:, gch:gch + 1], scale=1.0),
                        reads=["ps%d" % sbk, "ps%d" % (sbk + 1), "kb"], writes=[pk_])
                    for hf in range(2):
                        w0 = 64 if hf == 0 else 128
                        P.op("pe", lambda e, pa=pa, ch=ch, hf=hf, w0=w0, nch=nch: e.matmul(
                            self.psum[:, 4 + hf, :], lhsT=VX[:, ch, w0:w0 + 128], rhs=pa[:, hf * 512:(hf + 1) * 512], start=(ch == 0), stop=(ch == nch - 1)),
                            reads=["VX", pk_], writes=["ps%d" % (4 + hf)])
                    if ch == nch - 1:
                        rk_, rl = RL.next()
                        P.op("dve", lambda e, rl=rl: e.reciprocal(out=rl[64:65, :], in_=self.psum[64:65, 4, :]), reads=["ps4"], writes=[(rk_, 0)])
                        P.op("dve", lambda e, rl=rl: e.reciprocal(out=rl[0:1, :], in_=self.psum[0:1, 5, :]), reads=["ps5"], writes=[(rk_, 1)])
                        P.op("pe", lambda e, rl=rl: e.matmul(self.psum[:, 6, :], lhsT=sel[64:65, :], rhs=rl[64:65, :], start=True, stop=True),
                             reads=[(rk_, 0), "sel"], writes=["ps6"])
                        P.op("pe", lambda e, rl=rl: e.matmul(self.psum[:, 7, :], lhsT=sel[0:1, :], rhs=rl[0:1, :], start=True, stop=True),
                             reads=[(rk_, 1), "sel"], writes=["ps7"])
                        bk_, bc = BC.next()
                        P.op("act", lambda e, bc=bc: e.activation(out=bc[0:64, :], in_=self.psum[0:64, 6, :], func=AF.Copy), reads=["ps6"], writes=[(bk_, 0)])
                        P.op("act", lambda e, bc=bc: e.activation(out=bc[64:128, :], in_=self.psum[64:128, 7, :], func=AF.Copy), reads=["ps7"], writes=[(bk_, 1)])
                        ok_, ot = OT.next()
                        P.op("dve", lambda e, ot=ot, bc=bc: e.tensor_tensor(out=ot[0:64, :], in0=self.psum[0:64, 4, :], in1=bc[0:64, :], op=ALU.mult),
                             reads=["ps4", (bk_, 0)], writes=[(ok_, 0)])
                        P.op("dve", lambda e, ot=ot, bc=bc: e.tensor_tensor(out=ot[64:128, :], in0=self.psum[64:128, 5, :], in1=bc[64:128, :], op=ALU.mult),
                             reads=["ps5", (bk_, 1)], writes=[(ok_, 1)])
                        P.op("pool", lambda e, ot=ot, p_=p_, cc=oc0 + qb * 512: [e.dma_start(out=self.oT[p_ * 128:(p_ + 1) * 128, cc:cc + 512], in_=ot[:])],
                             reads=[(ok_, 0), (ok_, 1)], writes=[("oTd", p_, oc0 + qb * 512), (ok_, 0), (ok_, 1)], ndma=1)

    def phase_b3(self):
        nc, P, cfg = self.nc, self.P, self.cfg
        sb = self.salloc
        nch_tot = cfg.rtot // 128
        nown = cfg.ls // 128
        cst = sb("cstt", [128, 5, 128], F32)
        ones = sb("ones", [128, 128], F32)
        ident = sb("ident", [128, 128], BF16)
        Aexp = sb("Aexp", [128, 32], F32)
        dtb = sb("dtb", [128, 32], F32)
        dsk = sb("dsk", [128, 1024], F32)
        gss = sb("gss", [128, 1024], F32)
        keep = sb("keep", [128, 2, nch_tot], F32)
        dtm = sb("dtm", [128, nch_tot], F32)
        nh = sb("nh", [128, 1], F32)
        S = [sb("S%d" % d, [128, 1024], F32) for d in range(2)]
        Sb = Rot("Sb", [sb("Sb%d" % i, [128, 1024], BF16) for i in range(2)])
        xsb = Rot("xsb", [sb("xsb%d" % i, [128, 1280], BF16) for i in range(3)])
        bct = Rot("bct", [sb("bct%d" % i, [128, 4, 128], BF16) for i in range(2)])
        zt = Rot("zt", [sb("zt%d" % i, [128, 1024], F32) for i in range(1)])
        xw = Rot("xw", [sb("xw%d" % i, [128, 1024], BF16) for i in range(2)])
        xdt = Rot("xdt", [sb("xdt%d" % i, [128, 1024], BF16) for i in range(2)])
        AU = Rot("AU", [sb("AU%d" % i, [128, 16, 128], F32) for i in range(1)])
        Mexp = sb("Mexp", [128, 16, 128], F32)
        MT = Rot("MT", [sb("MT%d" % i, [128, 16, 128], BF16) for i in range(2)])
        CBm = sb("CBm", [128, 2, 128], F32)
        tmpy = Rot("tmpy", [sb("tmpy%d" % i, [128, 1024], F32) for i in range(2)])
        yacc = sb("yacc", [128, nown, 1024], F32)
        f1 = sb("f1", [128, 1024], F32)
        f2 = sb("f2", [128, 1024], F32)
        fo = Rot("fo", [sb("fo%d" % i, [128, 1024], BF16) for i in range(2)])
        ost = Rot("ost", [sb("ost%d" % i, [128, 8, 128], BF16) for i in range(2)])
        fsm = Rot("fsm", [sb("fsm%d" % i, [128, 8], F32) for i in range(2)])
        TRI = [cst[:, 1, :], cst[:, 2, :]]
        UU = [cst[:, 3, :], cst[:, 4, :]]
        ps = self.psum
        P.dma(cst[:], self.cst, writes=["cst"])
        P.op("dve", lambda e: e.tensor_copy(out=ident[:], in_=cst[:, 0, :]), reads=["cst"], writes=["ident"])
        P.op("pool", lambda e: e.memset(ones[:], 1.0), writes=["ones"])
        P.op("pool", lambda e: e.memset(nh[:], -0.5), writes=["nh"])
        P.dma(Aexp[:], self.a_log.partition_broadcast(128), writes=["Aexp"])
        P.op("act", lambda e: e.activation(out=Aexp[:], in_=Aexp[:], func=AF.Exp), reads=["Aexp"], writes=["Aexp"])
        P.op("dve", lambda e: e.tensor_scalar(out=Aexp[:], in0=Aexp[:], scalar1=-1.0, scalar2=None, op0=ALU.mult), reads=["Aexp"], writes=["Aexp"])
        P.dma(dtb[:], self.dt_bias.partition_broadcast(128), writes=["dtb"])
        P.dma(dsk[:], self.dskip.partition_broadcast(128), writes=["dsk"])
        P.dma(gss[:], self.ssm_norm.partition_broadcast(128), writes=["gss"])
        P.dma(keep[:], self.keep, writes=["keep"])
        P.dma(dtm[:], self.dtmask, writes=["dtm"])

        nbmax = cfg.rp // 128
        BT = {k: sb("BT_" + k, [128, nbmax, 16], F32) for k in ("dt", "aa", "ecum", "etot", "ww")}

        def precompute(b0, nchb, d):
            g0 = b0 // 128
            DT, AA, EC, ET, WW = (BT[k][:, 0:nchb, :] for k in ("dt", "aa", "ecum", "etot", "ww"))
            bc_h = lambda ap: ap.unsqueeze(1).broadcast_to([128, nchb, 16])
            bc_c = lambda ap: ap.unsqueeze(2).broadcast_to([128, nchb, 16])
            P.dma(DT, self.proj[b0:b0 + nchb * 128, 2560 + d * 16:2576 + d * 16].rearrange("(c p) f -> p c f", p=128), writes=["BTdt"])
            P.op("dve", lambda e: e.tensor_tensor(out=AA, in0=DT, in1=bc_h(dtb[:, d * 16:(d + 1) * 16]), op=ALU.add), reads=["BTdt", "dtb"], writes=["BTaa"])
            P.op("dve", lambda e: e.scalar_tensor_tensor(out=EC, in0=AA, scalar=-1.0, in1=AA, op0=ALU.mult, op1=ALU.min), reads=["BTaa"], writes=["BTecum"])
            P.op("act", lambda e: e.activation(out=EC, in_=EC, func=AF.Exp), reads=["BTecum"], writes=["BTecum"])
            P.op("act", lambda e: e.activation(out=EC, in_=EC, func=AF.Ln, bias=1.0), reads=["BTecum"], writes=["BTecum"])
            P.op("dve", lambda e: e.scalar_tensor_tensor(out=DT, in0=AA, scalar=0.0, in1=EC, op0=ALU.max, op1=ALU.add), reads=["BTaa", "BTecum"], writes=["BTdt"])
            P.op("dve", lambda e: e.tensor_tensor(out=DT, in0=DT, in1=bc_c(dtm[:, g0:g0 + nchb]), op=ALU.mult), reads=["BTdt", "dtm"], writes=["BTdt"])
            P.op("dve", lambda e: e.tensor_tensor(out=AA, in0=DT, in1=bc_h(Aexp[:, d * 16:(d + 1) * 16]), op=ALU.mult), reads=["BTdt", "Aexp"], writes=["BTaa"])
            for c0 in range(0, nchb, 32):
                n = min(32, nchb - c0)
                rhs = BT["aa"][:, c0:c0 + n, :].rearrange("p c h -> p (c h)")
                P.op("pe", lambda e, rhs=rhs, n=n: e.matmul(ps[:, 0, 0:n * 16], lhsT=TRI[d], rhs=rhs, start=True, stop=True), reads=["BTaa", "cst"], writes=["ps0"])
                P.op("pe", lambda e, rhs=rhs, n=n: e.matmul(ps[:, 1, 0:n * 16], lhsT=ones[:], rhs=rhs, start=True, stop=True), reads=["BTaa", "ones"], writes=["ps1"])
                P.op("act", lambda e, c0=c0, n=n: e.activation(out=BT["ecum"][:, c0:c0 + n, :].rearrange("p c h -> p (c h)"), in_=ps[:, 0, 0:n * 16], func=AF.Copy),
                     reads=["ps0"], writes=["BTecum"])
                P.op("dve", lambda e, c0=c0, n=n: e.tensor_copy(out=BT["etot"][:, c0:c0 + n, :].rearrange("p c h -> p (c h)"), in_=ps[:, 1, 0:n * 16]),
                     reads=["ps1"], writes=["BTetot"])
            P.op("dve", lambda e: e.tensor_tensor(out=WW, in0=ET, in1=EC, op=ALU.subtract), reads=["BTetot", "BTecum"], writes=["BTww"])
            P.op("act", lambda e: e.activation(out=WW, in_=WW, func=AF.Exp), reads=["BTww"], writes=["BTww"])
            P.op("dve", lambda e: e.tensor_tensor(out=WW, in0=WW, in1=DT, op=ALU.mult), reads=["BTww", "BTdt"], writes=["BTww"])
            P.op("act", lambda e: e.activation(out=EC, in_=EC, func=AF.Exp), reads=["BTecum"], writes=["BTecum"])
            P.op("act", lambda e: e.activation(out=ET, in_=ET, func=AF.Exp), reads=["BTetot"], writes=["BTetot"])
            P.op("dve", lambda e: e.tensor_tensor(out=ET, in0=ET, in1=bc_c(keep[:, d, g0:g0 + nchb]), op=ALU.mult), reads=["BTetot", "keep"], writes=["BTetot"])

        def step(r0, gch, lch, d, with_y, slot, first_pass):
            xk, xa = xsb.next()
            P.dma(xa[:], self.xsB[r0:r0 + 128, :], writes=[xk])
            dt = BT["dt"][:, lch, :]
            aa = BT["aa"][:, lch, :]
            ww = BT["ww"][:, lch, :]
            ecum = BT["ecum"][:, lch, :]
            etot = BT["etot"][:, lch, :]
            wkk = "BTww"
            xwk, xwa = xw.next()
            xs3 = xa[:, 0:1024].rearrange("p (h d) -> p h d", d=64)
            P.op("dve", lambda e: e.tensor_tensor(out=xwa[:].rearrange("p (h d) -> p h d", d=64), in0=xs3,
                                                  in1=ww.unsqueeze(2).broadcast_to([128, 16, 64]), op=ALU.mult), reads=[xk, wkk], writes=[xwk])
            Sk = "S%d" % d
            if with_y:
                sbk, sba = Sb.next()
                P.op("act", lambda e: e.activation(out=sba[:], in_=S[d][:], func=AF.Identity, scale=keep[:, d, gch:gch + 1]), reads=[Sk, "keep"], writes=[sbk])
                xdk, xda = xdt.next()
                P.op("pool", lambda e: e.tensor_tensor(out=xda[:].rearrange("p (h d) -> p h d", d=64), in0=xs3,
                                                       in1=dt.unsqueeze(2).broadcast_to([128, 16, 64]), op=ALU.mult), reads=[xk, "BTdt"], writes=[xdk])
                bk, ba = bct.next()
                P.dma(ba[:], self.bcT.rearrange("(c p) t -> p c t", p=128)[:, :, r0:r0 + 128], writes=[bk])
                for g in range(2):
                    P.op("pe", lambda e, g=g: e.matmul(ps[:, 1, g * 128:(g + 1) * 128], lhsT=ba[:, g, :], rhs=ba[:, 2 + g, :], start=True, stop=True),
                         reads=[bk], writes=["ps1"])
                P.op("dve", lambda e: e.tensor_tensor(out=CBm[:], in0=ps[:, 1, 0:256].rearrange("p (g l) -> p g l", g=2),
                                                      in1=TRI[d].unsqueeze(1).broadcast_to([128, 2, 128]), op=ALU.mult), reads=["ps1", "cst"], writes=["CBm"])
                auk, aua = AU.next()
                P.op("pool", lambda e: e.tensor_tensor(out=aua[:], in0=UU[d].unsqueeze(1).broadcast_to([128, 16, 128]),
                                                       in1=aa.unsqueeze(2).broadcast_to([128, 16, 128]), op=ALU.mult), reads=["BTaa", "cst"], writes=[auk])
                for h in range(16):
                    P.op("pe", lambda e, h=h: e.matmul(ps[:, 4 + h // 4, (h % 4) * 128:(h % 4 + 1) * 128], lhsT=aua[:, h, :], rhs=TRI[d], start=True, stop=True),
                         reads=[auk, "cst"], writes=["ps%d" % (4 + h // 4)])
                P.op("act", lambda e: e.activation(out=Mexp[:].rearrange("p h l -> p (h l)"), in_=ps[:, 4:8, :].rearrange("p b f -> p (b f)"), func=AF.Exp),
                     reads=["ps4", "ps5", "ps6", "ps7"], writes=["Mexp"])
                mk, ma = MT.next()
                for g in range(2):
                    P.op("dve", lambda e, g=g: e.tensor_tensor(out=ma[:, g * 8:(g + 1) * 8, :], in0=Mexp[:, g * 8:(g + 1) * 8, :],
                                                           in1=CBm[:, g, :].unsqueeze(1).broadcast_to([128, 8, 128]), op=ALU.mult),
                         reads=["Mexp", "CBm"], writes=[(mk, g)])
                for g in range(2):
                    P.op("pe", lambda e, g=g: e.matmul(ps[:, 4 + g, :], lhsT=ba[:, 2 + g, :], rhs=sba[:, g * 512:(g + 1) * 512], start=True, stop=True),
                         reads=[bk, sbk], writes=["ps%d" % (4 + g)])
                for h in range(16):
                    P.op("pe", lambda e, h=h: e.matmul(ps[:, 6 + h // 8, (h % 8) * 64:(h % 8 + 1) * 64], lhsT=ma[:, h, :], rhs=xda[:, h * 64:(h + 1) * 64],
                                                     start=True, stop=True), reads=[(mk, h // 8), xdk], writes=["ps%d" % (6 + h // 8)])
                tk_, ta = tmpy.next()
                P.op("dve", lambda e: e.tensor_tensor(out=ta[:].rearrange("p (h d) -> p h d", d=64),
                                                      in0=ps[:, 4:6, :].rearrange("p b (h d) -> p (b h) d", d=64),
                                                      in1=ecum.unsqueeze(2).broadcast_to([128, 16, 64]), op=ALU.mult),
                     reads=["ps4", "ps5", "BTecum"], writes=[tk_])
                ydst = yacc[:, slot, :] if first_pass else ta[:]
                P.op("dve", lambda e: e.tensor_tensor(out=ydst, in0=ta[:], in1=ps[:, 6:8, :].rearrange("p b f -> p (b f)"), op=ALU.add),
                     reads=[tk_, "ps6", "ps7"], writes=[("yacc", slot)] if first_pass else [tk_])
                if not first_pass:
                    P.op("pool", lambda e: e.tensor_tensor(out=yacc[:, slot, :], in0=yacc[:, slot, :], in1=ta[:], op=ALU.add),
                         reads=[tk_, ("yacc", slot)], writes=[("yacc", slot)])
            for g in range(2):
                P.op("pe", lambda e, g=g: e.matmul(ps[:, 2 + g, :], lhsT=xa[:, 1024 + g * 128:1152 + g * 128], rhs=xwa[:, g * 512:(g + 1) * 512],
                                                 start=True, stop=True), reads=[xk, xwk], writes=["ps%d" % (2 + g)])
            P.op("pool", lambda e: e.tensor_tensor(out=S[d][:].rearrange("p (h d) -> p h d", d=64), in0=S[d][:].rearrange("p (h d) -> p h d", d=64),
                                                   in1=etot.unsqueeze(2).broadcast_to([128, 16, 64]), op=ALU.mult), reads=[Sk, "BTetot"], writes=[Sk])
            P.op("dve", lambda e: e.tensor_tensor(out=S[d][:], in0=S[d][:], in1=ps[:, 2:4, :].rearrange("p b f -> p (b f)"), op=ALU.add),
                 reads=[Sk, "ps2", "ps3"], writes=[Sk])
            return xk, xa

        def finalize(r0, slot, oc, xk, xa):
            zk, za = zt.next()
            P.dma(za[:], self.proj[r0:r0 + 128, 1536:2560], writes=[zk])
            P.op("dve", lambda e: e.tensor_tensor(out=f1[:], in0=xa[:, 0:1024], in1=dsk[:], op=ALU.mult), reads=[xk, "dsk"], writes=["f1"])
            P.op("dve", lambda e: e.tensor_tensor(out=f1[:], in0=f1[:], in1=yacc[:, slot, :], op=ALU.add), reads=["f1", ("yacc", slot)], writes=["f1"])
            P.op("act", lambda e: e.activation(out=f2[:], in_=za[:], func=AF.Exp, scale=-1.0), reads=[zk], writes=["f2"])
            P.op("pool", lambda e: e.tensor_scalar(out=f2[:], in0=f2[:], scalar1=1.0, scalar2=None, op0=ALU.add), reads=["f2"], writes=["f2"])
            P.op("dve", lambda e: e.reciprocal(out=f2[:], in_=f2[:]), reads=["f2"], writes=["f2"])
            P.op("pool", lambda e: e.tensor_tensor(out=f2[:], in0=f2[:], in1=za[:], op=ALU.mult), reads=["f2", zk], writes=["f2"])
            P.op("dve", lambda e: e.tensor_tensor(out=f1[:], in0=f1[:], in1=f2[:], op=ALU.mult), reads=["f1", "f2"], writes=["f1"])
            fk, fs = fsm.next()
            for g in range(2):
                P.op("act", lambda e, g=g: e.activation(out=f2[:, g * 512:(g + 1) * 512], in_=f1[:, g * 512:(g + 1) * 512], func=AF.Square,
                                                      accum_out=fs[:, g:g + 1]), reads=["f1"], writes=["f2", (fk, g)])
            P.op("dve", lambda e: e.tensor_scalar(out=fs[:, 2:4], in0=fs[:, 0:2], scalar1=1.0 / 512, scalar2=EPS, op0=ALU.mult, op1=ALU.add),
                 reads=[(fk, 0), (fk, 1)], writes=[(fk, 2)])
            P.op("pool", lambda e: e.tensor_tensor(out=fs[:, 4:6], in0=fs[:, 2:4], in1=nh[:, 0:1].broadcast_to([128, 2]), op=ALU.pow),
                 reads=[(fk, 2), "nh"], writes=[(fk, 3)])
            ok_, oa = fo.next()
            for g in range(2):
                P.op("dve", lambda e, g=g: e.scalar_tensor_tensor(out=oa[:, g * 512:(g + 1) * 512], in0=f1[:, g * 512:(g + 1) * 512], scalar=fs[:, 4 + g:5 + g],
                                                                in1=gss[:, g * 512:(g + 1) * 512], op0=ALU.mult, op1=ALU.mult),
                     reads=["f1", (fk, 3), "gss"], writes=[(ok_, g)])
            pst = ps[:, 1, :].bitcast(BF16)
            for c in range(8):
                P.op("pe", lambda e, c=c: e.transpose(out=pst[:, c * 128:(c + 1) * 128], in_=oa[:, c * 128:(c + 1) * 128], identity=ident[:]),
                     reads=[(ok_, c // 4), "ident"], writes=["ps1"])
            sk_, sa = ost.next()
            P.op("act", lambda e: e.activation(out=sa[:].rearrange("p c t -> p (c t)"), in_=pst, func=AF.Copy), reads=["ps1"], writes=[sk_])
            P.op("pool", lambda e: [e.dma_start(out=self.oT[1024:2048, :].rearrange("(c p) t -> p c t", p=128)[:, :, oc:oc + 128], in_=sa[:])],
                 reads=[sk_], writes=[("oTs", oc), sk_], ndma=1)

        for (b0, nrows, oc0) in self.blocks:
            nchb = nrows // 128
            for d in range(2):
                P.op("pool", lambda e, d=d: e.memset(S[d][:], 0.0), reads=["S%d" % d], writes=["S%d" % d])
            others = list(range(nown, nchb))
            precompute(b0, nchb, 0)
            for ch in others:
                step(b0 + ch * 128, b0 // 128 + ch, ch, 0, False, None, True)
            for ch in range(nown):
                step(b0 + ch * 128, b0 // 128 + ch, ch, 0, True, ch, True)
            precompute(b0, nchb, 1)
            for ch in reversed(others):
                step(b0 + ch * 128, b0 // 128 + ch, ch, 1, False, None, False)
            for ch in reversed(range(nown)):
                xk, xa = step(b0 + ch * 128, b0 // 128 + ch, ch, 1, True, ch, False)
                finalize(b0 + ch * 128, ch, oc0 + ch * 128, xk, xa)

    def build(self):
        nc, P, cfg = self.nc, self.P, self.cfg
        self.declare()
        with contextlib.ExitStack() as top:
            self.psum = top.enter_context(nc.psum_tensor("psum", [128, 8, 512], F32))
            self.sb_words = 52992
            self.SB = top.enter_context(nc.sbuf_tensor("SB", [128, self.sb_words], F32))
            if cfg.do_w:
                with contextlib.ExitStack() as st:
                    self.sb_off = 0
                    self.phase_w(st)
                    P.barrier()
            if cfg.do_a:
                with contextlib.ExitStack() as st:
                    self.sb_off = 0
                    self.phase_a(st)
                    P.barrier()
            if cfg.do_b:
                for ph in cfg.b_phases:
                    self.sb_off = 0
                    getattr(self, "phase_" + ph)()
                    P.barrier()
            if cfg.do_c:
                with contextlib.ExitStack() as st:
                    self.sb_off = 0
                    self.phase_c(st)
                    P.barrier()
            P.emit()
        return nc


def _slot_heads():
    order = []
    for p in range(8):
        if p < 4:
            order += [p, 4 + p]
        else:
            order += [8 + (p - 4), 12 + (p - 4)]
    return order


def _partner():
    d = np.arange(64)
    a, b, i = d // 32, (d // 16) % 2, d % 16
    return a * 32 + (1 - b) * 16 + i


def _rope_rows(tok):
    f32 = np.float32
    inv = (f32(10000.0) ** (-(np.arange(16, dtype=f32) / f32(16)))).astype(f32)
    t = np.maximum(tok, 0)
    row = (t // 64).astype(f32)
    col = (t % 64).astype(f32)
    ang = np.stack([row[:, None] * inv, col[:, None] * inv], axis=1).astype(f32)
    ang = np.where((tok >= 0)[:, None, None], ang, f32(0))
    c = np.cos(ang).astype(f32)
    s = np.sin(ang).astype(f32)
    cos = np.broadcast_to(c[:, :, None, :], (len(tok), 2, 2, 16)).reshape(len(tok), 64)
    sgn = np.array([-1.0, 1.0], dtype=f32)[None, None, :, None]
    sin = (np.broadcast_to(s[:, :, None, :], (len(tok), 2, 2, 16)) * sgn).reshape(len(tok), 64)
    return np.concatenate([cos, sin], axis=1).astype(f32)


def prep_inputs(cfg, inp):
    f32 = np.float32
    NS, LS = cfg.ncores, cfg.ls
    xp = np.asarray(inp["x_prompt"], f32)[0]
    xs = np.asarray(inp["x_sample"], f32)
    meta = np.asarray(inp["meta_tokens"], f32)
    sq = lambda k: np.ascontiguousarray(np.asarray(inp[k], f32)[0])
    w_in = sq("w_in")
    heads = _slot_heads()
    qcols = np.concatenate([np.arange(h * 64, (h + 1) * 64) for h in heads])
    perm = np.concatenate([qcols, np.arange(1024, 1536), np.arange(1536, 2560), np.arange(4096, 4128), np.arange(2560, 4096)])
    w_out = sq("w_out")
    shared = {
        "ff1_wg": sq("ff1_w_gate"), "ff1_wu": sq("ff1_w_up"), "ff1_wd": sq("ff1_w_down"),
        "ff2_wg": sq("ff2_w_gate"), "ff2_wu": sq("ff2_w_up"), "ff2_wd": sq("ff2_w_down"),
        "w_in_p": np.ascontiguousarray(w_in[:, perm]),
        "w_out_p": np.ascontiguousarray(np.concatenate([w_out[qcols], w_out[1024:]], axis=0)),
        "ff1_pre": sq("ff1_norm_pre"), "ff1_post": sq("ff1_norm_post"), "mix_pre": sq("mix_norm_pre"),
        "mix_post": sq("mix_norm_post"), "ff2_pre": sq("ff2_norm_pre"), "ff2_post": sq("ff2_norm_post"),
        "ident": np.eye(128, dtype=f32),
        "conv_w": sq("conv_w"), "conv_b": sq("conv_b"),
        "a_log": sq("a_log").reshape(32), "dt_bias": sq("dt_bias").reshape(32),
        "dskip": np.repeat(sq("d_skip"), 64), "ssm_norm": sq("ssm_norm"),
    }
    pt = _partner()
    qn, kn = sq("q_norm"), sq("k_norm")
    shared["qkg"] = np.stack([qn, qn[pt], kn, kn[pt]]).astype(f32)
    i = np.arange(128)
    tri_f = (i[:, None] <= i[None, :]).astype(f32)
    tri_b = (i[:, None] >= i[None, :]).astype(f32)
    u_f = (i[:, None] > i[None, :]).astype(f32)
    u_b = (i[:, None] < i[None, :]).astype(f32)
    shared["cst"] = np.ascontiguousarray(np.stack([np.eye(128, dtype=f32), tri_f, tri_b, u_f, u_b], axis=1))
    seam_x = np.concatenate([np.zeros((112, D), f32), meta], axis=0)
    seam_tok = np.full(128, -1)
    seam_valid = np.concatenate([np.zeros(112), np.ones(16)])
    maps = []
    for c in range(NS):
        own = np.arange(c * LS, (c + 1) * LS)
        after = np.arange((c + 1) * LS, NS * LS)
        before = np.arange(0, c * LS)
        xin = np.concatenate([xp[own], xp[after], seam_x, xp[before], xs[c], seam_x], axis=0)
        tok = np.concatenate([own, after, seam_tok, before, np.arange(LS), seam_tok])
        valid = np.concatenate([np.ones(LS + len(after)), seam_valid, np.ones(len(before)), np.ones(LS), seam_valid])
        assert xin.shape[0] == cfg.rtot
        nch = cfg.rtot // 128
        lay = lambda v: np.ascontiguousarray(v.reshape(nch, 128).T.astype(f32))
        seam_ch = (LS + len(after)) // 128
        keepf = np.ones(nch, f32)
        keepb = np.ones(nch, f32)
        keepf[seam_ch] = 0
        keepb[seam_ch - 1] = 0
        s_seam = cfg.rs0 // 128 + LS // 128
        keepf[s_seam] = 0
        keepb[s_seam - 1] = 0
        m = dict(shared)
        m["xin"] = np.ascontiguousarray(xin)
        m["ropet"] = _rope_rows(tok)
        m["kbias"] = lay((1 - valid) * -30000.0)
        m["dtmask"] = lay(valid)
        m["keep"] = np.ascontiguousarray(np.stack([np.broadcast_to(keepf, (128, nch)), np.broadcast_to(keepb, (128, nch))], axis=1).astype(f32))
        maps.append(m)
    return maps


_CACHE = {}


def kernel(**inputs):
    cfg = Cfg()
    if "nc" not in _CACHE:
        _CACHE["nc"] = Builder(cfg).build()
    nc = _CACHE["nc"]
    maps = prep_inputs(cfg, inputs)
    res = run_bass_kernel_spmd(nc, maps, core_ids=list(range(cfg.ncores)))
    ys = [np.asarray(r["yout"]) for r in res.results]
    y_prompt = np.concatenate([y[0:cfg.ls] for y in ys], axis=0)[None]
    y_sample = np.stack([y[cfg.ls:2 * cfg.ls] for y in ys], axis=0)
    return (y_prompt.astype(np.float32), y_sample.astype(np.float32))
```

```python
import contextlib
import numpy as np
import ml_dtypes
import concourse.bass as bass
import concourse.mybir as mybir
from concourse.bass_utils import run_bass_kernel_spmd

import concourse.bass as bass
import concourse.mybir as mybir

F32 = mybir.dt.float32
BF16 = mybir.dt.bfloat16
AF = mybir.ActivationFunctionType
ALU = mybir.AluOpType
AX = mybir.AxisListType

N_DMA_SEMS = 56
N_SP_SEMS = 36


class Op:
    __slots__ = ("eng", "fn", "deps", "needs_inc", "sem", "semval", "ndma", "idx")


class Prog:
    ENGS = ("pe", "act", "dve", "pool", "sp")

    def __init__(self, nc):
        self.nc = nc
        self.ops = {e: [] for e in self.ENGS}
        self.last_w = {}
        self.readers = {}
        self.dma_rr = 0
        self.dma_rr2 = 0
        self.dma_sem_last = [None] * N_DMA_SEMS
        self.dma_sem_cnt = [0] * N_DMA_SEMS
        self.out_dmas = []

    def op(self, eng, fn, reads=(), writes=(), ndma=0, is_output=False):
        o = Op()
        o.eng = eng
        o.fn = fn
        o.deps = []
        o.needs_inc = False
        o.sem = None
        o.semval = 0
        o.ndma = ndma
        deps = {}
        for k in reads:
            w = self.last_w.get(k)
            if w is not None:
                deps[id(w)] = w
        for k in writes:
            w = self.last_w.get(k)
            if w is not None:
                deps[id(w)] = w
            for r in self.readers.get(k, {}).values():
                if isinstance(r, list):
                    for rr in r:
                        deps[id(rr)] = rr
                else:
                    deps[id(r)] = r
        if ndma:
            if eng == "sp":
                i = self.dma_rr % N_SP_SEMS
                self.dma_rr += 1
            else:
                i = N_SP_SEMS + self.dma_rr2 % (N_DMA_SEMS - N_SP_SEMS)
                self.dma_rr2 += 1
            prev = self.dma_sem_last[i]
            if prev is not None:
                deps[id(prev)] = prev
            self.dma_sem_cnt[i] += 16 * ndma
            o.sem = ("dma", i)
            o.semval = self.dma_sem_cnt[i]
            self.dma_sem_last[i] = o
            o.needs_inc = True
        for d in deps.values():
            if d is o:
                continue
            if d.eng == "pe" and eng == "pe":
                continue
            o.deps.append(d)
            d.needs_inc = True
        for k in reads:
            rd = self.readers.setdefault(k, {})
            if ndma:
                rd.setdefault("dma", []).append(o)
            else:
                rd[eng] = o
        for k in writes:
            self.last_w[k] = o
            self.readers[k] = {}
        self.ops[eng].append(o)
        if is_output:
            self.out_dmas.append(o)
        return o

    def barrier(self):
        lasts = []
        for e in ("pe", "act", "dve", "pool"):
            for o in reversed(self.ops[e]):
                if o.fn is not None and not o.ndma:
                    lasts.append(o)
                    break
        dl = [o for o in self.dma_sem_last if o is not None]
        for e in self.ENGS:
            b = Op()
            b.eng = e; b.fn = None; b.needs_inc = False; b.sem = None; b.semval = 0; b.ndma = 0
            b.deps = [o for o in lasts if o.eng != e] + dl
            for d in b.deps:
                d.needs_inc = True
            self.ops[e].append(b)
        self.last_w = {}
        self.readers = {}

    def dma(self, out, in_, reads=(), writes=(), is_output=False, **kw):
        return self.op("sp", lambda e: [e.dma_start(out=out, in_=in_, **kw)], reads, writes,
                       ndma=1, is_output=is_output)

    def emit(self):
        nc = self.nc
        fin = Op()
        fin.eng = "sp"; fin.fn = None; fin.deps = list(self.out_dmas); fin.needs_inc = False
        fin.sem = None; fin.semval = 0; fin.ndma = 0
        self.ops["sp"].append(fin)
        for e in self.ENGS:
            c = 0
            for o in self.ops[e]:
                if o.ndma:
                    continue
                if o.needs_inc:
                    c += 1
                    o.sem = ("eng", e)
                    o.semval = c
        import contextlib
        with contextlib.ExitStack() as st:
            sems = {}
            for e in self.ENGS:
                sems[("eng", e)] = st.enter_context(nc.semaphore("s_" + e))
            for i in range(N_DMA_SEMS):
                sems[("dma", i)] = st.enter_context(nc.semaphore("s_dma%d" % i))
            block = st.enter_context(nc.Block())

            def run(ename):
                def body(engine):
                    known = {}
                    for o in self.ops[ename]:
                        for d in o.deps:
                            if known.get(d.sem, 0) < d.semval:
                                engine.wait_ge(sems[d.sem], d.semval)
                                known[d.sem] = d.semval
                        if o.fn is None:
                            continue
                        r = o.fn(engine)
                        if o.ndma:
                            for ins in r:
                                ins.then_inc(sems[o.sem], 16)
                        elif o.needs_inc:
                            ins = r[-1] if isinstance(r, (list, tuple)) else r
                            ins.then_inc(sems[o.sem], 1)
                return body

            block.tensor(run("pe"))
            block.scalar(run("act"))
            block.vector(run("dve"))
            block.gpsimd(run("pool"))
            block.sync(run("sp"))


class Rot:
    def __init__(self, name, aps):
        self.name = name; self.aps = aps; self.i = 0

    def next(self):
        k = self.i % len(self.aps)
        self.i += 1
        return "%s%d" % (self.name, k), self.aps[k]


D = 2048
KC = 16
EPS = 1e-6
N_META = 16
ATT_W = 1024
SSM_W = 1024
CONV_DIM = 1536
NTM = 2592
IN_PROJ = 4128


class Cfg:
    def __init__(self, dff=5632, ls=2048, ncores=8, do_w=True, do_a=True, do_b=True, do_c=True, debug=False):
        self.dff = dff
        self.ls = ls
        self.ncores = ncores
        self.nts = ls // 512
        self.rp = ncores * ls + 128
        self.rs0 = self.rp
        self.rtot = self.rp + ls + 128
        self.do_w, self.do_a, self.do_b, self.do_c = do_w, do_a, do_b, do_c
        self.debug = debug
        self.b_phases = ("b0", "b1", "b2", "b3")


class Builder:
    def __init__(self, cfg):
        self.cfg = cfg
        self.nc = bass.Bass("TRN2", target_bir_lowering=False)
        self.P = Prog(self.nc)

    def salloc(self, name, shape, dtype):
        assert shape[0] == 128
        n = 1
        for d in shape[1:]:
            n *= d
        esz = 4 if dtype == F32 else 2
        words = (n * esz + 3) // 4
        words = (words + 7) // 8 * 8
        off = self.sb_off
        self.sb_off += words
        assert self.sb_off <= self.sb_words, (name, self.sb_off, self.sb_words)
        v = self.SB[:, off:off + words]
        if dtype != F32:
            v = v.bitcast(dtype)
        v = v[:, 0:n]
        if len(shape) == 3:
            v = v.rearrange("p (a b) -> p a b", a=shape[1])
        return v

    def declare(self):
        nc, cfg = self.nc, self.cfg
        dff = cfg.dff
        I = lambda n, s, d=F32: nc.dram_tensor(n, s, d, kind="ExternalInput").ap()
        S = lambda n, s, d: nc.dram_tensor(n, s, d, kind=("ExternalOutput" if cfg.debug else "Internal")).ap()
        self.xin = I("xin", [cfg.rtot, D])
        self.wf = {
            "g1": I("ff1_wg", [D, dff]), "u1": I("ff1_wu", [D, dff]), "d1": I("ff1_wd", [dff, D]),
            "win": I("w_in_p", [D, IN_PROJ]), "wout": I("w_out_p", [D, D]),
            "g2": I("ff2_wg", [D, dff]), "u2": I("ff2_wu", [D, dff]), "d2": I("ff2_wd", [dff, D]),
        }
        self.gains = {k: I(k, [D]) for k in ("ff1_pre", "ff1_post", "mix_pre", "mix_post", "ff2_pre", "ff2_post")}
        self.ident_in = I("ident", [128, 128])
        self.yout = nc.dram_tensor("yout", [2 * cfg.ls, D], F32, kind="ExternalOutput").ap()
        self.own_rows = [(0, 0), (cfg.rs0, cfg.ls)]
        self.wb = {k: S("wb_" + k, list(v.shape), BF16) for k, v in self.wf.items()}
        self.h1 = S("h1", [cfg.rtot, D], F32)
        self.proj = S("proj", [cfg.rtot, NTM], F32)
        self.xbcT = S("xbcT", [CONV_DIM, cfg.rtot], F32)
        if cfg.do_b:
            self.oT = S("oT", [D, 2 * cfg.ls], BF16)
        else:
            self.oT = I("oT", [D, 2 * cfg.ls], BF16)
        self.declare_b()

    def phase_w(self, st):
        nc, P = self.nc, self.P
        sb = self.salloc
        stg = Rot("wstg", [sb("wstg%d" % i, [128, 4096], F32) for i in range(3)])
        cvt = Rot("wcvt", [sb("wcvt%d" % i, [128, 4096], BF16) for i in range(3)])
        engs = ["pool", "dve", "act"]
        n = 0
        for name, w in self.wf.items():
            K, N = w.shape
            per = K * N // 128
            src = w.rearrange("(p r) n -> p (r n)", p=128)
            dst = self.wb[name].rearrange("(p r) n -> p (r n)", p=128)
            for c0 in range(0, per, 4096):
                sz = min(4096, per - c0)
                sk, sap = stg.next()
                ck, cap = cvt.next()
                P.dma(sap[:, 0:sz], src[:, c0:c0 + sz], writes=[sk])
                e = engs[n % 3]
                n += 1
                if e == "act":
                    P.op("act", lambda en, o=cap[:, 0:sz], i=sap[:, 0:sz]: en.activation(out=o, in_=i, func=AF.Copy),
                         reads=[sk], writes=[ck])
                else:
                    P.op(e, lambda en, o=cap[:, 0:sz], i=sap[:, 0:sz]: en.tensor_copy(out=o, in_=i),
                         reads=[sk], writes=[ck])
                P.op("pool", lambda en, o=dst[:, c0:c0 + sz], i=cap[:, 0:sz]: [en.dma_start(out=o, in_=i)],
                     reads=[ck], writes=[("wbd", name, c0)], ndma=1)

    def alloc_ac(self, st, with_stage=True):
        nc = self.nc
        sb = self.salloc
        self.xt = Rot("xt", [sb("xt%d" % i, [128, D], F32) for i in range(5)])
        self.ub = [sb("ub%d" % i, [128, KC, 512], BF16) for i in range(2)]
        self.hmid = sb("hmid", [128, 44, 512], BF16)
        self.wbuf = Rot("wbuf", [sb("wbuf%d" % i, [128, 4096], BF16) for i in range(6)])
        self.un = Rot("un", [sb("un%d" % i, [128, D], BF16) for i in range(4)])
        self.junk = sb("junk", [128, D], BF16)
        self.sg = Rot("sg", [sb("sg%d" % i, [128, 512], F32) for i in range(2)])
        self.small = Rot("small", [sb("small%d" % i, [128, 4], F32) for i in range(4)])
        self.ident = sb("identb", [128, 128], BF16)
        self.identf = sb("identf", [128, 128], F32)
        self.nh = sb("nh", [128, 1], F32)
        self.gcol = {}
        self.gph = {}
        if with_stage:
            self.stage = Rot("stage", [sb("stage%d" % i, [128, 512], F32) for i in range(2)])

    def init_consts(self, pre_names, post_names):
        P, nc = self.P, self.nc
        P.dma(self.identf[:], self.ident_in, writes=["identf"])
        P.op("dve", lambda e: e.tensor_copy(out=self.ident[:], in_=self.identf[:]), reads=["identf"], writes=["ident"])
        P.op("dve", lambda e: e.memset(self.nh[:], -0.5), writes=["nh"])
        for nm in pre_names:
            t = self.gcol[nm]
            P.op("sp", lambda e, t=t, nm=nm: [e.dma_start(out=t[:], in_=self.gains[nm].rearrange("(c p) -> p c", p=128),
                                                       allow_slow_non_contiguous=True)], writes=["gcol_" + nm], ndma=1)
        for nm in post_names:
            t = self.gph[nm]
            P.dma(t[:], self.gains[nm].partition_broadcast(128), writes=["gph_" + nm])
            P.op("dve", lambda e, t=t: e.tensor_scalar(out=t[:], in0=t[:], scalar1=0.5, scalar2=None, op0=ALU.mult),
                 reads=["gph_" + nm], writes=["gph_" + nm])

    def rstd_from(self, src_key, src_ap, junk_ap, junk_key):
        P = self.P
        sk, sm = self.small.next()
        P.op("act", lambda e: e.activation(out=junk_ap, in_=src_ap, func=AF.Square, accum_out=sm[:, 0:1]),
             reads=[src_key], writes=[junk_key, sk])
        P.op("dve", lambda e: e.tensor_scalar(out=sm[:, 1:2], in0=sm[:, 0:1], scalar1=1.0 / D, scalar2=EPS,
                                              op0=ALU.mult, op1=ALU.add), reads=[sk], writes=[sk])
        P.op("pool", lambda e: e.tensor_tensor(out=sm[:, 2:3], in0=sm[:, 1:2], in1=self.nh[:], op=ALU.pow),
             reads=[sk, "nh"], writes=[sk])
        return sk, sm[:, 2:3]

    def prenorm_a(self, xk, xap):
        P = self.P
        uk, un = self.un.next()
        rk, rstd = self.rstd_from(xk, xap[:], self.junk[:], "junk")
        P.op("dve", lambda e: e.tensor_scalar(out=un[:], in0=xap[:], scalar1=rstd, scalar2=None, op0=ALU.mult),
             reads=[xk, rk], writes=[uk])
        return uk, un

    def prenorm(self, xk, xap, gname, ub_i, s):
        uk, un = self.prenorm_a(xk, xap)
        self.prenorm_b(uk, un, gname, ub_i, s)

    def prenorm_b(self, uk, un, gname, ub_i, s):
        P = self.P
        pst = self.psum[:, 0:2, :].bitcast(BF16)
        for c in range(KC):
            P.op("pe", lambda e, c=c: e.transpose(out=pst[:, c // 8, (c % 8) * 128:(c % 8 + 1) * 128],
                                                  in_=un[:, c * 128:(c + 1) * 128], identity=self.ident[:]),
                 reads=[uk, "ident"], writes=["ps0" if c < 8 else "ps1"])
        gc = self.gcol[gname]
        ub = self.ub[ub_i]
        for hf in range(2):
            P.op("dve", lambda e, hf=hf: e.tensor_tensor(
                out=ub[:, hf * 8:(hf + 1) * 8, s * 128:(s + 1) * 128],
                in0=pst[:, hf, :].rearrange("p (c t) -> p c t", t=128),
                in1=gc[:, hf * 8:(hf + 1) * 8].unsqueeze(2).broadcast_to([128, 8, 128]), op=ALU.mult),
                reads=["ps%d" % hf, "gcol_" + gname], writes=[("ub", ub_i, s)])

    def load_cunit(self, name, c0, ncols):
        k, buf = self.wbuf.next()
        v = buf[:, 0:KC * ncols].rearrange("p (kc f) -> p kc f", kc=KC)
        src = self.wb[name].rearrange("(kc p) f -> p kc f", p=128)[:, :, c0:c0 + ncols]
        self.P.dma(v, src, writes=[k])
        return k, v

    def load_runit(self, name, kc0, nkc=2):
        k, buf = self.wbuf.next()
        v = buf[:, 0:nkc * D].rearrange("p (kc f) -> p kc f", kc=nkc)
        src = self.wb[name].rearrange("(kc p) f -> p kc f", p=128)[:, kc0:kc0 + nkc, :]
        self.P.dma(v, src, writes=[k])
        return k, v

    def gate_up(self, ub_i, nsub, gname, uname):
        P, cfg = self.P, self.cfg
        T = nsub * 128
        ub = self.ub[ub_i]
        ubkeys = [("ub", ub_i, s) for s in range(nsub)]
        nstep = cfg.dff // 256
        for step in range(nstep):
            gk, gv = self.load_cunit(gname, step * 256, 256)
            uk, uv = self.load_cunit(uname, step * 256, 256)
            b0 = (step % 2) * 4
            for jj in range(2):
                j = step * 2 + jj
                bg, bu = b0 + jj, b0 + 2 + jj
                for kc in range(KC):
                    P.op("pe", lambda e, kc=kc, jj=jj, bg=bg, gv=gv: e.matmul(
                        self.psum[:, bg, 0:T], lhsT=gv[:, kc, jj * 128:(jj + 1) * 128], rhs=ub[:, kc, 0:T],
                        start=(kc == 0), stop=(kc == KC - 1)), reads=[gk] + ubkeys, writes=["ps%d" % bg])
                for kc in range(KC):
                    P.op("pe", lambda e, kc=kc, jj=jj, bu=bu, uv=uv: e.matmul(
                        self.psum[:, bu, 0:T], lhsT=uv[:, kc, jj * 128:(jj + 1) * 128], rhs=ub[:, kc, 0:T],
                        start=(kc == 0), stop=(kc == KC - 1)), reads=[uk] + ubkeys, writes=["ps%d" % bu])
                sk, sg = self.sg.next()
                P.op("act", lambda e, bg=bg, sg=sg: e.activation(out=sg[:, 0:T], in_=self.psum[:, bg, 0:T], func=AF.Silu),
                     reads=["ps%d" % bg], writes=[sk])
                P.op("dve", lambda e, bu=bu, sg=sg, j=j: e.tensor_tensor(
                    out=self.hmid[:, j, 0:T], in0=sg[:, 0:T], in1=self.psum[:, bu, 0:T], op=ALU.mult),
                    reads=[sk, "ps%d" % bu], writes=[("hm", j)])

    def proj_rows(self, wname, nkc, lhs_fn, lhs_keys_fn, nsub, epilogue):
        P = self.P
        for sp0 in range(0, nsub, 2):
            ss = list(range(sp0, min(sp0 + 2, nsub)))
            for kc0 in range(0, nkc, 2):
                wk, wv = self.load_runit(wname, kc0, 2)
                for kk in range(2):
                    kc = kc0 + kk
                    for s in ss:
                        for blk in range(4):
                            b = (s % 2) * 4 + blk
                            P.op("pe", lambda e, kc=kc, kk=kk, s=s, blk=blk, b=b, wv=wv: e.matmul(
                                self.psum[:, b, :], lhsT=lhs_fn(kc, s), rhs=wv[:, kk, blk * 512:(blk + 1) * 512],
                                start=(kc == 0), stop=(kc == nkc - 1)),
                                reads=[wk] + lhs_keys_fn(kc, s), writes=["ps%d" % b])
            for s in ss:
                b0 = (s % 2) * 4
                epilogue(s, self.psum[:, b0:b0 + 4, :].rearrange("p b f -> p (b f)"), ["ps%d" % (b0 + i) for i in range(4)])

    def post_residual(self, yps, ykeys, xk, xap, gpost):
        P = self.P
        rk, rstd = self.rstd_from_multi(ykeys, yps)
        gph = self.gph[gpost]
        P.op("dve", lambda e: e.scalar_tensor_tensor(out=yps, in0=yps, scalar=rstd, in1=gph[:], op0=ALU.mult, op1=ALU.mult),
             reads=ykeys + [rk, "gph_" + gpost], writes=ykeys)
        P.op("dve", lambda e: e.tensor_tensor(out=xap[:], in0=yps, in1=xap[:], op=ALU.add),
             reads=ykeys + [xk], writes=[xk])

    def rstd_from_multi(self, keys, src_ap):
        P = self.P
        sk, sm = self.small.next()
        P.op("act", lambda e: e.activation(out=self.junk[:], in_=src_ap, func=AF.Square, accum_out=sm[:, 0:1]),
             reads=list(keys), writes=["junk", sk])
        P.op("dve", lambda e: e.tensor_scalar(out=sm[:, 1:2], in0=sm[:, 0:1], scalar1=1.0 / D, scalar2=EPS,
                                              op0=ALU.mult, op1=ALU.add), reads=[sk], writes=[sk])
        P.op("pool", lambda e: e.tensor_tensor(out=sm[:, 2:3], in0=sm[:, 1:2], in1=self.nh[:], op=ALU.pow),
             reads=[sk, "nh"], writes=[sk])
        return sk, sm[:, 2:3]

    def ffn(self, xts, ub_i, gn, un_, dn, gpre, gpost, pre=None, after_post=None):
        nsub = len(xts)
        for s, (xk, xap) in enumerate(xts):
            if pre is not None:
                self.prenorm_b(pre[s][0], pre[s][1], gpre, ub_i, s)
            else:
                self.prenorm(xk, xap, gpre, ub_i, s)
        self.gate_up(ub_i, nsub, gn, un_)
        nfc = self.cfg.dff // 128

        def epi(s, yps, ykeys):
            self.post_residual(yps, ykeys, xts[s][0], xts[s][1], gpost)
            if after_post is not None:
                after_post(s)
        self.proj_rows(dn, nfc, lambda kc, s: self.hmid[:, kc, s * 128:(s + 1) * 128],
                       lambda kc, s: [("hm", kc)], nsub, epi)

    def phase_a(self, st):
        nc, P, cfg = self.nc, self.P, self.cfg
        self.alloc_ac(st)
        sb = self.salloc
        for nm in ("ff1_pre", "mix_pre"):
            self.gcol[nm] = sb("gcol_" + nm, [128, KC], F32)
        self.gph["ff1_post"] = sb("gph_ff1_post", [128, D], F32)
        self.init_consts(("ff1_pre", "mix_pre"), ("ff1_post",))
        tiles = []
        for (b0, nrows) in ((0, cfg.rp), (cfg.rs0, cfg.ls + 128)):
            r = 0
            while r < nrows:
                n = min(512, nrows - r)
                tiles.append((b0 + r, n // 128, r < cfg.ls))
                r += n
        projv = self.hmid[:, 0:41, :].rearrange("p a b -> p (a b)").bitcast(F32)
        def load_tile(r0, nsub):
            xts = []
            for s in range(nsub):
                xk, xap = self.xt.next()
                P.dma(xap[:], self.xin[r0 + s * 128:r0 + (s + 1) * 128, :], writes=[xk])
                xts.append((xk, xap))
            return xts

        cur_xts = load_tile(tiles[0][0], tiles[0][1])
        cur_pre = None
        for ti, (r0, nsub, own) in enumerate(tiles):
            T = nsub * 128
            xts = cur_xts
            pend = [None] * nsub

            def after_post(s, xts=xts, pend=pend, own=own, r0=r0):
                xk, xap = xts[s]
                if own:
                    P.op("pool", lambda e: [e.dma_start(out=self.h1[r0 + s * 128:r0 + (s + 1) * 128, :], in_=xap[:])],
                         reads=[xk], writes=[("h1", r0, s)], ndma=1)
                pend[s] = self.prenorm_a(xk, xap)
            self.ffn(xts, 0, "g1", "u1", "d1", "ff1_pre", "ff1_post", pre=cur_pre, after_post=after_post)
            for s in range(nsub):
                self.prenorm_b(pend[s][0], pend[s][1], "mix_pre", 1, s)
            if ti + 1 < len(tiles):
                cur_xts = load_tile(tiles[ti + 1][0], tiles[ti + 1][1])
                cur_pre = [self.prenorm_a(xk, xap) for (xk, xap) in cur_xts]
            ub = self.ub[1]
            hmkeys = [("hm", j) for j in range(44)]
            c0 = 0
            u = 0
            first = True
            while c0 < NTM:
                ncols = min(256, NTM - c0)
                if not own and not (1024 <= c0 < 1536 or c0 >= 2560):
                    c0 += ncols
                    u += 1
                    continue
                wk, wv = self.load_cunit("win", c0, ncols)
                for s in range(nsub):
                    b = (u % 2) * 4 + s
                    for kc in range(KC):
                        P.op("pe", lambda e, kc=kc, s=s, b=b, wv=wv, ncols=ncols: e.matmul(
                            self.psum[:, b, 0:ncols], lhsT=ub[:, kc, s * 128:(s + 1) * 128], rhs=wv[:, kc, :],
                            start=(kc == 0), stop=(kc == KC - 1)), reads=[wk, ("ub", 1, s)], writes=["ps%d" % b])
                    eng = "act" if (u + s) % 2 == 0 else "dve"
                    dstv = projv[:, s * NTM + c0:s * NTM + c0 + ncols]
                    if eng == "act":
                        P.op("act", lambda e, b=b, dstv=dstv, ncols=ncols: e.activation(out=dstv, in_=self.psum[:, b, 0:ncols], func=AF.Copy),
                             reads=["ps%d" % b] + (hmkeys if first else []), writes=[("pj", s, u)] + (hmkeys if first else []))
                    else:
                        P.op("dve", lambda e, b=b, dstv=dstv, ncols=ncols: e.tensor_copy(out=dstv, in_=self.psum[:, b, 0:ncols]),
                             reads=["ps%d" % b] + (hmkeys if first else []), writes=[("pj", s, u)] + (hmkeys if first else []))
                    first = False
                c0 += ncols
                u += 1
            nu = u
            P.op("pool", lambda e, r0=r0, nsub=nsub: [e.dma_start(out=self.proj[r0 + s * 128:r0 + (s + 1) * 128, :],
                                                                 in_=projv[:, s * NTM:(s + 1) * NTM]) for s in range(nsub)],
                 reads=[("pj", s, uu) for uu in range(nu) for s in range(nsub)], writes=[("projd", r0)] + hmkeys, ndma=nsub)
            for cu in range(CONV_DIM // 256):
                wk, wv = self.load_cunit("win", NTM + cu * 256, 256)
                for jj in range(2):
                    j = cu * 2 + jj
                    b = j % 8
                    for kc in range(KC):
                        P.op("pe", lambda e, kc=kc, jj=jj, b=b, wv=wv, T=T: e.matmul(
                            self.psum[:, b, 0:T], lhsT=wv[:, kc, jj * 128:(jj + 1) * 128], rhs=ub[:, kc, 0:T],
                            start=(kc == 0), stop=(kc == KC - 1)),
                            reads=[wk] + [("ub", 1, s) for s in range(nsub)], writes=["ps%d" % b])
                    sk, sg = self.stage.next()
                    if j % 2 == 0:
                        P.op("act", lambda e, b=b, sg=sg, T=T: e.activation(out=sg[:, 0:T], in_=self.psum[:, b, 0:T], func=AF.Copy),
                             reads=["ps%d" % b], writes=[sk])
                    else:
                        P.op("dve", lambda e, b=b, sg=sg, T=T: e.tensor_copy(out=sg[:, 0:T], in_=self.psum[:, b, 0:T]),
                             reads=["ps%d" % b], writes=[sk])
                    P.op("pool", lambda e, j=j, sg=sg, r0=r0, T=T: [e.dma_start(out=self.xbcT[j * 128:(j + 1) * 128, r0:r0 + T], in_=sg[:, 0:T])],
                         reads=[sk], writes=[("xbcd", j, r0)], ndma=1)

    def phase_c(self, st):
        nc, P, cfg = self.nc, self.P, self.cfg
        self.alloc_ac(st, with_stage=False)
        sb = self.salloc
        self.gcol["ff2_pre"] = sb("gcol_ff2_pre", [128, KC], F32)
        self.gph["mix_post"] = sb("gph_mix_post", [128, D], F32)
        self.gph["ff2_post"] = sb("gph_ff2_post", [128, D], F32)
        self.init_consts(("ff2_pre",), ("mix_post", "ff2_post"))
        P.op("dve", lambda e: e.tensor_scalar(out=self.gph["mix_post"][:], in0=self.gph["mix_post"][:], scalar1=2.0,
                                              scalar2=None, op0=ALU.mult), reads=["gph_mix_post"], writes=["gph_mix_post"])
        ctiles = [(sr0 + t * 512, or0 + t * 512) for (sr0, or0) in self.own_rows for t in range(cfg.nts)]
        for (hr0, r0) in ctiles:
            xts = []
            for s in range(4):
                xk, xap = self.xt.next()
                P.dma(xap[:], self.h1[hr0 + s * 128:hr0 + (s + 1) * 128, :], writes=[xk])
                xts.append((xk, xap))
            ub = self.ub[1]
            P.dma(ub[:], self.oT.rearrange("(kc p) t -> p kc t", p=128)[:, :, r0:r0 + 512], writes=[("ub", 1, s) for s in range(4)])
            pend = [None] * 4

            def epi_c(s, yps, ykeys, xts=xts, pend=pend):
                self.post_residual(yps, ykeys, xts[s][0], xts[s][1], "mix_post")
                pend[s] = self.prenorm_a(xts[s][0], xts[s][1])
            self.proj_rows("wout", KC, lambda kc, s: ub[:, kc, s * 128:(s + 1) * 128],
                           lambda kc, s: [("ub", 1, s)], 4, epi_c)

            def store(s, xts=xts, r0=r0):
                xk, xap = xts[s]
                P.op("pool", lambda e: [e.dma_start(out=self.yout[r0 + s * 128:r0 + (s + 1) * 128, :], in_=xap[:])],
                     reads=[xk], ndma=1, is_output=True)
            self.ffn(xts, 0, "g2", "u2", "d2", "ff2_pre", "ff2_post", pre=pend, after_post=store)

    def declare_b(self):
        nc, cfg = self.nc, self.cfg
        I = lambda n, s, d=F32: nc.dram_tensor(n, s, d, kind="ExternalInput").ap()
        S = lambda n, s, d: nc.dram_tensor(n, s, d, kind=("ExternalOutput" if cfg.debug else "Internal")).ap()
        nch = cfg.rtot // 128
        self.cst = I("cst", [128, 5, 128])
        self.ropet = I("ropet", [cfg.rtot, 128])
        self.qkg = I("qkg", [4, 64])
        self.kbias = I("kbias", [128, nch])
        self.dtmask = I("dtmask", [128, nch])
        self.keep = I("keep", [128, 2, nch])
        self.conv_w = I("conv_w", [5, CONV_DIM])
        self.conv_b = I("conv_b", [CONV_DIM])
        self.a_log = I("a_log", [32])
        self.dt_bias = I("dt_bias", [32])
        self.dskip = I("dskip", [SSM_W])
        self.ssm_norm = I("ssm_norm", [SSM_W])
        self.xsB = S("xsB", [cfg.rtot, 1280], BF16)
        self.bcT = S("bcT", [512, cfg.rtot], BF16)
        self.qT = S("qT", [1024, 2 * cfg.ls], BF16)
        self.kT = S("kT", [256, cfg.rtot], BF16)
        self.vx = S("vx", [cfg.rtot, 640], BF16)
        self.blocks = [(0, cfg.rp, 0), (cfg.rs0, cfg.ls + 128, cfg.ls)]

    def phase_b0(self):
        nc, P, cfg = self.nc, self.P, self.cfg
        sb = self.salloc
        identf = sb("identf", [128, 128], F32)
        ident = sb("ident", [128, 128], BF16)
        wcol = sb("wcol", [128, 5, 12], F32)
        bcol = sb("bcol", [128, 12], F32)
        xin = Rot("cxin", [sb("cxin%d" % i, [128, 1028], F32) for i in range(4)])
        acc = Rot("cacc", [sb("cacc%d" % i, [128, 1024], F32) for i in range(3)])
        cvo = [sb("cvo%d" % i, [128, 1024], BF16) for i in range(12)]
        stg = Rot("cstg", [sb("cstg%d" % i, [128, 1280], BF16) for i in range(2)])
        P.dma(identf[:], self.cst[:, 0, :], writes=["identf"])
        P.op("dve", lambda e: e.tensor_copy(out=ident[:], in_=identf[:]), reads=["identf"], writes=["ident"])
        P.op("sp", lambda e: [e.dma_start(out=wcol[:, j, :], in_=self.conv_w[j, :].rearrange("(c p) -> p c", p=128), allow_slow_non_contiguous=True)
                              for j in range(5)], writes=["wcol"], ndma=5)
        P.op("sp", lambda e: [e.dma_start(out=bcol[:], in_=self.conv_b.rearrange("(c p) -> p c", p=128), allow_slow_non_contiguous=True)],
             writes=["bcol"], ndma=1)
        pst = self.psum[:, 0:4, :].bitcast(BF16)
        tcount = 0
        for (b0, nrows, _) in self.blocks:
            c = 0
            while c < nrows:
                w = min(1024, nrows - c)
                c0 = b0 + c
                lh = b0 + (c - 2) % nrows
                rh = b0 + (c + w) % nrows
                for cc in range(12):
                    xk, xa = xin.next()
                    rows = slice(cc * 128, (cc + 1) * 128)
                    P.op("sp", lambda e, xa=xa, rows=rows, c0=c0, w=w, lh=lh, rh=rh: [
                        e.dma_start(out=xa[:, 2:2 + w], in_=self.xbcT[rows, c0:c0 + w]),
                        e.dma_start(out=xa[:, 0:2], in_=self.xbcT[rows, lh:lh + 2]),
                        e.dma_start(out=xa[:, 2 + w:4 + w], in_=self.xbcT[rows, rh:rh + 2])], writes=[xk], ndma=3)
                    ak, aa = acc.next()
                    P.op("act", lambda e, xa=xa, aa=aa, cc=cc, w=w: e.activation(
                        out=aa[:, 0:w], in_=xa[:, 0:w], func=AF.Identity, scale=wcol[:, 0, cc:cc + 1], bias=bcol[:, cc:cc + 1]),
                        reads=[xk, "wcol", "bcol"], writes=[ak])
                    for j in range(1, 5):
                        P.op("dve", lambda e, xa=xa, aa=aa, cc=cc, w=w, j=j: e.scalar_tensor_tensor(
                            out=aa[:, 0:w], in0=xa[:, j:j + w], scalar=wcol[:, j, cc:cc + 1], in1=aa[:, 0:w],
                            op0=ALU.mult, op1=ALU.add), reads=[xk, ak, "wcol"], writes=[ak])
                    P.op("act", lambda e, aa=aa, cc=cc, w=w: e.activation(out=cvo[cc][:, 0:w], in_=aa[:, 0:w], func=AF.Silu),
                         reads=[ak], writes=[("cvo", cc)])
                for cc in (range(8, 12) if c < cfg.ls else ()):
                    P.op("pool", lambda e, cc=cc, c0=c0, w=w: [e.dma_start(out=self.bcT[(cc - 8) * 128:(cc - 7) * 128, c0:c0 + w], in_=cvo[cc][:, 0:w])],
                         reads=[("cvo", cc)], writes=[("bcTd", cc, c0)], ndma=1)
                for s in range(w // 128):
                    pb = (tcount % 2) * 2
                    tcount += 1
                    for cc in range(10):
                        P.op("pe", lambda e, cc=cc, s=s, pb=pb: e.transpose(
                            out=pst[:, pb + cc // 8, (cc % 8) * 128:(cc % 8 + 1) * 128], in_=cvo[cc][:, s * 128:(s + 1) * 128], identity=ident[:]),
                            reads=[("cvo", cc), "ident"], writes=["ps%d" % (pb + cc // 8)])
                    sk, sa = stg.next()
                    P.op("act", lambda e, sa=sa, pb=pb: e.activation(out=sa[:, 0:1024], in_=pst[:, pb, :], func=AF.Copy),
                         reads=["ps%d" % pb], writes=[sk])
                    P.op("dve", lambda e, sa=sa, pb=pb: e.tensor_copy(out=sa[:, 1024:1280], in_=pst[:, pb + 1, 0:256]),
                         reads=["ps%d" % (pb + 1), sk], writes=[sk])
                    P.op("pool", lambda e, sa=sa, r=c0 + s * 128: [e.dma_start(out=self.xsB[r:r + 128, :], in_=sa[:])],
                         reads=[sk], writes=[("xsBd", c0, s)], ndma=1)
                c += w

    def phase_b1(self):
        nc, P, cfg = self.nc, self.P, self.cfg
        sb = self.salloc
        identf = sb("identf", [128, 128], F32)
        ident = sb("ident", [128, 128], BF16)
        nh = sb("nh", [128, 1], F32)
        gqq = sb("gqq", [128, 128], F32)
        gkk = sb("gkk", [128, 128], F32)
        pj = Rot("pj", [sb("pj%d" % i, [128, 1536], F32) for i in range(2)])
        rp = Rot("rp", [sb("rp%d" % i, [128, 128], F32) for i in range(2)])
        sq = sb("sq", [128, 1280], F32)
        xn = sb("xn", [128, 1280], F32)
        t1 = sb("t1", [128, 1280], F32)
        t2 = sb("t2", [128, 1280], F32)
        tq = sb("tq", [128, 128], F32)
        tk = sb("tk", [128, 128], F32)
        sm = Rot("b1sm", [sb("b1sm%d" % i, [128, 64], F32) for i in range(2)])
        yb = Rot("yb", [sb("yb%d" % i, [128, 1280], BF16) for i in range(2)])
        vxt = Rot("vxt", [sb("vxt%d" % i, [128, 640], BF16) for i in range(2)])
        qst = Rot("qst", [sb("qst%d" % i, [128, 8, 512], BF16) for i in range(2)])
        kst = Rot("kst", [sb("kst%d" % i, [128, 2, 512], BF16) for i in range(2)])
        pj4 = Rot("pj4", [sb("pj4_%d" % i, [128, 4, 512], F32) for i in range(2)])
        rp4 = Rot("rp4", [sb("rp4_%d" % i, [128, 4, 128], F32) for i in range(2)])
        sq4 = sb("sq4", [128, 4, 256], F32)
        xn4 = sb("xn4", [128, 4, 256], F32)
        xw4 = sb("xw4", [128, 4, 256], F32)
        t14 = sb("t14", [128, 4, 256], F32)
        t24 = sb("t24", [128, 4, 256], F32)
        tk4 = sb("tk4", [128, 4, 128], F32)
        sm4 = Rot("sm4", [sb("sm4_%d" % i, [128, 64], F32) for i in range(2)])
        yb4 = Rot("yb4", [sb("yb4_%d" % i, [128, 4, 256], BF16) for i in range(2)])
        vx4 = Rot("vx4", [sb("vx4_%d" % i, [128, 4, 640], BF16) for i in range(2)])
        for i in range(2):
            P.op("pool", lambda e, i=i: e.memset(vx4.aps[i][:].rearrange("p s w -> p (s w)"), 1.0), writes=["vx4_%d" % i])
        P.dma(identf[:], self.cst[:, 0, :], writes=["identf"])
        P.op("dve", lambda e: e.tensor_copy(out=ident[:], in_=identf[:]), reads=["identf"], writes=["ident"])
        P.op("dve", lambda e: e.memset(nh[:], -0.5), writes=["nh"])
        P.dma(gqq[:], self.qkg[0:2, :].rearrange("a d -> (a d)").partition_broadcast(128), writes=["gqq"])
        P.dma(gkk[:], self.qkg[2:4, :].rearrange("a d -> (a d)").partition_broadcast(128), writes=["gkk"])
        P.op("dve", lambda e: e.tensor_scalar(out=gqq[:], in0=gqq[:], scalar1=0.125, scalar2=None, op0=ALU.mult), reads=["gqq"], writes=["gqq"])
        for i in range(2):
            P.op("pool", lambda e, i=i: e.memset(vxt.aps[i][:], 1.0), writes=["vxt%d" % i])
        pst = self.psum[:, 0:4, :].bitcast(BF16)
        tcount = 0
        for bi, (b0, nrows, oc0) in enumerate(self.blocks):
            r = 0
            while r < nrows:
                n = min(512, nrows - r)
                own = r < cfg.ls
                if not own:
                    ns = n // 128
                    r0 = b0 + r
                    pk, pa_ = pj4.next()
                    rk, ra_ = rp4.next()
                    pa = pa_[:, 0:ns, :]
                    ra = ra_[:, 0:ns, :]
                    P.dma(pa, self.proj[r0:r0 + n, 1024:1536].rearrange("(s p) f -> p s f", p=128), writes=[pk])
                    P.dma(ra, self.ropet[r0:r0 + n, :].rearrange("(s p) f -> p s f", p=128), writes=[rk])
                    sk, sa = sm4.next()
                    nh4 = ns * 4
                    kv = lambda t: t[:, 0:ns, :]
                    h3 = lambda ap: ap.rearrange("p s (h d) -> p (s h) d", d=64)
                    P.op("act", lambda e, pa=pa, ns=ns: e.activation(out=sq4[:, 0:ns, :], in_=pa[:, :, 0:256], func=AF.Square), reads=[pk], writes=["sq4"])
                    P.op("dve", lambda e, sa=sa, ns=ns, nh4=nh4: e.tensor_reduce(out=sa[:, 0:nh4], in_=sq4[:, 0:ns, :].rearrange("p s (h d) -> p (s h) d", d=64),
                                                                             axis=AX.X, op=ALU.add), reads=["sq4"], writes=[sk])
                    P.op("dve", lambda e, sa=sa, nh4=nh4: e.tensor_scalar(out=sa[:, 16:16 + nh4], in0=sa[:, 0:nh4], scalar1=1.0 / 64, scalar2=EPS,
                                                                      op0=ALU.mult, op1=ALU.add), reads=[sk], writes=[sk])
                    P.op("pool", lambda e, sa=sa, nh4=nh4: e.tensor_tensor(out=sa[:, 32:32 + nh4], in0=sa[:, 16:16 + nh4],
                                                                       in1=nh[:, 0:1].broadcast_to([128, nh4]), op=ALU.pow), reads=[sk, "nh"], writes=[sk])
                    P.op("dve", lambda e, pa=pa, sa=sa, ns=ns, nh4=nh4: e.tensor_tensor(
                        out=xn4[:, 0:ns, :].rearrange("p s (h d) -> p s h d", d=64), in0=pa[:, :, 0:256].rearrange("p s (h d) -> p s h d", d=64),
                        in1=sa[:, 32:32 + nh4].rearrange("p (s h) -> p s h", h=4).unsqueeze(3).broadcast_to([128, ns, 4, 64]), op=ALU.mult),
                        reads=[pk, sk], writes=["xn4"])
                    P.op("pool", lambda e, ra=ra, ns=ns: e.tensor_tensor(out=tk4[:, 0:ns, :], in0=ra, in1=gkk[:].unsqueeze(1).broadcast_to([128, ns, 128]), op=ALU.mult),
                         reads=[rk, "gkk"], writes=["tk4"])
                    for b in range(2):
                        x5 = xn4[:, 0:ns, :].rearrange("p s (h a b i) -> p (s h) a b i", a=2, b=2, i=16)
                        o5 = xw4[:, 0:ns, :].rearrange("p s (h a b i) -> p (s h) a b i", a=2, b=2, i=16)
                        P.op("pool", lambda e, x5=x5, o5=o5, b=b: e.tensor_copy(out=o5[:, :, :, b, :], in_=x5[:, :, :, 1 - b, :]), reads=["xn4"], writes=[("xw4", b)])
                    q4 = lambda t, ns=ns: t[:, 0:ns, :].rearrange("p s (h d) -> p s h d", d=64)
                    P.op("dve", lambda e, ns=ns, q4=q4: e.tensor_tensor(out=q4(t14), in0=q4(xn4), in1=tk4[:, 0:ns, 0:64].unsqueeze(2).broadcast_to([128, ns, 4, 64]),
                                                                    op=ALU.mult), reads=["xn4", "tk4"], writes=["t14"])
                    P.op("dve", lambda e, ns=ns, q4=q4: e.tensor_tensor(out=q4(t24), in0=q4(xw4), in1=tk4[:, 0:ns, 64:128].unsqueeze(2).broadcast_to([128, ns, 4, 64]),
                                                                    op=ALU.mult), reads=[("xw4", 0), ("xw4", 1), "tk4"], writes=["t24"])
                    yk, ya_ = yb4.next()
                    ya = ya_[:, 0:ns, :]
                    P.op("dve", lambda e, ya=ya, ns=ns: e.tensor_tensor(out=ya, in0=t14[:, 0:ns, :], in1=t24[:, 0:ns, :], op=ALU.add), reads=["t14", "t24"], writes=[yk])
                    vk, va_ = vx4.next()
                    va = va_[:, 0:ns, :]
                    for kp in range(2):
                        P.op("act", lambda e, va=va, pa=pa, kp=kp: e.activation(
                            out=va[:, :, kp * 320 + 64:kp * 320 + 320].rearrange("p s (j w) -> p s j w", j=2)[:, :, :, 0:64],
                            in_=pa[:, :, 256 + kp * 128:384 + kp * 128].rearrange("p s (j d) -> p s j d", j=2), func=AF.Copy), reads=[pk], writes=[(vk, kp)])
                    P.op("pool", lambda e, va=va, r0=r0, n=n: [e.dma_start(out=self.vx[r0:r0 + n, :].rearrange("(s p) w -> p s w", p=128), in_=va)],
                         reads=[(vk, 0), (vk, 1)], writes=[("vxd", r0), (vk, 0), (vk, 1)], ndma=1)
                    pb = (tcount % 2) * 2
                    tcount += 1
                    for s_ in range(ns):
                        for p_ in range(2):
                            P.op("pe", lambda e, p_=p_, s_=s_, ya=ya, pb=pb: e.transpose(out=pst[:, pb, p_ * 512 + s_ * 128:p_ * 512 + (s_ + 1) * 128],
                                                                                   in_=ya[:, s_, p_ * 128:(p_ + 1) * 128], identity=ident[:]),
                                 reads=[yk, "ident"], writes=["ps%d" % pb])
                    kk_, ka = kst.next()
                    P.op("dve", lambda e, ka=ka, pb=pb, n=n: e.tensor_copy(out=ka[:, :, 0:n], in_=pst[:, pb, :].rearrange("p (c t) -> p c t", c=2)[:, :, 0:n]),
                         reads=["ps%d" % pb], writes=[(kk_, s_) for s_ in range(4)])
                    P.op("pool", lambda e, ka=ka, rr=r0, n=n: [e.dma_start(out=self.kT.rearrange("(c p) t -> p c t", p=128)[:, :, rr:rr + n], in_=ka[:, :, 0:n])],
                         reads=[(kk_, s_) for s_ in range(4)], writes=[("kTd", r0)] + [(kk_, s_) for s_ in range(4)], ndma=1)
                    r += n
                    continue
                qk_, qa = qst.next()
                kk_, ka = kst.next()
                for s in range(n // 128):
                    r0 = b0 + r + s * 128
                    c_lo = 0 if own else 1024
                    h_lo = 0 if own else 16
                    nh_ = 20 - h_lo
                    pk, pa = pj.next()
                    rk, ra = rp.next()
                    P.dma(pa[:, c_lo:1536], self.proj[r0:r0 + 128, c_lo:1536], writes=[pk])
                    P.dma(ra[:], self.ropet[r0:r0 + 128, :], writes=[rk])
                    sk, sa = sm.next()
                    P.op("act", lambda e, pa=pa, c_lo=c_lo: e.activation(out=sq[:, c_lo:1280], in_=pa[:, c_lo:1280], func=AF.Square),
                         reads=[pk], writes=["sq"])
                    P.op("dve", lambda e, sa=sa, c_lo=c_lo, h_lo=h_lo: e.tensor_reduce(
                        out=sa[:, h_lo:20], in_=sq[:, c_lo:1280].rearrange("p (h d) -> p h d", d=64), axis=AX.X, op=ALU.add),
                        reads=["sq"], writes=[sk])
                    P.op("dve", lambda e, sa=sa, h_lo=h_lo: e.tensor_scalar(out=sa[:, 20 + h_lo:40], in0=sa[:, h_lo:20], scalar1=1.0 / 64, scalar2=EPS,
                                                                       op0=ALU.mult, op1=ALU.add), reads=[sk], writes=[sk])
                    P.op("pool", lambda e, sa=sa, h_lo=h_lo, nh_=nh_: e.tensor_tensor(out=sa[:, 40 + h_lo:60], in0=sa[:, 20 + h_lo:40],
                                                                                in1=nh[:, 0:1].broadcast_to([128, nh_]), op=ALU.pow),
                         reads=[sk, "nh"], writes=[sk])
                    P.op("dve", lambda e, pa=pa, sa=sa, c_lo=c_lo, h_lo=h_lo, nh_=nh_: e.tensor_tensor(
                        out=xn[:, c_lo:1280].rearrange("p (h d) -> p h d", d=64), in0=pa[:, c_lo:1280].rearrange("p (h d) -> p h d", d=64),
                        in1=sa[:, 40 + h_lo:60].unsqueeze(2).broadcast_to([128, nh_, 64]), op=ALU.mult), reads=[pk, sk], writes=["xn"])
                    if own:
                        P.op("pool", lambda e, ra=ra: e.tensor_tensor(out=tq[:], in0=ra[:], in1=gqq[:], op=ALU.mult), reads=[rk, "gqq"], writes=["tq"])
                    P.op("pool", lambda e, ra=ra: e.tensor_tensor(out=tk[:], in0=ra[:], in1=gkk[:], op=ALU.mult), reads=[rk, "gkk"], writes=["tk"])
                    groups = ([(0, 16, tq, "tq")] if own else []) + [(16, 4, tk, "tk")]
                    for (h0, hn, tt, tkey) in groups:
                        xv = xn[:, h0 * 64:(h0 + hn) * 64]
                        P.op("dve", lambda e, xv=xv, tt=tt, h0=h0, hn=hn: e.tensor_tensor(
                            out=t1[:, h0 * 64:(h0 + hn) * 64].rearrange("p (h d) -> p h d", d=64), in0=xv.rearrange("p (h d) -> p h d", d=64),
                            in1=tt[:, 0:64].unsqueeze(1).broadcast_to([128, hn, 64]), op=ALU.mult), reads=["xn", tkey], writes=[("t1", h0)])
                        for b in range(2):
                            x5 = xv.rearrange("p (h a b i) -> p h a b i", a=2, b=2, i=16)
                            o5 = t2[:, h0 * 64:(h0 + hn) * 64].rearrange("p (h a b i) -> p h a b i", a=2, b=2, i=16)
                            s4 = tt[:, 64:128].rearrange("p (a b i) -> p a b i", a=2, b=2, i=16)
                            P.op("pool", lambda e, x5=x5, o5=o5, s4=s4, b=b, hn=hn: e.tensor_tensor(
                                out=o5[:, :, :, b, :], in0=x5[:, :, :, 1 - b, :],
                                in1=s4[:, :, b, :].unsqueeze(1).broadcast_to([128, hn, 2, 16]), op=ALU.mult),
                                reads=["xn", tkey], writes=[("t2", h0, b)])
                    yk, ya = yb.next()
                    P.op("dve", lambda e, ya=ya, c_lo=c_lo: e.tensor_tensor(out=ya[:, c_lo:1280], in0=t1[:, c_lo:1280], in1=t2[:, c_lo:1280], op=ALU.add),
                         reads=[("t1", 0), ("t1", 16), ("t2", 0, 0), ("t2", 0, 1), ("t2", 16, 0), ("t2", 16, 1)], writes=[yk])
                    vk, va = vxt.next()
                    P.op("act", lambda e, va=va, pa=pa: e.activation(
                        out=va[:].rearrange("p (kp w) -> p kp w", kp=2)[:, :, 64:320].rearrange("p kp (j w) -> p kp j w", j=2)[:, :, :, 0:64],
                        in_=pa[:, 1280:1536].rearrange("p (kp j d) -> p kp j d", kp=2, j=2), func=AF.Copy), reads=[pk], writes=[vk])
                    P.op("pool", lambda e, va=va, r0=r0: [e.dma_start(out=self.vx[r0:r0 + 128, :], in_=va[:])], reads=[vk], writes=[("vxd", r0)], ndma=1)
                    pb = (tcount % 2) * 2
                    tcount += 1
                    if own:
                        for p_ in range(8):
                            P.op("pe", lambda e, p_=p_, ya=ya, pb=pb: e.transpose(out=pst[:, pb, p_ * 128:(p_ + 1) * 128], in_=ya[:, p_ * 128:(p_ + 1) * 128],
                                                                              identity=ident[:]), reads=[yk, "ident"], writes=["ps%d" % pb])
                        P.op("act", lambda e, qa=qa, pb=pb, s=s: e.activation(out=qa[:, :, s * 128:(s + 1) * 128],
                                                                          in_=pst[:, pb, :].rearrange("p (c t) -> p c t", t=128), func=AF.Copy),
                             reads=["ps%d" % pb], writes=[(qk_, s)])
                    for p_ in range(2):
                        P.op("pe", lambda e, p_=p_, ya=ya, pb=pb: e.transpose(out=pst[:, pb + 1, p_ * 128:(p_ + 1) * 128],
                                                                          in_=ya[:, 1024 + p_ * 128:1024 + (p_ + 1) * 128], identity=ident[:]),
                             reads=[yk, "ident"], writes=["ps%d" % (pb + 1)])
                    P.op("dve", lambda e, ka=ka, pb=pb, s=s: e.tensor_copy(out=ka[:, :, s * 128:(s + 1) * 128],
                                                                       in_=pst[:, pb + 1, 0:256].rearrange("p (c t) -> p c t", t=128)),
                         reads=["ps%d" % (pb + 1)], writes=[(kk_, s)])
                ns = n // 128
                if own:
                    P.op("pool", lambda e, qa=qa, oc=oc0 + r, n=n: [e.dma_start(out=self.qT.rearrange("(c p) t -> p c t", p=128)[:, :, oc:oc + n], in_=qa[:, :, 0:n])],
                         reads=[(qk_, s) for s in range(ns)], writes=[("qTd", oc0 + r)] + [(qk_, s) for s in range(ns)], ndma=1)
                P.op("pool", lambda e, ka=ka, rr=b0 + r, n=n: [e.dma_start(out=self.kT.rearrange("(c p) t -> p c t", p=128)[:, :, rr:rr + n], in_=ka[:, :, 0:n])],
                     reads=[(kk_, s) for s in range(ns)], writes=[("kTd", b0 + r)] + [(kk_, s) for s in range(ns)], ndma=1)
                r += n

    def phase_b2(self):
        nc, P, cfg = self.nc, self.P, self.cfg
        sb = self.salloc
        nchmax = cfg.rp // 128
        nch_tot = cfg.rtot // 128
        KT = sb("KT", [128, cfg.rp], BF16)
        VX = sb("VX", [128, nchmax, 320], BF16)
        QT = Rot("QT", [sb("QT%d" % i, [128, cfg.ls], BF16) for i in range(4)])
        PT = Rot("PT", [sb("PT%d" % i, [128, 1024], BF16) for i in range(3)])
        OT = Rot("OT", [sb("OT%d" % i, [128, 512], BF16) for i in range(2)])
        BC = Rot("BC", [sb("BC%d" % i, [128, 512], F32) for i in range(2)])
        RL = Rot("RL", [sb("RL%d" % i, [128, 512], F32) for i in range(2)])
        kb = sb("kb", [128, nch_tot], F32)
        sel = sb("sel", [128, 128], F32)
        gq = sb("gq", [128, 256], F32)
        gsm = sb("gsm", [128, 8], F32)
        P.dma(kb[:], self.kbias, writes=["kb"])
        P.dma(gq[:], self.qkg.rearrange("a d -> (a d)").partition_broadcast(128), writes=["gq"])
        P.op("dve", lambda e: e.tensor_reduce(out=gsm[:, 0:4], in_=gq[:].rearrange("p (a d) -> p a d", a=4), axis=AX.X, op=ALU.max,
                                              apply_absolute_value=True), reads=["gq"], writes=["gsm"])
        P.op("dve", lambda e: e.tensor_tensor(out=gsm[:, 4:5], in0=gsm[:, 0:1], in1=gsm[:, 2:3], op=ALU.mult), reads=["gsm"], writes=["gsm"])
        P.op("dve", lambda e: e.tensor_scalar(out=gsm[:, 5:6], in0=gsm[:, 4:5], scalar1=-8.0, scalar2=None, op0=ALU.mult), reads=["gsm"], writes=["gsm"])
        P.op("dve", lambda e: e.tensor_scalar(out=kb[:], in0=kb[:], scalar1=gsm[:, 5:6], scalar2=None, op0=ALU.add), reads=["gsm", "kb"], writes=["kb"])
        P.op("pool", lambda e: e.memset(sel[:], 1.0), writes=["sel"])
        for bi, (b0, nrows, oc0) in enumerate(self.blocks):
            nch = nrows // 128
            for kp in range(2):
                P.dma(KT[:, 0:nrows], self.kT[kp * 128:(kp + 1) * 128, b0:b0 + nrows], writes=["KT"])
                P.dma(VX[:, 0:nch, :], self.vx[b0:b0 + nrows, kp * 320:(kp + 1) * 320].rearrange("(c p) w -> p c w", p=128), writes=["VX"])
                steps = []
                for i in range(4):
                    p_ = kp * 4 + i
                    qk_, qa = QT.next()
                    P.dma(qa[:], self.qT[p_ * 128:(p_ + 1) * 128, oc0:oc0 + cfg.ls], writes=[qk_])
                    for qb in range(cfg.ls // 512):
                        for ch in range(nch):
                            steps.append((p_, qk_, qa, qb, ch))

                def emit_s(idx, st):
                    p_, qk_, qa, qb, ch = st
                    sbk = (idx % 2) * 2
                    for hf in range(2):
                        lo = hf * 64
                        P.op("pe", lambda e, lo=lo, ch=ch, qa=qa, qb=qb, b=sbk + hf: e.matmul(
                            self.psum[:, b, :], lhsT=KT[lo:lo + 64, ch * 128:(ch + 1) * 128], rhs=qa[lo:lo + 64, qb * 512:(qb + 1) * 512],
                            start=True, stop=True), reads=["KT", qk_], writes=["ps%d" % (sbk + hf)])

                emit_s(0, steps[0])
                for idx, st in enumerate(steps):
                    p_, qk_, qa, qb, ch = st
                    if idx + 1 < len(steps):
                        emit_s(idx + 1, steps[idx + 1])
                    sbk = (idx % 2) * 2
                    pk_, pa = PT.next()
                    gch = b0 // 128 + ch
                    P.op("act", lambda e, pa=pa, sbk=sbk, gch=gch: e.activation(
                        out=pa[:], in_=self.psum[:, sbk:sbk + 2, :].rearrange("p b f -> p (b f)"), func=AF.Exp, bias=kb[:, gch:gch + 1], scale=1.0),
                        reads=["ps%d" % sbk, "ps%d" % (sbk + 1), "kb"], writes=[pk_])
                    for hf in range(2):
                        w0 = 64 if hf == 0 else 128
                        P.op("pe", lambda e, pa=pa, ch=ch, hf=hf, w0=w0, nch=nch: e.matmul(
                            self.psum[:, 4 + hf, :], lhsT=VX[:, ch, w0:w0 + 128], rhs=pa[:, hf * 512:(hf + 1) * 512], start=(ch == 0), stop=(ch == nch - 1)),
                            reads=["VX", pk_], writes=["ps%d" % (4 + hf)])
                    if ch == nch - 1:
                        rk_, rl = RL.next()
                        P.op("dve", lambda e, rl=rl: e.reciprocal(out=rl[64:65, :], in_=self.psum[64:65, 4, :]), reads=["ps4"], writes=[(rk_, 0)])
                        P.op("dve", lambda e, rl=rl: e.reciprocal(out=rl[0:1, :], in_=self.psum[0:1, 5, :]), reads=["ps5"], writes=[(rk_, 1)])
                        P.op("pe", lambda e, rl=rl: e.matmul(self.psum[:, 6, :], lhsT=sel[64:65, :], rhs=rl[64:65, :], start=True, stop=True),
                             reads=[(rk_, 0), "sel"], writes=["ps6"])
                        P.op("pe", lambda e, rl=rl: e.matmul(self.psum[:, 7, :], lhsT=sel[0:1, :], rhs=rl[0:1, :], start=True, stop=True),
                             reads=[(rk_, 1), "sel"], writes=["ps7"])
                        bk_, bc = BC.next()
                        P.op("act", lambda e, bc=bc: e.activation(out=bc[0:64, :], in_=self.psum[0:64, 6, :], func=AF.Copy), reads=["ps6"], writes=[(bk_, 0)])
                        P.op("act", lambda e, bc=bc: e.activation(out=bc[64:128, :], in_=self.psum[64:128, 7, :], func=AF.Copy), reads=["ps7"], writes=[(bk_, 1)])
                        ok_, ot = OT.next()
                        P.op("dve", lambda e, ot=ot, bc=bc: e.tensor_tensor(out=ot[0:64, :], in0=self.psum[0:64, 4, :], in1=bc[0:64, :], op=ALU.mult),
                             reads=["ps4", (bk_, 0)], writes=[(ok_, 0)])
                        P.op("dve", lambda e, ot=ot, bc=bc: e.tensor_tensor(out=ot[64:128, :], in0=self.psum[64:128, 5, :], in1=bc[64:128, :], op=ALU.mult),
                             reads=["ps5", (bk_, 1)], writes=[(ok_, 1)])
                        P.op("pool", lambda e, ot=ot, p_=p_, cc=oc0 + qb * 512: [e.dma_start(out=self.oT[p_ * 128:(p_ + 1) * 128, cc:cc + 512], in_=ot[:])],
                             reads=[(ok_, 0), (ok_, 1)], writes=[("oTd", p_, oc0 + qb * 512), (ok_, 0), (ok_, 1)], ndma=1)

    def phase_b3(self):
        nc, P, cfg = self.nc, self.P, self.cfg
        sb = self.salloc
        nch_tot = cfg.rtot // 128
        nown = cfg.ls // 128
        cst = sb("cstt", [128, 5, 128], F32)
        ones = sb("ones", [128, 128], F32)
        ident = sb("ident", [128, 128], BF16)
        Aexp = sb("Aexp", [128, 32], F32)
        dtb = sb("dtb", [128, 32], F32)
        dsk = sb("dsk", [128, 1024], F32)
        gss = sb("gss", [128, 1024], F32)
        keep = sb("keep", [128, 2, nch_tot], F32)
        dtm = sb("dtm", [128, nch_tot], F32)
        nh = sb("nh", [128, 1], F32)
        S = [sb("S%d" % d, [128, 1024], F32) for d in range(2)]
        Sb = Rot("Sb", [sb("Sb%d" % i, [128, 1024], BF16) for i in range(2)])
        xsb = Rot("xsb", [sb("xsb%d" % i, [128, 1280], BF16) for i in range(3)])
        bct = Rot("bct", [sb("bct%d" % i, [128, 4, 128], BF16) for i in range(2)])
        zt = Rot("zt", [sb("zt%d" % i, [128, 1024], F32) for i in range(1)])
        xw = Rot("xw", [sb("xw%d" % i, [128, 1024], BF16) for i in range(2)])
        xdt = Rot("xdt", [sb("xdt%d" % i, [128, 1024], BF16) for i in range(2)])
        AU = Rot("AU", [sb("AU%d" % i, [128, 16, 128], F32) for i in range(1)])
        Mexp = sb("Mexp", [128, 16, 128], F32)
        MT = Rot("MT", [sb("MT%d" % i, [128, 16, 128], BF16) for i in range(2)])
        CBm = sb("CBm", [128, 2, 128], F32)
        tmpy = Rot("tmpy", [sb("tmpy%d" % i, [128, 1024], F32) for i in range(2)])
        yacc = sb("yacc", [128, nown, 1024], F32)
        f1 = sb("f1", [128, 1024], F32)
        f2 = sb("f2", [128, 1024], F32)
        fo = Rot("fo", [sb("fo%d" % i, [128, 1024], BF16) for i in range(2)])
        ost = Rot("ost", [sb("ost%d" % i, [128, 8, 128], BF16) for i in range(2)])
        fsm = Rot("fsm", [sb("fsm%d" % i, [128, 8], F32) for i in range(2)])
        TRI = [cst[:, 1, :], cst[:, 2, :]]
        UU = [cst[:, 3, :], cst[:, 4, :]]
        ps = self.psum
        P.dma(cst[:], self.cst, writes=["cst"])
        P.op("dve", lambda e: e.tensor_copy(out=ident[:], in_=cst[:, 0, :]), reads=["cst"], writes=["ident"])
        P.op("pool", lambda e: e.memset(ones[:], 1.0), writes=["ones"])
        P.op("pool", lambda e: e.memset(nh[:], -0.5), writes=["nh"])
        P.dma(Aexp[:], self.a_log.partition_broadcast(128), writes=["Aexp"])
        P.op("act", lambda e: e.activation(out=Aexp[:], in_=Aexp[:], func=AF.Exp), reads=["Aexp"], writes=["Aexp"])
        P.op("dve", lambda e: e.tensor_scalar(out=Aexp[:], in0=Aexp[:], scalar1=-1.0, scalar2=None, op0=ALU.mult), reads=["Aexp"], writes=["Aexp"])
        P.dma(dtb[:], self.dt_bias.partition_broadcast(128), writes=["dtb"])
        P.dma(dsk[:], self.dskip.partition_broadcast(128), writes=["dsk"])
        P.dma(gss[:], self.ssm_norm.partition_broadcast(128), writes=["gss"])
        P.dma(keep[:], self.keep, writes=["keep"])
        P.dma(dtm[:], self.dtmask, writes=["dtm"])

        nbmax = cfg.rp // 128
        BT = {k: sb("BT_" + k, [128, nbmax, 16], F32) for k in ("dt", "aa", "ecum", "etot", "ww")}

        def precompute(b0, nchb, d):
            g0 = b0 // 128
            DT, AA, EC, ET, WW = (BT[k][:, 0:nchb, :] for k in ("dt", "aa", "ecum", "etot", "ww"))
            bc_h = lambda ap: ap.unsqueeze(1).broadcast_to([128, nchb, 16])
            bc_c = lambda ap: ap.unsqueeze(2).broadcast_to([128, nchb, 16])
            P.dma(DT, self.proj[b0:b0 + nchb * 128, 2560 + d * 16:2576 + d * 16].rearrange("(c p) f -> p c f", p=128), writes=["BTdt"])
            P.op("dve", lambda e: e.tensor_tensor(out=AA, in0=DT, in1=bc_h(dtb[:, d * 16:(d + 1) * 16]), op=ALU.add), reads=["BTdt", "dtb"], writes=["BTaa"])
            P.op("dve", lambda e: e.scalar_tensor_tensor(out=EC, in0=AA, scalar=-1.0, in1=AA, op0=ALU.mult, op1=ALU.min), reads=["BTaa"], writes=["BTecum"])
            P.op("act", lambda e: e.activation(out=EC, in_=EC, func=AF.Exp), reads=["BTecum"], writes=["BTecum"])
            P.op("act", lambda e: e.activation(out=EC, in_=EC, func=AF.Ln, bias=1.0), reads=["BTecum"], writes=["BTecum"])
            P.op("dve", lambda e: e.scalar_tensor_tensor(out=DT, in0=AA, scalar=0.0, in1=EC, op0=ALU.max, op1=ALU.add), reads=["BTaa", "BTecum"], writes=["BTdt"])
            P.op("dve", lambda e: e.tensor_tensor(out=DT, in0=DT, in1=bc_c(dtm[:, g0:g0 + nchb]), op=ALU.mult), reads=["BTdt", "dtm"], writes=["BTdt"])
            P.op("dve", lambda e: e.tensor_tensor(out=AA, in0=DT, in1=bc_h(Aexp[:, d * 16:(d + 1) * 16]), op=ALU.mult), reads=["BTdt", "Aexp"], writes=["BTaa"])
            for c0 in range(0, nchb, 32):
                n = min(32, nchb - c0)
                rhs = BT["aa"][:, c0:c0 + n, :].rearrange("p c h -> p (c h)")
                P.op("pe", lambda e, rhs=rhs, n=n: e.matmul(ps[:, 0, 0:n * 16], lhsT=TRI[d], rhs=rhs, start=True, stop=True), reads=["BTaa", "cst"], writes=["ps0"])
                P.op("pe", lambda e, rhs=rhs, n=n: e.matmul(ps[:, 1, 0:n * 16], lhsT=ones[:], rhs=rhs, start=True, stop=True), reads=["BTaa", "ones"], writes=["ps1"])
                P.op("act", lambda e, c0=c0, n=n: e.activation(out=BT["ecum"][:, c0:c0 + n, :].rearrange("p c h -> p (c h)"), in_=ps[:, 0, 0:n * 16], func=AF.Copy),
                     reads=["ps0"], writes=["BTecum"])
                P.op("dve", lambda e, c0=c0, n=n: e.tensor_copy(out=BT["etot"][:, c0:c0 + n, :].rearrange("p c h -> p (c h)"), in_=ps[:, 1, 0:n * 16]),
                     reads=["ps1"], writes=["BTetot"])
            P.op("dve", lambda e: e.tensor_tensor(out=WW, in0=ET, in1=EC, op=ALU.subtract), reads=["BTetot", "BTecum"], writes=["BTww"])
            P.op("act", lambda e: e.activation(out=WW, in_=WW, func=AF.Exp), reads=["BTww"], writes=["BTww"])
            P.op("dve", lambda e: e.tensor_tensor(out=WW, in0=WW, in1=DT, op=ALU.mult), reads=["BTww", "BTdt"], writes=["BTww"])
            P.op("act", lambda e: e.activation(out=EC, in_=EC, func=AF.Exp), reads=["BTecum"], writes=["BTecum"])
            P.op("act", lambda e: e.activation(out=ET, in_=ET, func=AF.Exp), reads=["BTetot"], writes=["BTetot"])
            P.op("dve", lambda e: e.tensor_tensor(out=ET, in0=ET, in1=bc_c(keep[:, d, g0:g0 + nchb]), op=ALU.mult), reads=["BTetot", "keep"], writes=["BTetot"])

        def step(r0, gch, lch, d, with_y, slot, first_pass):
            xk, xa = xsb.next()
            P.dma(xa[:], self.xsB[r0:r0 + 128, :], writes=[xk])
            dt = BT["dt"][:, lch, :]
            aa = BT["aa"][:, lch, :]
            ww = BT["ww"][:, lch, :]
            ecum = BT["ecum"][:, lch, :]
            etot = BT["etot"][:, lch, :]
            wkk = "BTww"
            xwk, xwa = xw.next()
            xs3 = xa[:, 0:1024].rearrange("p (h d) -> p h d", d=64)
            P.op("dve", lambda e: e.tensor_tensor(out=xwa[:].rearrange("p (h d) -> p h d", d=64), in0=xs3,
                                                  in1=ww.unsqueeze(2).broadcast_to([128, 16, 64]), op=ALU.mult), reads=[xk, wkk], writes=[xwk])
            Sk = "S%d" % d
            if with_y:
                sbk, sba = Sb.next()
                P.op("act", lambda e: e.activation(out=sba[:], in_=S[d][:], func=AF.Identity, scale=keep[:, d, gch:gch + 1]), reads=[Sk, "keep"], writes=[sbk])
                xdk, xda = xdt.next()
                P.op("pool", lambda e: e.tensor_tensor(out=xda[:].rearrange("p (h d) -> p h d", d=64), in0=xs3,
                                                       in1=dt.unsqueeze(2).broadcast_to([128, 16, 64]), op=ALU.mult), reads=[xk, "BTdt"], writes=[xdk])
                bk, ba = bct.next()
                P.dma(ba[:], self.bcT.rearrange("(c p) t -> p c t", p=128)[:, :, r0:r0 + 128], writes=[bk])
                for g in range(2):
                    P.op("pe", lambda e, g=g: e.matmul(ps[:, 1, g * 128:(g + 1) * 128], lhsT=ba[:, g, :], rhs=ba[:, 2 + g, :], start=True, stop=True),
                         reads=[bk], writes=["ps1"])
                P.op("dve", lambda e: e.tensor_tensor(out=CBm[:], in0=ps[:, 1, 0:256].rearrange("p (g l) -> p g l", g=2),
                                                      in1=TRI[d].unsqueeze(1).broadcast_to([128, 2, 128]), op=ALU.mult), reads=["ps1", "cst"], writes=["CBm"])
                auk, aua = AU.next()
                P.op("pool", lambda e: e.tensor_tensor(out=aua[:], in0=UU[d].unsqueeze(1).broadcast_to([128, 16, 128]),
                                                       in1=aa.unsqueeze(2).broadcast_to([128, 16, 128]), op=ALU.mult), reads=["BTaa", "cst"], writes=[auk])
                for h in range(16):
                    P.op("pe", lambda e, h=h: e.matmul(ps[:, 4 + h // 4, (h % 4) * 128:(h % 4 + 1) * 128], lhsT=aua[:, h, :], rhs=TRI[d], start=True, stop=True),
                         reads=[auk, "cst"], writes=["ps%d" % (4 + h // 4)])
                P.op("act", lambda e: e.activation(out=Mexp[:].rearrange("p h l -> p (h l)"), in_=ps[:, 4:8, :].rearrange("p b f -> p (b f)"), func=AF.Exp),
                     reads=["ps4", "ps5", "ps6", "ps7"], writes=["Mexp"])
                mk, ma = MT.next()
                for g in range(2):
                    P.op("dve", lambda e, g=g: e.tensor_tensor(out=ma[:, g * 8:(g + 1) * 8, :], in0=Mexp[:, g * 8:(g + 1) * 8, :],
                                                           in1=CBm[:, g, :].unsqueeze(1).broadcast_to([128, 8, 128]), op=ALU.mult),
                         reads=["Mexp", "CBm"], writes=[(mk, g)])
                for g in range(2):
                    P.op("pe", lambda e, g=g: e.matmul(ps[:, 4 + g, :], lhsT=ba[:, 2 + g, :], rhs=sba[:, g * 512:(g + 1) * 512], start=True, stop=True),
                         reads=[bk, sbk], writes=["ps%d" % (4 + g)])
                for h in range(16):
                    P.op("pe", lambda e, h=h: e.matmul(ps[:, 6 + h // 8, (h % 8) * 64:(h % 8 + 1) * 64], lhsT=ma[:, h, :], rhs=xda[:, h * 64:(h + 1) * 64],
                                                     start=True, stop=True), reads=[(mk, h // 8), xdk], writes=["ps%d" % (6 + h // 8)])
                tk_, ta = tmpy.next()
                P.op("dve", lambda e: e.tensor_tensor(out=ta[:].rearrange("p (h d) -> p h d", d=64),
                                                      in0=ps[:, 4:6, :].rearrange("p b (h d) -> p (b h) d", d=64),
                                                      in1=ecum.unsqueeze(2).broadcast_to([128, 16, 64]), op=ALU.mult),
                     reads=["ps4", "ps5", "BTecum"], writes=[tk_])
                ydst = yacc[:, slot, :] if first_pass else ta[:]
                P.op("dve", lambda e: e.tensor_tensor(out=ydst, in0=ta[:], in1=ps[:, 6:8, :].rearrange("p b f -> p (b f)"), op=ALU.add),
                     reads=[tk_, "ps6", "ps7"], writes=[("yacc", slot)] if first_pass else [tk_])
                if not first_pass:
                    P.op("pool", lambda e: e.tensor_tensor(out=yacc[:, slot, :], in0=yacc[:, slot, :], in1=ta[:], op=ALU.add),
                         reads=[tk_, ("yacc", slot)], writes=[("yacc", slot)])
            for g in range(2):
                P.op("pe", lambda e, g=g: e.matmul(ps[:, 2 + g, :], lhsT=xa[:, 1024 + g * 128:1152 + g * 128], rhs=xwa[:, g * 512:(g + 1) * 512],
                                                 start=True, stop=True), reads=[xk, xwk], writes=["ps%d" % (2 + g)])
            P.op("dve", lambda e: e.tensor_tensor(out=S[d][:].rearrange("p (h d) -> p h d", d=64), in0=S[d][:].rearrange("p (h d) -> p h d", d=64),
                                                   in1=etot.unsqueeze(2).broadcast_to([128, 16, 64]), op=ALU.mult), reads=[Sk, "BTetot"], writes=[Sk])
            P.op("dve", lambda e: e.tensor_tensor(out=S[d][:], in0=S[d][:], in1=ps[:, 2:4, :].rearrange("p b f -> p (b f)"), op=ALU.add),
                 reads=[Sk, "ps2", "ps3"], writes=[Sk])
            return xk, xa

        def finalize(r0, slot, oc, xk, xa):
            zk, za = zt.next()
            P.dma(za[:], self.proj[r0:r0 + 128, 1536:2560], writes=[zk])
            P.op("dve", lambda e: e.tensor_tensor(out=f1[:], in0=xa[:, 0:1024], in1=dsk[:], op=ALU.mult), reads=[xk, "dsk"], writes=["f1"])
            P.op("dve", lambda e: e.tensor_tensor(out=f1[:], in0=f1[:], in1=yacc[:, slot, :], op=ALU.add), reads=["f1", ("yacc", slot)], writes=["f1"])
            P.op("act", lambda e: e.activation(out=f2[:], in_=za[:], func=AF.Exp, scale=-1.0), reads=[zk], writes=["f2"])
            P.op("act", lambda e: e.activation(out=f2[:], in_=f2[:], func=AF.Ln, bias=1.0), reads=["f2"], writes=["f2"])
            P.op("act", lambda e: e.activation(out=f2[:], in_=f2[:], func=AF.Exp, scale=-1.0), reads=["f2"], writes=["f2"])
            P.op("pool", lambda e: e.tensor_tensor(out=f2[:], in0=f2[:], in1=za[:], op=ALU.mult), reads=["f2", zk], writes=["f2"])
            P.op("dve", lambda e: e.tensor_tensor(out=f1[:], in0=f1[:], in1=f2[:], op=ALU.mult), reads=["f1", "f2"], writes=["f1"])
            fk, fs = fsm.next()
            for g in range(2):
                P.op("act", lambda e, g=g: e.activation(out=f2[:, g * 512:(g + 1) * 512], in_=f1[:, g * 512:(g + 1) * 512], func=AF.Square,
                                                      accum_out=fs[:, g:g + 1]), reads=["f1"], writes=["f2", (fk, g)])
            P.op("dve", lambda e: e.tensor_scalar(out=fs[:, 2:4], in0=fs[:, 0:2], scalar1=1.0 / 512, scalar2=EPS, op0=ALU.mult, op1=ALU.add),
                 reads=[(fk, 0), (fk, 1)], writes=[(fk, 2)])
            P.op("pool", lambda e: e.tensor_tensor(out=fs[:, 4:6], in0=fs[:, 2:4], in1=nh[:, 0:1].broadcast_to([128, 2]), op=ALU.pow),
                 reads=[(fk, 2), "nh"], writes=[(fk, 3)])
            ok_, oa = fo.next()
            for g in range(2):
                P.op("dve", lambda e, g=g: e.scalar_tensor_tensor(out=oa[:, g * 512:(g + 1) * 512], in0=f1[:, g * 512:(g + 1) * 512], scalar=fs[:, 4 + g:5 + g],
                                                                in1=gss[:, g * 512:(g + 1) * 512], op0=ALU.mult, op1=ALU.mult),
                     reads=["f1", (fk, 3), "gss"], writes=[(ok_, g)])
            pst = ps[:, 1, :].bitcast(BF16)
            for c in range(8):
                P.op("pe", lambda e, c=c: e.transpose(out=pst[:, c * 128:(c + 1) * 128], in_=oa[:, c * 128:(c + 1) * 128], identity=ident[:]),
                     reads=[(ok_, c // 4), "ident"], writes=["ps1"])
            sk_, sa = ost.next()
            P.op("act", lambda e: e.activation(out=sa[:].rearrange("p c t -> p (c t)"), in_=pst, func=AF.Copy), reads=["ps1"], writes=[sk_])
            P.op("pool", lambda e: [e.dma_start(out=self.oT[1024:2048, :].rearrange("(c p) t -> p c t", p=128)[:, :, oc:oc + 128], in_=sa[:])],
                 reads=[sk_], writes=[("oTs", oc), sk_], ndma=1)

        for (b0, nrows, oc0) in self.blocks:
            nchb = nrows // 128
            for d in range(2):
                P.op("pool", lambda e, d=d: e.memset(S[d][:], 0.0), reads=["S%d" % d], writes=["S%d" % d])
            others = list(range(nown, nchb))
            precompute(b0, nchb, 0)
            for ch in others:
                step(b0 + ch * 128, b0 // 128 + ch, ch, 0, False, None, True)
            for ch in range(nown):
                step(b0 + ch * 128, b0 // 128 + ch, ch, 0, True, ch, True)
            precompute(b0, nchb, 1)
            for ch in reversed(others):
                step(b0 + ch * 128, b0 // 128 + ch, ch, 1, False, None, False)
            for ch in reversed(range(nown)):
                xk, xa = step(b0 + ch * 128, b0 // 128 + ch, ch, 1, True, ch, False)
                finalize(b0 + ch * 128, ch, oc0 + ch * 128, xk, xa)

    def build(self):
        nc, P, cfg = self.nc, self.P, self.cfg
        self.declare()
        with contextlib.ExitStack() as top:
            self.psum = top.enter_context(nc.psum_tensor("psum", [128, 8, 512], F32))
            self.sb_words = 52992
            self.SB = top.enter_context(nc.sbuf_tensor("SB", [128, self.sb_words], F32))
            if cfg.do_w:
                with contextlib.ExitStack() as st:
                    self.sb_off = 0
                    self.phase_w(st)
                    P.barrier()
            if cfg.do_a:
                with contextlib.ExitStack() as st:
                    self.sb_off = 0
                    self.phase_a(st)
                    P.barrier()
            if cfg.do_b:
                for ph in cfg.b_phases:
                    self.sb_off = 0
                    getattr(self, "phase_" + ph)()
                    P.barrier()
            if cfg.do_c:
                with contextlib.ExitStack() as st:
                    self.sb_off = 0
                    self.phase_c(st)
                    P.barrier()
            P.emit()
        return nc


def _slot_heads():
    order = []
    for p in range(8):
        if p < 4:
            order += [p, 4 + p]
        else:
            order += [8 + (p - 4), 12 + (p - 4)]
    return order


def _partner():
    d = np.arange(64)
    a, b, i = d // 32, (d // 16) % 2, d % 16
    return a * 32 + (1 - b) * 16 + i


def _rope_rows(tok):
    f32 = np.float32
    inv = (f32(10000.0) ** (-(np.arange(16, dtype=f32) / f32(16)))).astype(f32)
    t = np.maximum(tok, 0)
    row = (t // 64).astype(f32)
    col = (t % 64).astype(f32)
    ang = np.stack([row[:, None] * inv, col[:, None] * inv], axis=1).astype(f32)
    ang = np.where((tok >= 0)[:, None, None], ang, f32(0))
    c = np.cos(ang).astype(f32)
    s = np.sin(ang).astype(f32)
    cos = np.broadcast_to(c[:, :, None, :], (len(tok), 2, 2, 16)).reshape(len(tok), 64)
    sgn = np.array([-1.0, 1.0], dtype=f32)[None, None, :, None]
    sin = (np.broadcast_to(s[:, :, None, :], (len(tok), 2, 2, 16)) * sgn).reshape(len(tok), 64)
    return np.concatenate([cos, sin], axis=1).astype(f32)


def prep_inputs(cfg, inp):
    f32 = np.float32
    NS, LS = cfg.ncores, cfg.ls
    xp = np.asarray(inp["x_prompt"], f32)[0]
    xs = np.asarray(inp["x_sample"], f32)
    meta = np.asarray(inp["meta_tokens"], f32)
    sq = lambda k: np.ascontiguousarray(np.asarray(inp[k], f32)[0])
    w_in = sq("w_in")
    heads = _slot_heads()
    qcols = np.concatenate([np.arange(h * 64, (h + 1) * 64) for h in heads])
    perm = np.concatenate([qcols, np.arange(1024, 1536), np.arange(1536, 2560), np.arange(4096, 4128), np.arange(2560, 4096)])
    w_out = sq("w_out")
    shared = {
        "ff1_wg": sq("ff1_w_gate"), "ff1_wu": sq("ff1_w_up"), "ff1_wd": sq("ff1_w_down"),
        "ff2_wg": sq("ff2_w_gate"), "ff2_wu": sq("ff2_w_up"), "ff2_wd": sq("ff2_w_down"),
        "w_in_p": np.ascontiguousarray(w_in[:, perm]),
        "w_out_p": np.ascontiguousarray(np.concatenate([w_out[qcols], w_out[1024:]], axis=0)),
        "ff1_pre": sq("ff1_norm_pre"), "ff1_post": sq("ff1_norm_post"), "mix_pre": sq("mix_norm_pre"),
        "mix_post": sq("mix_norm_post"), "ff2_pre": sq("ff2_norm_pre"), "ff2_post": sq("ff2_norm_post"),
        "ident": np.eye(128, dtype=f32),
        "conv_w": sq("conv_w"), "conv_b": sq("conv_b"),
        "a_log": sq("a_log").reshape(32), "dt_bias": sq("dt_bias").reshape(32),
        "dskip": np.repeat(sq("d_skip"), 64), "ssm_norm": sq("ssm_norm"),
    }
    pt = _partner()
    qn, kn = sq("q_norm"), sq("k_norm")
    shared["qkg"] = np.stack([qn, qn[pt], kn, kn[pt]]).astype(f32)
    i = np.arange(128)
    tri_f = (i[:, None] <= i[None, :]).astype(f32)
    tri_b = (i[:, None] >= i[None, :]).astype(f32)
    u_f = (i[:, None] > i[None, :]).astype(f32)
    u_b = (i[:, None] < i[None, :]).astype(f32)
    shared["cst"] = np.ascontiguousarray(np.stack([np.eye(128, dtype=f32), tri_f, tri_b, u_f, u_b], axis=1))
    seam_x = np.concatenate([np.zeros((112, D), f32), meta], axis=0)
    seam_tok = np.full(128, -1)
    seam_valid = np.concatenate([np.zeros(112), np.ones(16)])
    maps = []
    for c in range(NS):
        own = np.arange(c * LS, (c + 1) * LS)
        after = np.arange((c + 1) * LS, NS * LS)
        before = np.arange(0, c * LS)
        xin = np.concatenate([xp[own], xp[after], seam_x, xp[before], xs[c], seam_x], axis=0)
        tok = np.concatenate([own, after, seam_tok, before, np.arange(LS), seam_tok])
        valid = np.concatenate([np.ones(LS + len(after)), seam_valid, np.ones(len(before)), np.ones(LS), seam_valid])
        assert xin.shape[0] == cfg.rtot
        nch = cfg.rtot // 128
        lay = lambda v: np.ascontiguousarray(v.reshape(nch, 128).T.astype(f32))
        seam_ch = (LS + len(after)) // 128
        keepf = np.ones(nch, f32)
        keepb = np.ones(nch, f32)
        keepf[seam_ch] = 0
        keepb[seam_ch - 1] = 0
        s_seam = cfg.rs0 // 128 + LS // 128
        keepf[s_seam] = 0
        keepb[s_seam - 1] = 0
        m = dict(shared)
        m["xin"] = np.ascontiguousarray(xin)
        m["ropet"] = _rope_rows(tok)
        m["kbias"] = lay((1 - valid) * -30000.0)
        m["dtmask"] = lay(valid)
        m["keep"] = np.ascontiguousarray(np.stack([np.broadcast_to(keepf, (128, nch)), np.broadcast_to(keepb, (128, nch))], axis=1).astype(f32))
        maps.append(m)
    return maps


_CACHE = {}


def kernel(**inputs):
    cfg = Cfg()
    if "nc" not in _CACHE:
        _CACHE["nc"] = Builder(cfg).build()
    nc = _CACHE["nc"]
    maps = prep_inputs(cfg, inputs)
    res = run_bass_kernel_spmd(nc, maps, core_ids=list(range(cfg.ncores)))
    ys = [np.asarray(r["yout"]) for r in res.results]
    y_prompt = np.concatenate([y[0:cfg.ls] for y in ys], axis=0)[None]
    y_sample = np.stack([y[cfg.ls:2 * cfg.ls] for y in ys], axis=0)
    return (y_prompt.astype(np.float32), y_sample.astype(np.float32))
```

```python
import contextlib
import numpy as np
import ml_dtypes
import concourse.bass as bass
import concourse.mybir as mybir
from concourse.bass_utils import run_bass_kernel_spmd

import concourse.bass as bass
import concourse.mybir as mybir

F32 = mybir.dt.float32
BF16 = mybir.dt.bfloat16
AF = mybir.ActivationFunctionType
ALU = mybir.AluOpType
AX = mybir.AxisListType

N_DMA_SEMS = 56
N_SP_SEMS = 36


class Op:
    __slots__ = ("eng", "fn", "deps", "needs_inc", "sem", "semval", "ndma", "idx")


class Prog:
    ENGS = ("pe", "act", "dve", "pool", "sp")

    def __init__(self, nc):
        self.nc = nc
        self.ops = {e: [] for e in self.ENGS}
        self.last_w = {}
        self.readers = {}
        self.dma_rr = 0
        self.dma_rr2 = 0
        self.dma_sem_last = [None] * N_DMA_SEMS
        self.dma_sem_cnt = [0] * N_DMA_SEMS
        self.out_dmas = []

    def op(self, eng, fn, reads=(), writes=(), ndma=0, is_output=False):
        o = Op()
        o.eng = eng
        o.fn = fn
        o.deps = []
        o.needs_inc = False
        o.sem = None
        o.semval = 0
        o.ndma = ndma
        deps = {}
        for k in reads:
            w = self.last_w.get(k)
            if w is not None:
                deps[id(w)] = w
        for k in writes:
            w = self.last_w.get(k)
            if w is not None:
                deps[id(w)] = w
            for r in self.readers.get(k, {}).values():
                if isinstance(r, list):
                    for rr in r:
                        deps[id(rr)] = rr
                else:
                    deps[id(r)] = r
        if ndma:
            if eng == "sp":
                i = self.dma_rr % N_SP_SEMS
                self.dma_rr += 1
            else:
                i = N_SP_SEMS + self.dma_rr2 % (N_DMA_SEMS - N_SP_SEMS)
                self.dma_rr2 += 1
            prev = self.dma_sem_last[i]
            if prev is not None:
                deps[id(prev)] = prev
            self.dma_sem_cnt[i] += 16 * ndma
            o.sem = ("dma", i)
            o.semval = self.dma_sem_cnt[i]
            self.dma_sem_last[i] = o
            o.needs_inc = True
        for d in deps.values():
            if d is o:
                continue
            if d.eng == "pe" and eng == "pe":
                continue
            o.deps.append(d)
            d.needs_inc = True
        for k in reads:
            rd = self.readers.setdefault(k, {})
            if ndma:
                rd.setdefault("dma", []).append(o)
            else:
                rd[eng] = o
        for k in writes:
            self.last_w[k] = o
            self.readers[k] = {}
        self.ops[eng].append(o)
        if is_output:
            self.out_dmas.append(o)
        return o

    def barrier(self):
        lasts = []
        for e in ("pe", "act", "dve", "pool"):
            for o in reversed(self.ops[e]):
                if o.fn is not None and not o.ndma:
                    lasts.append(o)
                    break
        dl = [o for o in self.dma_sem_last if o is not None]
        for e in self.ENGS:
            b = Op()
            b.eng = e; b.fn = None; b.needs_inc = False; b.sem = None; b.semval = 0; b.ndma = 0
            b.deps = [o for o in lasts if o.eng != e] + dl
            for d in b.deps:
                d.needs_inc = True
            self.ops[e].append(b)
        self.last_w = {}
        self.readers = {}

    def dma(self, out, in_, reads=(), writes=(), is_output=False, **kw):
        return self.op("sp", lambda e: [e.dma_start(out=out, in_=in_, **kw)], reads, writes,
                       ndma=1, is_output=is_output)

    def emit(self):
        nc = self.nc
        fin = Op()
        fin.eng = "sp"; fin.fn = None; fin.deps = list(self.out_dmas); fin.needs_inc = False
        fin.sem = None; fin.semval = 0; fin.ndma = 0
        self.ops["sp"].append(fin)
        for e in self.ENGS:
            c = 0
            for o in self.ops[e]:
                if o.ndma:
                    continue
                if o.needs_inc:
                    c += 1
                    o.sem = ("eng", e)
                    o.semval = c
        import contextlib
        with contextlib.ExitStack() as st:
            sems = {}
            for e in self.ENGS:
                sems[("eng", e)] = st.enter_context(nc.semaphore("s_" + e))
            for i in range(N_DMA_SEMS):
                sems[("dma", i)] = st.enter_context(nc.semaphore("s_dma%d" % i))
            block = st.enter_context(nc.Block())

            def run(ename):
                def body(engine):
                    known = {}
                    for o in self.ops[ename]:
                        for d in o.deps:
                            if known.get(d.sem, 0) < d.semval:
                                engine.wait_ge(sems[d.sem], d.semval)
                                known[d.sem] = d.semval
                        if o.fn is None:
                            continue
                        r = o.fn(engine)
                        if o.ndma:
                            for ins in r:
                                ins.then_inc(sems[o.sem], 16)
                        elif o.needs_inc:
                            ins = r[-1] if isinstance(r, (list, tuple)) else r
                            ins.then_inc(sems[o.sem], 1)
                return body

            block.tensor(run("pe"))
            block.scalar(run("act"))
            block.vector(run("dve"))
            block.gpsimd(run("pool"))
            block.sync(run("sp"))


class Rot:
    def __init__(self, name, aps):
        self.name = name; self.aps = aps; self.i = 0

    def next(self):
        k = self.i % len(self.aps)
        self.i += 1
        return "%s%d" % (self.name, k), self.aps[k]


D = 2048
KC = 16
EPS = 1e-6
N_META = 16
ATT_W = 1024
SSM_W = 1024
CONV_DIM = 1536
NTM = 2592
IN_PROJ = 4128


class Cfg:
    def __init__(self, dff=5632, ls=2048, ncores=8, do_w=True, do_a=True, do_b=True, do_c=True, debug=False):
        self.dff = dff
        self.ls = ls
        self.ncores = ncores
        self.nts = ls // 512
        self.rp = ncores * ls + 128
        self.rs0 = self.rp
        self.rtot = self.rp + ls + 128
        self.do_w, self.do_a, self.do_b, self.do_c = do_w, do_a, do_b, do_c
        self.debug = debug
        self.b_phases = ("b0", "b1", "b2", "b3")


class Builder:
    def __init__(self, cfg):
        self.cfg = cfg
        self.nc = bass.Bass("TRN2", target_bir_lowering=False)
        self.P = Prog(self.nc)

    def salloc(self, name, shape, dtype):
        assert shape[0] == 128
        n = 1
        for d in shape[1:]:
            n *= d
        esz = 4 if dtype == F32 else 2
        words = (n * esz + 3) // 4
        words = (words + 7) // 8 * 8
        off = self.sb_off
        self.sb_off += words
        assert self.sb_off <= self.sb_words, (name, self.sb_off, self.sb_words)
        v = self.SB[:, off:off + words]
        if dtype != F32:
            v = v.bitcast(dtype)
        v = v[:, 0:n]
        if len(shape) == 3:
            v = v.rearrange("p (a b) -> p a b", a=shape[1])
        return v

    def declare(self):
        nc, cfg = self.nc, self.cfg
        dff = cfg.dff
        I = lambda n, s, d=F32: nc.dram_tensor(n, s, d, kind="ExternalInput").ap()
        S = lambda n, s, d: nc.dram_tensor(n, s, d, kind=("ExternalOutput" if cfg.debug else "Internal")).ap()
        self.xin = I("xin", [cfg.rtot, D])
        self.wf = {
            "g1": I("ff1_wg", [D, dff]), "u1": I("ff1_wu", [D, dff]), "d1": I("ff1_wd", [dff, D]),
            "win": I("w_in_p", [D, IN_PROJ]), "wout": I("w_out_p", [D, D]),
            "g2": I("ff2_wg", [D, dff]), "u2": I("ff2_wu", [D, dff]), "d2": I("ff2_wd", [dff, D]),
        }
        self.gains = {k: I(k, [D]) for k in ("ff1_pre", "ff1_post", "mix_pre", "mix_post", "ff2_pre", "ff2_post")}
        self.ident_in = I("ident", [128, 128])
        self.yout = nc.dram_tensor("yout", [2 * cfg.ls, D], F32, kind="ExternalOutput").ap()
        self.own_rows = [(0, 0), (cfg.rs0, cfg.ls)]
        self.wb = {k: S("wb_" + k, list(v.shape), BF16) for k, v in self.wf.items()}
        self.h1 = S("h1", [cfg.rtot, D], F32)
        self.proj = S("proj", [cfg.rtot, NTM], F32)
        self.xbcT = S("xbcT", [CONV_DIM, cfg.rtot], BF16)
        if cfg.do_b:
            self.oT = S("oT", [D, 2 * cfg.ls], BF16)
        else:
            self.oT = I("oT", [D, 2 * cfg.ls], BF16)
        self.declare_b()

    def phase_w(self, st, names=None, nbuf=3):
        for _ in self.phase_w_gen(st, names, nbuf, False):
            pass

    def phase_w_gen(self, st, names=None, nbuf=3, as_gen=False):
        nc, P = self.nc, self.P
        sb = self.salloc
        stg = Rot("wstg", [sb("wstg%d" % i, [128, 4096], F32) for i in range(nbuf)])
        cvt = Rot("wcvt", [sb("wcvt%d" % i, [128, 4096], BF16) for i in range(nbuf)])
        engs = ["pool", "dve"] if as_gen else ["pool", "dve", "act"]
        ne = len(engs)
        n = 0
        for name in (names or list(self.wf.keys())):
            w = self.wf[name]
            K, N = w.shape
            per = K * N // 128
            src = w.rearrange("(p r) n -> p (r n)", p=128)
            dst = self.wb[name].rearrange("(p r) n -> p (r n)", p=128)
            for c0 in range(0, per, 4096):
                sz = min(4096, per - c0)
                sk, sap = stg.next()
                ck, cap = cvt.next()
                P.dma(sap[:, 0:sz], src[:, c0:c0 + sz], writes=[sk])
                e = engs[n % ne]
                n += 1
                if e == "act":
                    P.op("act", lambda en, o=cap[:, 0:sz], i=sap[:, 0:sz]: en.activation(out=o, in_=i, func=AF.Copy),
                         reads=[sk], writes=[ck])
                else:
                    P.op(e, lambda en, o=cap[:, 0:sz], i=sap[:, 0:sz]: en.tensor_copy(out=o, in_=i),
                         reads=[sk], writes=[ck])
                P.op("pool", lambda en, o=dst[:, c0:c0 + sz], i=cap[:, 0:sz]: [en.dma_start(out=o, in_=i)],
                     reads=[ck], writes=[("wbd", name, c0)], ndma=1)
                if as_gen:
                    yield None

    def alloc_ac(self, st, with_stage=True):
        nc = self.nc
        sb = self.salloc
        self.xt = Rot("xt", [sb("xt%d" % i, [128, D], F32) for i in range(5)])
        self.ub = [sb("ub%d" % i, [128, KC, 512], BF16) for i in range(2)]
        self.hmid = sb("hmid", [128, 44, 512], BF16)
        self.wbuf = Rot("wbuf", [sb("wbuf%d" % i, [128, 4096], BF16) for i in range(6)])
        self.un = Rot("un", [sb("un%d" % i, [128, D], BF16) for i in range(4)])
        self.junk = sb("junk", [128, D], BF16)
        self.sg = Rot("sg", [sb("sg%d" % i, [128, 512], F32) for i in range(2)])
        self.small = Rot("small", [sb("small%d" % i, [128, 4], F32) for i in range(4)])
        self.ident = sb("identb", [128, 128], BF16)
        self.identf = sb("identf", [128, 128], F32)
        self.nh = sb("nh", [128, 1], F32)
        self.gcol = {}
        self.gph = {}
        if with_stage:
            self.stage = Rot("stage", [sb("stage%d" % i, [128, 512], BF16) for i in range(2)])

    def init_consts(self, pre_names, post_names):
        P, nc = self.P, self.nc
        P.dma(self.identf[:], self.ident_in, writes=["identf"])
        P.op("dve", lambda e: e.tensor_copy(out=self.ident[:], in_=self.identf[:]), reads=["identf"], writes=["ident"])
        P.op("dve", lambda e: e.memset(self.nh[:], -0.5), writes=["nh"])
        for nm in pre_names:
            t = self.gcol[nm]
            P.op("sp", lambda e, t=t, nm=nm: [e.dma_start(out=t[:], in_=self.gains[nm].rearrange("(c p) -> p c", p=128),
                                                       allow_slow_non_contiguous=True)], writes=["gcol_" + nm], ndma=1)
        for nm in post_names:
            t = self.gph[nm]
            P.dma(t[:], self.gains[nm].partition_broadcast(128), writes=["gph_" + nm])
            P.op("dve", lambda e, t=t: e.tensor_scalar(out=t[:], in0=t[:], scalar1=0.5, scalar2=None, op0=ALU.mult),
                 reads=["gph_" + nm], writes=["gph_" + nm])

    def rstd_from(self, src_key, src_ap, junk_ap, junk_key):
        P = self.P
        sk, sm = self.small.next()
        P.op("act", lambda e: e.activation(out=junk_ap, in_=src_ap, func=AF.Square, accum_out=sm[:, 0:1]),
             reads=[src_key], writes=[junk_key, sk])
        P.op("dve", lambda e: e.tensor_scalar(out=sm[:, 1:2], in0=sm[:, 0:1], scalar1=1.0 / D, scalar2=EPS,
                                              op0=ALU.mult, op1=ALU.add), reads=[sk], writes=[sk])
        P.op("pool", lambda e: e.tensor_tensor(out=sm[:, 2:3], in0=sm[:, 1:2], in1=self.nh[:], op=ALU.pow),
             reads=[sk, "nh"], writes=[sk])
        return sk, sm[:, 2:3]

    def prenorm_a(self, xk, xap):
        P = self.P
        uk, un = self.un.next()
        rk, rstd = self.rstd_from(xk, xap[:], self.junk[:], "junk")
        P.op("dve", lambda e: e.tensor_scalar(out=un[:], in0=xap[:], scalar1=rstd, scalar2=None, op0=ALU.mult),
             reads=[xk, rk], writes=[uk])
        return uk, un

    def prenorm(self, xk, xap, gname, ub_i, s):
        uk, un = self.prenorm_a(xk, xap)
        self.prenorm_b(uk, un, gname, ub_i, s)

    def prenorm_b(self, uk, un, gname, ub_i, s):
        P = self.P
        pst = self.psum[:, 0:2, :].bitcast(BF16)
        for c in range(KC):
            P.op("pe", lambda e, c=c: e.transpose(out=pst[:, c // 8, (c % 8) * 128:(c % 8 + 1) * 128],
                                                  in_=un[:, c * 128:(c + 1) * 128], identity=self.ident[:]),
                 reads=[uk, "ident"], writes=["ps0" if c < 8 else "ps1"])
        gc = self.gcol[gname]
        ub = self.ub[ub_i]
        for hf in range(2):
            P.op("dve", lambda e, hf=hf: e.tensor_tensor(
                out=ub[:, hf * 8:(hf + 1) * 8, s * 128:(s + 1) * 128],
                in0=pst[:, hf, :].rearrange("p (c t) -> p c t", t=128),
                in1=gc[:, hf * 8:(hf + 1) * 8].unsqueeze(2).broadcast_to([128, 8, 128]), op=ALU.mult),
                reads=["ps%d" % hf, "gcol_" + gname], writes=[("ub", ub_i, s)])

    def load_cunit(self, name, c0, ncols):
        k, buf = self.wbuf.next()
        v = buf[:, 0:KC * ncols].rearrange("p (kc f) -> p kc f", kc=KC)
        src = self.wb[name].rearrange("(kc p) f -> p kc f", p=128)[:, :, c0:c0 + ncols]
        self.P.dma(v, src, writes=[k])
        return k, v

    def load_runit(self, name, kc0, nkc=2):
        k, buf = self.wbuf.next()
        v = buf[:, 0:nkc * D].rearrange("p (kc f) -> p kc f", kc=nkc)
        src = self.wb[name].rearrange("(kc p) f -> p kc f", p=128)[:, kc0:kc0 + nkc, :]
        self.P.dma(v, src, writes=[k])
        return k, v

    def gate_up(self, ub_i, nsub, gname, uname):
        P, cfg = self.P, self.cfg
        T = nsub * 128
        ub = self.ub[ub_i]
        ubkeys = [("ub", ub_i, s) for s in range(nsub)]
        nstep = cfg.dff // 256
        for step in range(nstep):
            gk, gv = self.load_cunit(gname, step * 256, 256)
            uk, uv = self.load_cunit(uname, step * 256, 256)
            b0 = (step % 2) * 4
            for jj in range(2):
                j = step * 2 + jj
                bg, bu = b0 + jj, b0 + 2 + jj
                for kc in range(KC):
                    P.op("pe", lambda e, kc=kc, jj=jj, bg=bg, gv=gv: e.matmul(
                        self.psum[:, bg, 0:T], lhsT=gv[:, kc, jj * 128:(jj + 1) * 128], rhs=ub[:, kc, 0:T],
                        start=(kc == 0), stop=(kc == KC - 1)), reads=[gk] + ubkeys, writes=["ps%d" % bg])
                for kc in range(KC):
                    P.op("pe", lambda e, kc=kc, jj=jj, bu=bu, uv=uv: e.matmul(
                        self.psum[:, bu, 0:T], lhsT=uv[:, kc, jj * 128:(jj + 1) * 128], rhs=ub[:, kc, 0:T],
                        start=(kc == 0), stop=(kc == KC - 1)), reads=[uk] + ubkeys, writes=["ps%d" % bu])
                sk, sg = self.sg.next()
                P.op("act", lambda e, bg=bg, sg=sg: e.activation(out=sg[:, 0:T], in_=self.psum[:, bg, 0:T], func=AF.Silu),
                     reads=["ps%d" % bg], writes=[sk])
                P.op("dve", lambda e, bu=bu, sg=sg, j=j: e.tensor_tensor(
                    out=self.hmid[:, j, 0:T], in0=sg[:, 0:T], in1=self.psum[:, bu, 0:T], op=ALU.mult),
                    reads=[sk, "ps%d" % bu], writes=[("hm", j)])

    def proj_rows(self, wname, nkc, lhs_fn, lhs_keys_fn, nsub, epilogue):
        P = self.P
        for sp0 in range(0, nsub, 2):
            ss = list(range(sp0, min(sp0 + 2, nsub)))
            G = 3
            for g0 in range(0, nkc, 2 * G):
                units = [(kc0, self.load_runit(wname, kc0, 2)) for kc0 in range(g0, min(g0 + 2 * G, nkc), 2)]
                for s in ss:
                    for (kc0, (wk, wv)) in units:
                        for kk in range(2):
                            kc = kc0 + kk
                            for blk in range(4):
                                b = (s % 2) * 4 + blk
                                P.op("pe", lambda e, kc=kc, kk=kk, s=s, blk=blk, b=b, wv=wv: e.matmul(
                                    self.psum[:, b, :], lhsT=lhs_fn(kc, s), rhs=wv[:, kk, blk * 512:(blk + 1) * 512],
                                    start=(kc == 0), stop=(kc == nkc - 1)),
                                    reads=[wk] + lhs_keys_fn(kc, s), writes=["ps%d" % b])
            for s in ss:
                b0 = (s % 2) * 4
                epilogue(s, self.psum[:, b0:b0 + 4, :].rearrange("p b f -> p (b f)"), ["ps%d" % (b0 + i) for i in range(4)])

    def post_residual(self, yps, ykeys, xk, xap, gpost):
        P = self.P
        rk, rstd = self.rstd_from_multi(ykeys, yps)
        gph = self.gph[gpost]
        P.op("dve", lambda e: e.scalar_tensor_tensor(out=yps, in0=yps, scalar=rstd, in1=gph[:], op0=ALU.mult, op1=ALU.mult),
             reads=ykeys + [rk, "gph_" + gpost], writes=ykeys)
        P.op("dve", lambda e: e.tensor_tensor(out=xap[:], in0=yps, in1=xap[:], op=ALU.add),
             reads=ykeys + [xk], writes=[xk])

    def rstd_from_multi(self, keys, src_ap):
        P = self.P
        sk, sm = self.small.next()
        P.op("act", lambda e: e.activation(out=self.junk[:], in_=src_ap, func=AF.Square, accum_out=sm[:, 0:1]),
             reads=list(keys), writes=["junk", sk])
        P.op("dve", lambda e: e.tensor_scalar(out=sm[:, 1:2], in0=sm[:, 0:1], scalar1=1.0 / D, scalar2=EPS,
                                              op0=ALU.mult, op1=ALU.add), reads=[sk], writes=[sk])
        P.op("pool", lambda e: e.tensor_tensor(out=sm[:, 2:3], in0=sm[:, 1:2], in1=self.nh[:], op=ALU.pow),
             reads=[sk, "nh"], writes=[sk])
        return sk, sm[:, 2:3]

    def ffn(self, xts, ub_i, gn, un_, dn, gpre, gpost, pre=None, after_post=None):
        nsub = len(xts)
        for s, (xk, xap) in enumerate(xts):
            if pre is not None:
                self.prenorm_b(pre[s][0], pre[s][1], gpre, ub_i, s)
            else:
                self.prenorm(xk, xap, gpre, ub_i, s)
        self.gate_up(ub_i, nsub, gn, un_)
        nfc = self.cfg.dff // 128

        def epi(s, yps, ykeys):
            self.post_residual(yps, ykeys, xts[s][0], xts[s][1], gpost)
            if after_post is not None:
                after_post(s)
        self.proj_rows(dn, nfc, lambda kc, s: self.hmid[:, kc, s * 128:(s + 1) * 128],
                       lambda kc, s: [("hm", kc)], nsub, epi)

    def phase_a(self, st):
        nc, P, cfg = self.nc, self.P, self.cfg
        self.alloc_ac(st)
        sb = self.salloc
        for nm in ("ff1_pre", "mix_pre"):
            self.gcol[nm] = sb("gcol_" + nm, [128, KC], F32)
        self.gph["ff1_post"] = sb("gph_ff1_post", [128, D], F32)
        self.init_consts(("ff1_pre", "mix_pre"), ("ff1_post",))
        tiles = []
        for (b0, nrows) in ((0, cfg.rp), (cfg.rs0, cfg.ls + 128)):
            r = 0
            while r < nrows:
                n = min(512, nrows - r)
                tiles.append((b0 + r, n // 128, r < cfg.ls))
                r += n
        projv = self.hmid[:, 0:41, :].rearrange("p a b -> p (a b)").bitcast(F32)
        def load_tile(r0, nsub):
            xts = []
            for s in range(nsub):
                xk, xap = self.xt.next()
                P.dma(xap[:], self.xin[r0 + s * 128:r0 + (s + 1) * 128, :], writes=[xk])
                xts.append((xk, xap))
            return xts

        cur_xts = load_tile(tiles[0][0], tiles[0][1])
        cur_pre = None
        for ti, (r0, nsub, own) in enumerate(tiles):
            T = nsub * 128
            xts = cur_xts
            pend = [None] * nsub

            def after_post(s, xts=xts, pend=pend, own=own, r0=r0):
                xk, xap = xts[s]
                if own:
                    P.op("pool", lambda e: [e.dma_start(out=self.h1[r0 + s * 128:r0 + (s + 1) * 128, :], in_=xap[:])],
                         reads=[xk], writes=[("h1", r0, s)], ndma=1)
                pend[s] = self.prenorm_a(xk, xap)
            self.ffn(xts, 0, "g1", "u1", "d1", "ff1_pre", "ff1_post", pre=cur_pre, after_post=after_post)
            for s in range(nsub):
                self.prenorm_b(pend[s][0], pend[s][1], "mix_pre", 1, s)
            if ti + 1 < len(tiles):
                cur_xts = load_tile(tiles[ti + 1][0], tiles[ti + 1][1])
                cur_pre = [self.prenorm_a(xk, xap) for (xk, xap) in cur_xts]
            ub = self.ub[1]
            hmkeys = [("hm", j) for j in range(44)]
            c0 = 0
            u = 0
            first = True
            while c0 < NTM:
                ncols = min(256, NTM - c0)
                if not own and not (1024 <= c0 < 1536 or c0 >= 2560):
                    c0 += ncols
                    u += 1
                    continue
                wk, wv = self.load_cunit("win", c0, ncols)
                for s in range(nsub):
                    b = (u % 2) * 4 + s
                    for kc in range(KC):
                        P.op("pe", lambda e, kc=kc, s=s, b=b, wv=wv, ncols=ncols: e.matmul(
                            self.psum[:, b, 0:ncols], lhsT=ub[:, kc, s * 128:(s + 1) * 128], rhs=wv[:, kc, :],
                            start=(kc == 0), stop=(kc == KC - 1)), reads=[wk, ("ub", 1, s)], writes=["ps%d" % b])
                    eng = "act" if (u + s) % 2 == 0 else "dve"
                    dstv = projv[:, s * NTM + c0:s * NTM + c0 + ncols]
                    if eng == "act":
                        P.op("act", lambda e, b=b, dstv=dstv, ncols=ncols: e.activation(out=dstv, in_=self.psum[:, b, 0:ncols], func=AF.Copy),
                             reads=["ps%d" % b] + (hmkeys if first else []), writes=[("pj", s, u)] + (hmkeys if first else []))
                    else:
                        P.op("dve", lambda e, b=b, dstv=dstv, ncols=ncols: e.tensor_copy(out=dstv, in_=self.psum[:, b, 0:ncols]),
                             reads=["ps%d" % b] + (hmkeys if first else []), writes=[("pj", s, u)] + (hmkeys if first else []))
                    first = False
                c0 += ncols
                u += 1
            nu = u
            P.op("pool", lambda e, r0=r0, nsub=nsub: [e.dma_start(out=self.proj[r0 + s * 128:r0 + (s + 1) * 128, :],
                                                                 in_=projv[:, s * NTM:(s + 1) * NTM]) for s in range(nsub)],
                 reads=[("pj", s, uu) for uu in range(nu) for s in range(nsub)], writes=[("projd", r0)] + hmkeys, ndma=nsub)
            for cu in range(CONV_DIM // 256):
                wk, wv = self.load_cunit("win", NTM + cu * 256, 256)
                for jj in range(2):
                    j = cu * 2 + jj
                    b = j % 8
                    for kc in range(KC):
                        P.op("pe", lambda e, kc=kc, jj=jj, b=b, wv=wv, T=T: e.matmul(
                            self.psum[:, b, 0:T], lhsT=wv[:, kc, jj * 128:(jj + 1) * 128], rhs=ub[:, kc, 0:T],
                            start=(kc == 0), stop=(kc == KC - 1)),
                            reads=[wk] + [("ub", 1, s) for s in range(nsub)], writes=["ps%d" % b])
                    sk, sg = self.stage.next()
                    if j % 2 == 0:
                        P.op("act", lambda e, b=b, sg=sg, T=T: e.activation(out=sg[:, 0:T], in_=self.psum[:, b, 0:T], func=AF.Copy),
                             reads=["ps%d" % b], writes=[sk])
                    else:
                        P.op("dve", lambda e, b=b, sg=sg, T=T: e.tensor_copy(out=sg[:, 0:T], in_=self.psum[:, b, 0:T]),
                             reads=["ps%d" % b], writes=[sk])
                    P.op("pool", lambda e, j=j, sg=sg, r0=r0, T=T: [e.dma_start(out=self.xbcT[j * 128:(j + 1) * 128, r0:r0 + T], in_=sg[:, 0:T])],
                         reads=[sk], writes=[("xbcd", j, r0)], ndma=1)

    def phase_c(self, st):
        nc, P, cfg = self.nc, self.P, self.cfg
        self.alloc_ac(st, with_stage=False)
        sb = self.salloc
        self.gcol["ff2_pre"] = sb("gcol_ff2_pre", [128, KC], F32)
        self.gph["mix_post"] = sb("gph_mix_post", [128, D], F32)
        self.gph["ff2_post"] = sb("gph_ff2_post", [128, D], F32)
        self.init_consts(("ff2_pre",), ("mix_post", "ff2_post"))
        P.op("dve", lambda e: e.tensor_scalar(out=self.gph["mix_post"][:], in0=self.gph["mix_post"][:], scalar1=2.0,
                                              scalar2=None, op0=ALU.mult), reads=["gph_mix_post"], writes=["gph_mix_post"])
        ctiles = [(sr0 + t * 512, or0 + t * 512) for (sr0, or0) in self.own_rows for t in range(cfg.nts)]
        for (hr0, r0) in ctiles:
            xts = []
            for s in range(4):
                xk, xap = self.xt.next()
                P.dma(xap[:], self.h1[hr0 + s * 128:hr0 + (s + 1) * 128, :], writes=[xk])
                xts.append((xk, xap))
            ub = self.ub[1]
            P.dma(ub[:], self.oT.rearrange("(kc p) t -> p kc t", p=128)[:, :, r0:r0 + 512], writes=[("ub", 1, s) for s in range(4)])
            pend = [None] * 4

            def epi_c(s, yps, ykeys, xts=xts, pend=pend):
                self.post_residual(yps, ykeys, xts[s][0], xts[s][1], "mix_post")
                pend[s] = self.prenorm_a(xts[s][0], xts[s][1])
            self.proj_rows("wout", KC, lambda kc, s: ub[:, kc, s * 128:(s + 1) * 128],
                           lambda kc, s: [("ub", 1, s)], 4, epi_c)

            def store(s, xts=xts, r0=r0):
                xk, xap = xts[s]
                P.op("pool", lambda e: [e.dma_start(out=self.yout[r0 + s * 128:r0 + (s + 1) * 128, :], in_=xap[:])],
                     reads=[xk], ndma=1, is_output=True)
            self.ffn(xts, 0, "g2", "u2", "d2", "ff2_pre", "ff2_post", pre=pend, after_post=store)

    def declare_b(self):
        nc, cfg = self.nc, self.cfg
        I = lambda n, s, d=F32: nc.dram_tensor(n, s, d, kind="ExternalInput").ap()
        S = lambda n, s, d: nc.dram_tensor(n, s, d, kind=("ExternalOutput" if cfg.debug else "Internal")).ap()
        nch = cfg.rtot // 128
        self.cst = I("cst", [128, 5, 128])
        self.ropet = I("ropet", [cfg.rtot, 128])
        self.qkg = I("qkg", [4, 64])
        self.kbias = I("kbias", [128, nch])
        self.dtmask = I("dtmask", [128, nch])
        self.keep = I("keep", [128, 2, nch])
        self.conv_w = I("conv_w", [5, CONV_DIM])
        self.conv_b = I("conv_b", [CONV_DIM])
        self.a_log = I("a_log", [32])
        self.dt_bias = I("dt_bias", [32])
        self.dskip = I("dskip", [SSM_W])
        self.ssm_norm = I("ssm_norm", [SSM_W])
        self.xsB = S("xsB", [cfg.rtot, 1280], BF16)
        self.bcT = S("bcT", [512, cfg.rtot], BF16)
        self.qT = S("qT", [1024, 2 * cfg.ls], BF16)
        self.kT = S("kT", [256, cfg.rtot], BF16)
        self.vx = S("vx", [cfg.rtot, 640], BF16)
        self.blocks = [(0, cfg.rp, 0), (cfg.rs0, cfg.ls + 128, cfg.ls)]

    def phase_b0(self):
        nc, P, cfg = self.nc, self.P, self.cfg
        sb = self.salloc
        identf = sb("identf", [128, 128], F32)
        ident = sb("ident", [128, 128], BF16)
        wcol = sb("wcol", [128, 5, 12], F32)
        bcol = sb("bcol", [128, 12], F32)
        xin = Rot("cxin", [sb("cxin%d" % i, [128, 1028], BF16) for i in range(4)])
        wd = sb("wd", [128, 60, 128], BF16)
        cvo = [sb("cvo%d" % i, [128, 1024], BF16) for i in range(12)]
        stg = Rot("cstg", [sb("cstg%d" % i, [128, 1280], BF16) for i in range(2)])
        P.dma(identf[:], self.cst[:, 0, :], writes=["identf"])
        P.op("dve", lambda e: e.tensor_copy(out=ident[:], in_=identf[:]), reads=["identf"], writes=["ident"])
        P.op("sp", lambda e: [e.dma_start(out=wcol[:, j, :], in_=self.conv_w[j, :].rearrange("(c p) -> p c", p=128), allow_slow_non_contiguous=True)
                              for j in range(5)], writes=["wcol"], ndma=5)
        P.op("sp", lambda e: [e.dma_start(out=bcol[:], in_=self.conv_b.rearrange("(c p) -> p c", p=128), allow_slow_non_contiguous=True)],
             writes=["bcol"], ndma=1)
        for cc in range(12):
            for j in range(5):
                P.op("dve" if (cc + j) % 2 == 0 else "pool", lambda e, cc=cc, j=j: e.tensor_scalar(
                    out=wd[:, cc * 5 + j, :], in0=identf[:], scalar1=wcol[:, j, cc:cc + 1], scalar2=None, op0=ALU.mult),
                    reads=["identf", "wcol"], writes=[("wd", cc, j)])
        pst = self.psum[:, 0:4, :].bitcast(BF16)
        tcount = 0
        ccount = 0
        for (b0, nrows, _) in self.blocks:
            c = 0
            while c < nrows:
                w = min(1024, nrows - c)
                c0 = b0 + c
                lh = b0 + (c - 2) % nrows
                rh = b0 + (c + w) % nrows
                for cc in range(12):
                    xk, xa = xin.next()
                    rows = slice(cc * 128, (cc + 1) * 128)
                    P.op("sp", lambda e, xa=xa, rows=rows, c0=c0, w=w, lh=lh, rh=rh: [
                        e.dma_start(out=xa[:, 2:2 + w], in_=self.xbcT[rows, c0:c0 + w]),
                        e.dma_start(out=xa[:, 0:2], in_=self.xbcT[rows, lh:lh + 2]),
                        e.dma_start(out=xa[:, 2 + w:4 + w], in_=self.xbcT[rows, rh:rh + 2])], writes=[xk], ndma=3)
                    for hf in range((w + 511) // 512):
                        wh = min(512, w - hf * 512)
                        bank = 4 + ccount % 4
                        ccount += 1
                        for j in range(5):
                            P.op("pe", lambda e, xa=xa, cc=cc, j=j, hf=hf, wh=wh, bank=bank: e.matmul(
                                self.psum[:, bank, 0:wh], lhsT=wd[:, cc * 5 + j, :], rhs=xa[:, hf * 512 + j:hf * 512 + j + wh],
                                start=(j == 0), stop=(j == 4)), reads=[xk] + [("wd", cc, j)], writes=["ps%d" % bank])
                        P.op("act", lambda e, cc=cc, hf=hf, wh=wh, bank=bank: e.activation(
                            out=cvo[cc][:, hf * 512:hf * 512 + wh], in_=self.psum[:, bank, 0:wh], func=AF.Silu, bias=bcol[:, cc:cc + 1]),
                            reads=["ps%d" % bank, "bcol"], writes=[("cvo", cc)])
                for cc in (range(8, 12) if c < cfg.ls else ()):
                    P.op("pool", lambda e, cc=cc, c0=c0, w=w: [e.dma_start(out=self.bcT[(cc - 8) * 128:(cc - 7) * 128, c0:c0 + w], in_=cvo[cc][:, 0:w])],
                         reads=[("cvo", cc)], writes=[("bcTd", cc, c0)], ndma=1)
                for s in range(w // 128):
                    pb = (tcount % 2) * 2
                    tcount += 1
                    for cc in range(10):
                        P.op("pe", lambda e, cc=cc, s=s, pb=pb: e.transpose(
                            out=pst[:, pb + cc // 8, (cc % 8) * 128:(cc % 8 + 1) * 128], in_=cvo[cc][:, s * 128:(s + 1) * 128], identity=ident[:]),
                            reads=[("cvo", cc), "ident"], writes=["ps%d" % (pb + cc // 8)])
                    sk, sa = stg.next()
                    P.op("act", lambda e, sa=sa, pb=pb: e.activation(out=sa[:, 0:1024], in_=pst[:, pb, :], func=AF.Copy),
                         reads=["ps%d" % pb], writes=[sk])
                    P.op("dve", lambda e, sa=sa, pb=pb: e.tensor_copy(out=sa[:, 1024:1280], in_=pst[:, pb + 1, 0:256]),
                         reads=["ps%d" % (pb + 1), sk], writes=[sk])
                    P.op("pool", lambda e, sa=sa, r=c0 + s * 128: [e.dma_start(out=self.xsB[r:r + 128, :], in_=sa[:])],
                         reads=[sk], writes=[("xsBd", c0, s)], ndma=1)
                c += w

    def phase_b1(self):
        nc, P, cfg = self.nc, self.P, self.cfg
        sb = self.salloc
        identf = sb("identf", [128, 128], F32)
        ident = sb("ident", [128, 128], BF16)
        nh = sb("nh", [128, 1], F32)
        gqq = sb("gqq", [128, 128], F32)
        gkk = sb("gkk", [128, 128], F32)
        pj = Rot("pj", [sb("pj%d" % i, [128, 1536], F32) for i in range(2)])
        rp = Rot("rp", [sb("rp%d" % i, [128, 128], F32) for i in range(2)])
        sq = sb("sq", [128, 1280], F32)
        xn = sb("xn", [128, 1280], F32)
        t1 = sb("t1", [128, 1280], F32)
        t2 = sb("t2", [128, 1280], F32)
        tq = sb("tq", [128, 128], F32)
        tk = sb("tk", [128, 128], F32)
        sm = Rot("b1sm", [sb("b1sm%d" % i, [128, 64], F32) for i in range(2)])
        yb = Rot("yb", [sb("yb%d" % i, [128, 1280], BF16) for i in range(2)])
        vxt = Rot("vxt", [sb("vxt%d" % i, [128, 640], BF16) for i in range(2)])
        qst = Rot("qst", [sb("qst%d" % i, [128, 8, 512], BF16) for i in range(2)])
        kst = Rot("kst", [sb("kst%d" % i, [128, 2, 512], BF16) for i in range(2)])
        pj4 = Rot("pj4", [sb("pj4_%d" % i, [128, 4, 512], F32) for i in range(2)])
        rp4 = Rot("rp4", [sb("rp4_%d" % i, [128, 4, 128], F32) for i in range(2)])
        sq4 = sb("sq4", [128, 4, 256], F32)
        xn4 = sb("xn4", [128, 4, 256], F32)
        xw4 = sb("xw4", [128, 4, 256], F32)
        t14 = sb("t14", [128, 4, 256], F32)
        t24 = sb("t24", [128, 4, 256], F32)
        tk4 = sb("tk4", [128, 4, 128], F32)
        sm4 = Rot("sm4", [sb("sm4_%d" % i, [128, 64], F32) for i in range(2)])
        yb4 = Rot("yb4", [sb("yb4_%d" % i, [128, 4, 256], BF16) for i in range(2)])
        vx4 = Rot("vx4", [sb("vx4_%d" % i, [128, 4, 640], BF16) for i in range(2)])
        for i in range(2):
            P.op("pool", lambda e, i=i: e.memset(vx4.aps[i][:].rearrange("p s w -> p (s w)"), 1.0), writes=["vx4_%d" % i])
        P.dma(identf[:], self.cst[:, 0, :], writes=["identf"])
        P.op("dve", lambda e: e.tensor_copy(out=ident[:], in_=identf[:]), reads=["identf"], writes=["ident"])
        P.op("dve", lambda e: e.memset(nh[:], -0.5), writes=["nh"])
        P.dma(gqq[:], self.qkg[0:2, :].rearrange("a d -> (a d)").partition_broadcast(128), writes=["gqq"])
        P.dma(gkk[:], self.qkg[2:4, :].rearrange("a d -> (a d)").partition_broadcast(128), writes=["gkk"])
        P.op("dve", lambda e: e.tensor_scalar(out=gqq[:], in0=gqq[:], scalar1=0.125, scalar2=None, op0=ALU.mult), reads=["gqq"], writes=["gqq"])
        for i in range(2):
            P.op("pool", lambda e, i=i: e.memset(vxt.aps[i][:], 1.0), writes=["vxt%d" % i])
        pst = self.psum[:, 0:4, :].bitcast(BF16)
        tcount = 0
        for bi, (b0, nrows, oc0) in enumerate(self.blocks):
            r = 0
            while r < nrows:
                n = min(512, nrows - r)
                own = r < cfg.ls
                if not own:
                    ns = n // 128
                    r0 = b0 + r
                    pk, pa_ = pj4.next()
                    rk, ra_ = rp4.next()
                    pa = pa_[:, 0:ns, :]
                    ra = ra_[:, 0:ns, :]
                    P.dma(pa, self.proj[r0:r0 + n, 1024:1536].rearrange("(s p) f -> p s f", p=128), writes=[pk])
                    P.dma(ra, self.ropet[r0:r0 + n, :].rearrange("(s p) f -> p s f", p=128), writes=[rk])
                    sk, sa = sm4.next()
                    nh4 = ns * 4
                    kv = lambda t: t[:, 0:ns, :]
                    h3 = lambda ap: ap.rearrange("p s (h d) -> p (s h) d", d=64)
                    P.op("act", lambda e, pa=pa, ns=ns: e.activation(out=sq4[:, 0:ns, :], in_=pa[:, :, 0:256], func=AF.Square), reads=[pk], writes=["sq4"])
                    P.op("dve", lambda e, sa=sa, ns=ns, nh4=nh4: e.tensor_reduce(out=sa[:, 0:nh4], in_=sq4[:, 0:ns, :].rearrange("p s (h d) -> p (s h) d", d=64),
                                                                             axis=AX.X, op=ALU.add), reads=["sq4"], writes=[sk])
                    P.op("dve", lambda e, sa=sa, nh4=nh4: e.tensor_scalar(out=sa[:, 16:16 + nh4], in0=sa[:, 0:nh4], scalar1=1.0 / 64, scalar2=EPS,
                                                                      op0=ALU.mult, op1=ALU.add), reads=[sk], writes=[sk])
                    P.op("pool", lambda e, sa=sa, nh4=nh4: e.tensor_tensor(out=sa[:, 32:32 + nh4], in0=sa[:, 16:16 + nh4],
                                                                       in1=nh[:, 0:1].broadcast_to([128, nh4]), op=ALU.pow), reads=[sk, "nh"], writes=[sk])
                    P.op("dve", lambda e, pa=pa, sa=sa, ns=ns, nh4=nh4: e.tensor_tensor(
                        out=xn4[:, 0:ns, :].rearrange("p s (h d) -> p s h d", d=64), in0=pa[:, :, 0:256].rearrange("p s (h d) -> p s h d", d=64),
                        in1=sa[:, 32:32 + nh4].rearrange("p (s h) -> p s h", h=4).unsqueeze(3).broadcast_to([128, ns, 4, 64]), op=ALU.mult),
                        reads=[pk, sk], writes=["xn4"])
                    P.op("pool", lambda e, ra=ra, ns=ns: e.tensor_tensor(out=tk4[:, 0:ns, :], in0=ra, in1=gkk[:].unsqueeze(1).broadcast_to([128, ns, 128]), op=ALU.mult),
                         reads=[rk, "gkk"], writes=["tk4"])
                    for b in range(2):
                        x5 = xn4[:, 0:ns, :].rearrange("p s (h a b i) -> p (s h) a b i", a=2, b=2, i=16)
                        o5 = xw4[:, 0:ns, :].rearrange("p s (h a b i) -> p (s h) a b i", a=2, b=2, i=16)
                        P.op("pool", lambda e, x5=x5, o5=o5, b=b: e.tensor_copy(out=o5[:, :, :, b, :], in_=x5[:, :, :, 1 - b, :]), reads=["xn4"], writes=[("xw4", b)])
                    q4 = lambda t, ns=ns: t[:, 0:ns, :].rearrange("p s (h d) -> p s h d", d=64)
                    P.op("dve", lambda e, ns=ns, q4=q4: e.tensor_tensor(out=q4(t14), in0=q4(xn4), in1=tk4[:, 0:ns, 0:64].unsqueeze(2).broadcast_to([128, ns, 4, 64]),
                                                                    op=ALU.mult), reads=["xn4", "tk4"], writes=["t14"])
                    P.op("dve", lambda e, ns=ns, q4=q4: e.tensor_tensor(out=q4(t24), in0=q4(xw4), in1=tk4[:, 0:ns, 64:128].unsqueeze(2).broadcast_to([128, ns, 4, 64]),
                                                                    op=ALU.mult), reads=[("xw4", 0), ("xw4", 1), "tk4"], writes=["t24"])
                    yk, ya_ = yb4.next()
                    ya = ya_[:, 0:ns, :]
                    P.op("dve", lambda e, ya=ya, ns=ns: e.tensor_tensor(out=ya, in0=t14[:, 0:ns, :], in1=t24[:, 0:ns, :], op=ALU.add), reads=["t14", "t24"], writes=[yk])
                    vk, va_ = vx4.next()
                    va = va_[:, 0:ns, :]
                    for kp in range(2):
                        P.op("act", lambda e, va=va, pa=pa, kp=kp: e.activation(
                            out=va[:, :, kp * 320 + 64:kp * 320 + 320].rearrange("p s (j w) -> p s j w", j=2)[:, :, :, 0:64],
                            in_=pa[:, :, 256 + kp * 128:384 + kp * 128].rearrange("p s (j d) -> p s j d", j=2), func=AF.Copy), reads=[pk], writes=[(vk, kp)])
                    P.op("pool", lambda e, va=va, r0=r0, n=n: [e.dma_start(out=self.vx[r0:r0 + n, :].rearrange("(s p) w -> p s w", p=128), in_=va)],
                         reads=[(vk, 0), (vk, 1)], writes=[("vxd", r0), (vk, 0), (vk, 1)], ndma=1)
                    pb = (tcount % 2) * 2
                    tcount += 1
                    for s_ in range(ns):
                        for p_ in range(2):
                            P.op("pe", lambda e, p_=p_, s_=s_, ya=ya, pb=pb: e.transpose(out=pst[:, pb, p_ * 512 + s_ * 128:p_ * 512 + (s_ + 1) * 128],
                                                                                   in_=ya[:, s_, p_ * 128:(p_ + 1) * 128], identity=ident[:]),
                                 reads=[yk, "ident"], writes=["ps%d" % pb])
                    kk_, ka = kst.next()
                    P.op("dve", lambda e, ka=ka, pb=pb, n=n: e.tensor_copy(out=ka[:, :, 0:n], in_=pst[:, pb, :].rearrange("p (c t) -> p c t", c=2)[:, :, 0:n]),
                         reads=["ps%d" % pb], writes=[(kk_, s_) for s_ in range(4)])
                    P.op("pool", lambda e, ka=ka, rr=r0, n=n: [e.dma_start(out=self.kT.rearrange("(c p) t -> p c t", p=128)[:, :, rr:rr + n], in_=ka[:, :, 0:n])],
                         reads=[(kk_, s_) for s_ in range(4)], writes=[("kTd", r0)] + [(kk_, s_) for s_ in range(4)], ndma=1)
                    r += n
                    continue
                qk_, qa = qst.next()
                kk_, ka = kst.next()
                for s in range(n // 128):
                    r0 = b0 + r + s * 128
                    c_lo = 0 if own else 1024
                    h_lo = 0 if own else 16
                    nh_ = 20 - h_lo
                    pk, pa = pj.next()
                    rk, ra = rp.next()
                    P.dma(pa[:, c_lo:1536], self.proj[r0:r0 + 128, c_lo:1536], writes=[pk])
                    P.dma(ra[:], self.ropet[r0:r0 + 128, :], writes=[rk])
                    sk, sa = sm.next()
                    P.op("act", lambda e, pa=pa, c_lo=c_lo: e.activation(out=sq[:, c_lo:1280], in_=pa[:, c_lo:1280], func=AF.Square),
                         reads=[pk], writes=["sq"])
                    P.op("dve", lambda e, sa=sa, c_lo=c_lo, h_lo=h_lo: e.tensor_reduce(
                        out=sa[:, h_lo:20], in_=sq[:, c_lo:1280].rearrange("p (h d) -> p h d", d=64), axis=AX.X, op=ALU.add),
                        reads=["sq"], writes=[sk])
                    P.op("dve", lambda e, sa=sa, h_lo=h_lo: e.tensor_scalar(out=sa[:, 20 + h_lo:40], in0=sa[:, h_lo:20], scalar1=1.0 / 64, scalar2=EPS,
                                                                       op0=ALU.mult, op1=ALU.add), reads=[sk], writes=[sk])
                    P.op("pool", lambda e, sa=sa, h_lo=h_lo, nh_=nh_: e.tensor_tensor(out=sa[:, 40 + h_lo:60], in0=sa[:, 20 + h_lo:40],
                                                                                in1=nh[:, 0:1].broadcast_to([128, nh_]), op=ALU.pow),
                         reads=[sk, "nh"], writes=[sk])
                    P.op("dve", lambda e, pa=pa, sa=sa, c_lo=c_lo, h_lo=h_lo, nh_=nh_: e.tensor_tensor(
                        out=xn[:, c_lo:1280].rearrange("p (h d) -> p h d", d=64), in0=pa[:, c_lo:1280].rearrange("p (h d) -> p h d", d=64),
                        in1=sa[:, 40 + h_lo:60].unsqueeze(2).broadcast_to([128, nh_, 64]), op=ALU.mult), reads=[pk, sk], writes=["xn"])
                    if own:
                        P.op("pool", lambda e, ra=ra: e.tensor_tensor(out=tq[:], in0=ra[:], in1=gqq[:], op=ALU.mult), reads=[rk, "gqq"], writes=["tq"])
                    P.op("pool", lambda e, ra=ra: e.tensor_tensor(out=tk[:], in0=ra[:], in1=gkk[:], op=ALU.mult), reads=[rk, "gkk"], writes=["tk"])
                    groups = ([(0, 16, tq, "tq")] if own else []) + [(16, 4, tk, "tk")]
                    for (h0, hn, tt, tkey) in groups:
                        xv = xn[:, h0 * 64:(h0 + hn) * 64]
                        P.op("dve", lambda e, xv=xv, tt=tt, h0=h0, hn=hn: e.tensor_tensor(
                            out=t1[:, h0 * 64:(h0 + hn) * 64].rearrange("p (h d) -> p h d", d=64), in0=xv.rearrange("p (h d) -> p h d", d=64),
                            in1=tt[:, 0:64].unsqueeze(1).broadcast_to([128, hn, 64]), op=ALU.mult), reads=["xn", tkey], writes=[("t1", h0)])
                        for b in range(2):
                            x5 = xv.rearrange("p (h a b i) -> p h a b i", a=2, b=2, i=16)
                            o5 = t2[:, h0 * 64:(h0 + hn) * 64].rearrange("p (h a b i) -> p h a b i", a=2, b=2, i=16)
                            s4 = tt[:, 64:128].rearrange("p (a b i) -> p a b i", a=2, b=2, i=16)
                            P.op("pool", lambda e, x5=x5, o5=o5, s4=s4, b=b, hn=hn: e.tensor_tensor(
                                out=o5[:, :, :, b, :], in0=x5[:, :, :, 1 - b, :],
                                in1=s4[:, :, b, :].unsqueeze(1).broadcast_to([128, hn, 2, 16]), op=ALU.mult),
                                reads=["xn", tkey], writes=[("t2", h0, b)])
                    yk, ya = yb.next()
                    P.op("dve", lambda e, ya=ya, c_lo=c_lo: e.tensor_tensor(out=ya[:, c_lo:1280], in0=t1[:, c_lo:1280], in1=t2[:, c_lo:1280], op=ALU.add),
                         reads=[("t1", 0), ("t1", 16), ("t2", 0, 0), ("t2", 0, 1), ("t2", 16, 0), ("t2", 16, 1)], writes=[yk])
                    vk, va = vxt.next()
                    P.op("act", lambda e, va=va, pa=pa: e.activation(
                        out=va[:].rearrange("p (kp w) -> p kp w", kp=2)[:, :, 64:320].rearrange("p kp (j w) -> p kp j w", j=2)[:, :, :, 0:64],
                        in_=pa[:, 1280:1536].rearrange("p (kp j d) -> p kp j d", kp=2, j=2), func=AF.Copy), reads=[pk], writes=[vk])
                    P.op("pool", lambda e, va=va, r0=r0: [e.dma_start(out=self.vx[r0:r0 + 128, :], in_=va[:])], reads=[vk], writes=[("vxd", r0)], ndma=1)
                    pb = (tcount % 2) * 2
                    tcount += 1
                    if own:
                        for p_ in range(8):
                            P.op("pe", lambda e, p_=p_, ya=ya, pb=pb: e.transpose(out=pst[:, pb, p_ * 128:(p_ + 1) * 128], in_=ya[:, p_ * 128:(p_ + 1) * 128],
                                                                              identity=ident[:]), reads=[yk, "ident"], writes=["ps%d" % pb])
                        P.op("act", lambda e, qa=qa, pb=pb, s=s: e.activation(out=qa[:, :, s * 128:(s + 1) * 128],
                                                                          in_=pst[:, pb, :].rearrange("p (c t) -> p c t", t=128), func=AF.Copy),
                             reads=["ps%d" % pb], writes=[(qk_, s)])
                    for p_ in range(2):
                        P.op("pe", lambda e, p_=p_, ya=ya, pb=pb: e.transpose(out=pst[:, pb + 1, p_ * 128:(p_ + 1) * 128],
                                                                          in_=ya[:, 1024 + p_ * 128:1024 + (p_ + 1) * 128], identity=ident[:]),
                             reads=[yk, "ident"], writes=["ps%d" % (pb + 1)])
                    P.op("dve", lambda e, ka=ka, pb=pb, s=s: e.tensor_copy(out=ka[:, :, s * 128:(s + 1) * 128],
                                                                       in_=pst[:, pb + 1, 0:256].rearrange("p (c t) -> p c t", t=128)),
                         reads=["ps%d" % (pb + 1)], writes=[(kk_, s)])
                ns = n // 128
                if own:
                    P.op("pool", lambda e, qa=qa, oc=oc0 + r, n=n: [e.dma_start(out=self.qT.rearrange("(c p) t -> p c t", p=128)[:, :, oc:oc + n], in_=qa[:, :, 0:n])],
                         reads=[(qk_, s) for s in range(ns)], writes=[("qTd", oc0 + r)] + [(qk_, s) for s in range(ns)], ndma=1)
                P.op("pool", lambda e, ka=ka, rr=b0 + r, n=n: [e.dma_start(out=self.kT.rearrange("(c p) t -> p c t", p=128)[:, :, rr:rr + n], in_=ka[:, :, 0:n])],
                     reads=[(kk_, s) for s in range(ns)], writes=[("kTd", b0 + r)] + [(kk_, s) for s in range(ns)], ndma=1)
                r += n

    def phase_b2(self):
        nc, P, cfg = self.nc, self.P, self.cfg
        sb = self.salloc
        nchmax = cfg.rp // 128
        nch_tot = cfg.rtot // 128
        KT = sb("KT", [128, cfg.rp], BF16)
        VX = sb("VX", [128, nchmax, 320], BF16)
        QT = Rot("QT", [sb("QT%d" % i, [128, cfg.ls], BF16) for i in range(4)])
        PT = Rot("PT", [sb("PT%d" % i, [128, 1024], BF16) for i in range(3)])
        OT = Rot("OT", [sb("OT%d" % i, [128, 512], BF16) for i in range(2)])
        BC = Rot("BC", [sb("BC%d" % i, [128, 512], F32) for i in range(2)])
        RL = Rot("RL", [sb("RL%d" % i, [128, 512], F32) for i in range(2)])
        kb = sb("kb", [128, nch_tot], F32)
        sel = sb("sel", [128, 128], F32)
        gq = sb("gq", [128, 256], F32)
        gsm = sb("gsm", [128, 8], F32)
        P.dma(kb[:], self.kbias, writes=["kb"])
        P.dma(gq[:], self.qkg.rearrange("a d -> (a d)").partition_broadcast(128), writes=["gq"])
        P.op("dve", lambda e: e.tensor_reduce(out=gsm[:, 0:4], in_=gq[:].rearrange("p (a d) -> p a d", a=4), axis=AX.X, op=ALU.max,
                                              apply_absolute_value=True), reads=["gq"], writes=["gsm"])
        P.op("dve", lambda e: e.tensor_tensor(out=gsm[:, 4:5], in0=gsm[:, 0:1], in1=gsm[:, 2:3], op=ALU.mult), reads=["gsm"], writes=["gsm"])
        P.op("dve", lambda e: e.tensor_scalar(out=gsm[:, 5:6], in0=gsm[:, 4:5], scalar1=-8.0, scalar2=None, op0=ALU.mult), reads=["gsm"], writes=["gsm"])
        P.op("dve", lambda e: e.tensor_scalar(out=kb[:], in0=kb[:], scalar1=gsm[:, 5:6], scalar2=None, op0=ALU.add), reads=["gsm", "kb"], writes=["kb"])
        P.op("pool", lambda e: e.memset(sel[:], 1.0), writes=["sel"])
        w2 = self.phase_w_gen(None, ("wout", "g2", "u2", "d2"), 2, True)
        n_w2 = sum(-(-(self.wf[k].shape[0] * self.wf[k].shape[1] // 128) // 4096) for k in ("wout", "g2", "u2", "d2"))
        n_steps_tot = sum(2 * 4 * (cfg.ls // 512) * (nr // 128) for (_, nr, _) in self.blocks)
        w2_every = max(1, n_steps_tot // (n_w2 + 2))
        self._w2cnt = 0
        for bi, (b0, nrows, oc0) in enumerate(self.blocks):
            nch = nrows // 128
            for kp in range(2):
                P.dma(KT[:, 0:nrows], self.kT[kp * 128:(kp + 1) * 128, b0:b0 + nrows], writes=["KT"])
                P.dma(VX[:, 0:nch, :], self.vx[b0:b0 + nrows, kp * 320:(kp + 1) * 320].rearrange("(c p) w -> p c w", p=128), writes=["VX"])
                steps = []
                for i in range(4):
                    p_ = kp * 4 + i
                    qk_, qa = QT.next()
                    P.dma(qa[:], self.qT[p_ * 128:(p_ + 1) * 128, oc0:oc0 + cfg.ls], writes=[qk_])
                    for qb in range(cfg.ls // 512):
                        for ch in range(nch):
                            steps.append((p_, qk_, qa, qb, ch))

                def emit_s(idx, st):
                    p_, qk_, qa, qb, ch = st
                    sbk = (idx % 2) * 2
                    for hf in range(2):
                        lo = hf * 64
                        P.op("pe", lambda e, lo=lo, ch=ch, qa=qa, qb=qb, b=sbk + hf: e.matmul(
                            self.psum[:, b, :], lhsT=KT[lo:lo + 64, ch * 128:(ch + 1) * 128], rhs=qa[lo:lo + 64, qb * 512:(qb + 1) * 512],
                            start=True, stop=True), reads=["KT", qk_], writes=["ps%d" % (sbk + hf)])

                emit_s(0, steps[0])
                for idx, st in enumerate(steps):
                    p_, qk_, qa, qb, ch = st
                    self._w2cnt += 1
                    if self._w2cnt % w2_every == 0:
                        next(w2, None)
                    if idx + 1 < len(steps):
                        emit_s(idx + 1, steps[idx + 1])
                    sbk = (idx % 2) * 2
                    pk_, pa = PT.next()
                    gch = b0 // 128 + ch
                    P.op("act", lambda e, pa=pa, sbk=sbk, gch=gch: e.activation(
                        out=pa[:], in_=self.psum[:, sbk:sbk + 2, :].rearrange("p b f -> p (b f)"), func=AF.Exp, bias=kb[:, gch:gch + 1], scale=1.0),
                        reads=["ps%d" % sbk, "ps%d" % (sbk + 1), "kb"], writes=[pk_])
                    for hf in range(2):
                        w0 = 64 if hf == 0 else 128
                        P.op("pe", lambda e, pa=pa, ch=ch, hf=hf, w0=w0, nch=nch: e.matmul(
                            self.psum[:, 4 + hf, :], lhsT=VX[:, ch, w0:w0 + 128], rhs=pa[:, hf * 512:(hf + 1) * 512], start=(ch == 0), stop=(ch == nch - 1)),
                            reads=["VX", pk_], writes=["ps%d" % (4 + hf)])
                    if ch == nch - 1:
                        rk_, rl = RL.next()
                        P.op("dve", lambda e, rl=rl: e.reciprocal(out=rl[64:65, :], in_=self.psum[64:65, 4, :]), reads=["ps4"], writes=[(rk_, 0)])
                        P.op("dve", lambda e, rl=rl: e.reciprocal(out=rl[0:1, :], in_=self.psum[0:1, 5, :]), reads=["ps5"], writes=[(rk_, 1)])
                        P.op("pe", lambda e, rl=rl: e.matmul(self.psum[:, 6, :], lhsT=sel[64:65, :], rhs=rl[64:65, :], start=True, stop=True),
                             reads=[(rk_, 0), "sel"], writes=["ps6"])
                        P.op("pe", lambda e, rl=rl: e.matmul(self.psum[:, 7, :], lhsT=sel[0:1, :], rhs=rl[0:1, :], start=True, stop=True),
                             reads=[(rk_, 1), "sel"], writes=["ps7"])
                        bk_, bc = BC.next()
                        P.op("act", lambda e, bc=bc: e.activation(out=bc[0:64, :], in_=self.psum[0:64, 6, :], func=AF.Copy), reads=["ps6"], writes=[(bk_, 0)])
                        P.op("act", lambda e, bc=bc: e.activation(out=bc[64:128, :], in_=self.psum[64:128, 7, :], func=AF.Copy), reads=["ps7"], writes=[(bk_, 1)])
                        ok_, ot = OT.next()
                        P.op("dve", lambda e, ot=ot, bc=bc: e.tensor_tensor(out=ot[0:64, :], in0=self.psum[0:64, 4, :], in1=bc[0:64, :], op=ALU.mult),
                             reads=["ps4", (bk_, 0)], writes=[(ok_, 0)])
                        P.op("dve", lambda e, ot=ot, bc=bc: e.tensor_tensor(out=ot[64:128, :], in0=self.psum[64:128, 5, :], in1=bc[64:128, :], op=ALU.mult),
                             reads=["ps5", (bk_, 1)], writes=[(ok_, 1)])
                        P.op("pool", lambda e, ot=ot, p_=p_, cc=oc0 + qb * 512: [e.dma_start(out=self.oT[p_ * 128:(p_ + 1) * 128, cc:cc + 512], in_=ot[:])],
                             reads=[(ok_, 0), (ok_, 1)], writes=[("oTd", p_, oc0 + qb * 512), (ok_, 0), (ok_, 1)], ndma=1)

        for _ in w2:
            pass

    def phase_b3(self):
        nc, P, cfg = self.nc, self.P, self.cfg
        sb = self.salloc
        nch_tot = cfg.rtot // 128
        nown = cfg.ls // 128
        cst = sb("cstt", [128, 5, 128], F32)
        ones = sb("ones", [128, 128], F32)
        ident = sb("ident", [128, 128], BF16)
        Aexp = sb("Aexp", [128, 32], F32)
        dtb = sb("dtb", [128, 32], F32)
        dsk = sb("dsk", [128, 1024], F32)
        gss = sb("gss", [128, 1024], F32)
        keep = sb("keep", [128, 2, nch_tot], F32)
        dtm = sb("dtm", [128, nch_tot], F32)
        nh = sb("nh", [128, 1], F32)
        S = [sb("S%d" % d, [128, 1024], F32) for d in range(2)]
        Sb = Rot("Sb", [sb("Sb%d" % i, [128, 1024], BF16) for i in range(2)])
        xsb = Rot("xsb", [sb("xsb%d" % i, [128, 1280], BF16) for i in range(3)])
        bct = Rot("bct", [sb("bct%d" % i, [128, 4, 128], BF16) for i in range(2)])
        zt = Rot("zt", [sb("zt%d" % i, [128, 1024], F32) for i in range(1)])
        xw = Rot("xw", [sb("xw%d" % i, [128, 1024], BF16) for i in range(2)])
        xdt = Rot("xdt", [sb("xdt%d" % i, [128, 1024], BF16) for i in range(2)])
        AU = Rot("AU", [sb("AU%d" % i, [128, 16, 128], F32) for i in range(1)])
        Mexp = sb("Mexp", [128, 16, 128], F32)
        MT = Rot("MT", [sb("MT%d" % i, [128, 16, 128], BF16) for i in range(2)])
        CBm = sb("CBm", [128, 2, 128], F32)
        tmpy = Rot("tmpy", [sb("tmpy%d" % i, [128, 1024], F32) for i in range(2)])
        yacc = sb("yacc", [128, nown, 1024], F32)
        f1 = sb("f1", [128, 1024], F32)
        f2 = sb("f2", [128, 1024], F32)
        fo = Rot("fo", [sb("fo%d" % i, [128, 1024], BF16) for i in range(2)])
        ost = Rot("ost", [sb("ost%d" % i, [128, 8, 128], BF16) for i in range(2)])
        fsm = Rot("fsm", [sb("fsm%d" % i, [128, 8], F32) for i in range(2)])
        TRI = [cst[:, 1, :], cst[:, 2, :]]
        UU = [cst[:, 3, :], cst[:, 4, :]]
        ps = self.psum
        P.dma(cst[:], self.cst, writes=["cst"])
        P.op("dve", lambda e: e.tensor_copy(out=ident[:], in_=cst[:, 0, :]), reads=["cst"], writes=["ident"])
        P.op("pool", lambda e: e.memset(ones[:], 1.0), writes=["ones"])
        P.op("pool", lambda e: e.memset(nh[:], -0.5), writes=["nh"])
        P.dma(Aexp[:], self.a_log.partition_broadcast(128), writes=["Aexp"])
        P.op("act", lambda e: e.activation(out=Aexp[:], in_=Aexp[:], func=AF.Exp), reads=["Aexp"], writes=["Aexp"])
        P.op("dve", lambda e: e.tensor_scalar(out=Aexp[:], in0=Aexp[:], scalar1=-1.0, scalar2=None, op0=ALU.mult), reads=["Aexp"], writes=["Aexp"])
        P.dma(dtb[:], self.dt_bias.partition_broadcast(128), writes=["dtb"])
        P.dma(dsk[:], self.dskip.partition_broadcast(128), writes=["dsk"])
        P.dma(gss[:], self.ssm_norm.partition_broadcast(128), writes=["gss"])
        P.dma(keep[:], self.keep, writes=["keep"])
        P.dma(dtm[:], self.dtmask, writes=["dtm"])

        nbmax = cfg.rp // 128
        BT = {k: sb("BT_" + k, [128, nbmax, 16], F32) for k in ("dt", "aa", "ecum", "etot", "ww")}

        def precompute(b0, nchb, d):
            g0 = b0 // 128
            DT, AA, EC, ET, WW = (BT[k][:, 0:nchb, :] for k in ("dt", "aa", "ecum", "etot", "ww"))
            bc_h = lambda ap: ap.unsqueeze(1).broadcast_to([128, nchb, 16])
            bc_c = lambda ap: ap.unsqueeze(2).broadcast_to([128, nchb, 16])
            P.dma(DT, self.proj[b0:b0 + nchb * 128, 2560 + d * 16:2576 + d * 16].rearrange("(c p) f -> p c f", p=128), writes=["BTdt"])
            P.op("dve", lambda e: e.tensor_tensor(out=AA, in0=DT, in1=bc_h(dtb[:, d * 16:(d + 1) * 16]), op=ALU.add), reads=["BTdt", "dtb"], writes=["BTaa"])
            P.op("dve", lambda e: e.scalar_tensor_tensor(out=EC, in0=AA, scalar=-1.0, in1=AA, op0=ALU.mult, op1=ALU.min), reads=["BTaa"], writes=["BTecum"])
            P.op("act", lambda e: e.activation(out=EC, in_=EC, func=AF.Exp), reads=["BTecum"], writes=["BTecum"])
            P.op("act", lambda e: e.activation(out=EC, in_=EC, func=AF.Ln, bias=1.0), reads=["BTecum"], writes=["BTecum"])
            P.op("dve", lambda e: e.scalar_tensor_tensor(out=DT, in0=AA, scalar=0.0, in1=EC, op0=ALU.max, op1=ALU.add), reads=["BTaa", "BTecum"], writes=["BTdt"])
            P.op("dve", lambda e: e.tensor_tensor(out=DT, in0=DT, in1=bc_c(dtm[:, g0:g0 + nchb]), op=ALU.mult), reads=["BTdt", "dtm"], writes=["BTdt"])
            P.op("dve", lambda e: e.tensor_tensor(out=AA, in0=DT, in1=bc_h(Aexp[:, d * 16:(d + 1) * 16]), op=ALU.mult), reads=["BTdt", "Aexp"], writes=["BTaa"])
            for c0 in range(0, nchb, 32):
                n = min(32, nchb - c0)
                rhs = BT["aa"][:, c0:c0 + n, :].rearrange("p c h -> p (c h)")
                P.op("pe", lambda e, rhs=rhs, n=n: e.matmul(ps[:, 0, 0:n * 16], lhsT=TRI[d], rhs=rhs, start=True, stop=True), reads=["BTaa", "cst"], writes=["ps0"])
                P.op("pe", lambda e, rhs=rhs, n=n: e.matmul(ps[:, 1, 0:n * 16], lhsT=ones[:], rhs=rhs, start=True, stop=True), reads=["BTaa", "ones"], writes=["ps1"])
                P.op("act", lambda e, c0=c0, n=n: e.activation(out=BT["ecum"][:, c0:c0 + n, :].rearrange("p c h -> p (c h)"), in_=ps[:, 0, 0:n * 16], func=AF.Copy),
                     reads=["ps0"], writes=["BTecum"])
                P.op("dve", lambda e, c0=c0, n=n: e.tensor_copy(out=BT["etot"][:, c0:c0 + n, :].rearrange("p c h -> p (c h)"), in_=ps[:, 1, 0:n * 16]),
                     reads=["ps1"], writes=["BTetot"])
            P.op("dve", lambda e: e.tensor_tensor(out=WW, in0=ET, in1=EC, op=ALU.subtract), reads=["BTetot", "BTecum"], writes=["BTww"])
            P.op("act", lambda e: e.activation(out=WW, in_=WW, func=AF.Exp), reads=["BTww"], writes=["BTww"])
            P.op("dve", lambda e: e.tensor_tensor(out=WW, in0=WW, in1=DT, op=ALU.mult), reads=["BTww", "BTdt"], writes=["BTww"])
            P.op("act", lambda e: e.activation(out=EC, in_=EC, func=AF.Exp), reads=["BTecum"], writes=["BTecum"])
            P.op("act", lambda e: e.activation(out=ET, in_=ET, func=AF.Exp), reads=["BTetot"], writes=["BTetot"])
            P.op("dve", lambda e: e.tensor_tensor(out=ET, in0=ET, in1=bc_c(keep[:, d, g0:g0 + nchb]), op=ALU.mult), reads=["BTetot", "keep"], writes=["BTetot"])

        def step(r0, gch, lch, d, with_y, slot, first_pass):
            xk, xa = xsb.next()
            P.dma(xa[:], self.xsB[r0:r0 + 128, :], writes=[xk])
            dt = BT["dt"][:, lch, :]
            aa = BT["aa"][:, lch, :]
            ww = BT["ww"][:, lch, :]
            ecum = BT["ecum"][:, lch, :]
            etot = BT["etot"][:, lch, :]
            wkk = "BTww"
            xwk, xwa = xw.next()
            xs3 = xa[:, 0:1024].rearrange("p (h d) -> p h d", d=64)
            P.op("dve", lambda e: e.tensor_tensor(out=xwa[:].rearrange("p (h d) -> p h d", d=64), in0=xs3,
                                                  in1=ww.unsqueeze(2).broadcast_to([128, 16, 64]), op=ALU.mult), reads=[xk, wkk], writes=[xwk])
            Sk = "S%d" % d
            if with_y:
                sbk, sba = Sb.next()
                P.op("act", lambda e: e.activation(out=sba[:], in_=S[d][:], func=AF.Identity, scale=keep[:, d, gch:gch + 1]), reads=[Sk, "keep"], writes=[sbk])
                xdk, xda = xdt.next()
                P.op("pool", lambda e: e.tensor_tensor(out=xda[:].rearrange("p (h d) -> p h d", d=64), in0=xs3,
                                                       in1=dt.unsqueeze(2).broadcast_to([128, 16, 64]), op=ALU.mult), reads=[xk, "BTdt"], writes=[xdk])
                bk, ba = bct.next()
                P.dma(ba[:], self.bcT.rearrange("(c p) t -> p c t", p=128)[:, :, r0:r0 + 128], writes=[bk])
                for g in range(2):
                    P.op("pe", lambda e, g=g: e.matmul(ps[:, 1, g * 128:(g + 1) * 128], lhsT=ba[:, g, :], rhs=ba[:, 2 + g, :], start=True, stop=True),
                         reads=[bk], writes=["ps1"])
                P.op("dve", lambda e: e.tensor_tensor(out=CBm[:], in0=ps[:, 1, 0:256].rearrange("p (g l) -> p g l", g=2),
                                                      in1=TRI[d].unsqueeze(1).broadcast_to([128, 2, 128]), op=ALU.mult), reads=["ps1", "cst"], writes=["CBm"])
                auk, aua = AU.next()
                P.op("pool", lambda e: e.tensor_tensor(out=aua[:], in0=UU[d].unsqueeze(1).broadcast_to([128, 16, 128]),
                                                       in1=aa.unsqueeze(2).broadcast_to([128, 16, 128]), op=ALU.mult), reads=["BTaa", "cst"], writes=[auk])
                for h in range(16):
                    P.op("pe", lambda e, h=h: e.matmul(ps[:, 4 + h // 4, (h % 4) * 128:(h % 4 + 1) * 128], lhsT=aua[:, h, :], rhs=TRI[d], start=True, stop=True),
                         reads=[auk, "cst"], writes=["ps%d" % (4 + h // 4)])
                P.op("act", lambda e: e.activation(out=Mexp[:].rearrange("p h l -> p (h l)"), in_=ps[:, 4:8, :].rearrange("p b f -> p (b f)"), func=AF.Exp),
                     reads=["ps4", "ps5", "ps6", "ps7"], writes=["Mexp"])
                mk, ma = MT.next()
                for g in range(2):
                    P.op("dve", lambda e, g=g: e.tensor_tensor(out=ma[:, g * 8:(g + 1) * 8, :], in0=Mexp[:, g * 8:(g + 1) * 8, :],
                                                           in1=CBm[:, g, :].unsqueeze(1).broadcast_to([128, 8, 128]), op=ALU.mult),
                         reads=["Mexp", "CBm"], writes=[(mk, g)])
                for g in range(2):
                    P.op("pe", lambda e, g=g: e.matmul(ps[:, 4 + g, :], lhsT=ba[:, 2 + g, :], rhs=sba[:, g * 512:(g + 1) * 512], start=True, stop=True),
                         reads=[bk, sbk], writes=["ps%d" % (4 + g)])
                for h in range(16):
                    P.op("pe", lambda e, h=h: e.matmul(ps[:, 6 + h // 8, (h % 8) * 64:(h % 8 + 1) * 64], lhsT=ma[:, h, :], rhs=xda[:, h * 64:(h + 1) * 64],
                                                     start=True, stop=True), reads=[(mk, h // 8), xdk], writes=["ps%d" % (6 + h // 8)])
                tk_, ta = tmpy.next()
                P.op("dve", lambda e: e.tensor_tensor(out=ta[:].rearrange("p (h d) -> p h d", d=64),
                                                      in0=ps[:, 4:6, :].rearrange("p b (h d) -> p (b h) d", d=64),
                                                      in1=ecum.unsqueeze(2).broadcast_to([128, 16, 64]), op=ALU.mult),
                     reads=["ps4", "ps5", "BTecum"], writes=[tk_])
                ydst = yacc[:, slot, :] if first_pass else ta[:]
                P.op("dve", lambda e: e.tensor_tensor(out=ydst, in0=ta[:], in1=ps[:, 6:8, :].rearrange("p b f -> p (b f)"), op=ALU.add),
                     reads=[tk_, "ps6", "ps7"], writes=[("yacc", slot)] if first_pass else [tk_])
                if not first_pass:
                    P.op("pool", lambda e: e.tensor_tensor(out=yacc[:, slot, :], in0=yacc[:, slot, :], in1=ta[:], op=ALU.add),
                         reads=[tk_, ("yacc", slot)], writes=[("yacc", slot)])
            for g in range(2):
                P.op("pe", lambda e, g=g: e.matmul(ps[:, 2 + g, :], lhsT=xa[:, 1024 + g * 128:1152 + g * 128], rhs=xwa[:, g * 512:(g + 1) * 512],
                                                 start=True, stop=True), reads=[xk, xwk], writes=["ps%d" % (2 + g)])
            P.op("dve", lambda e: e.tensor_tensor(out=S[d][:].rearrange("p (h d) -> p h d", d=64), in0=S[d][:].rearrange("p (h d) -> p h d", d=64),
                                                   in1=etot.unsqueeze(2).broadcast_to([128, 16, 64]), op=ALU.mult), reads=[Sk, "BTetot"], writes=[Sk])
            P.op("dve", lambda e: e.tensor_tensor(out=S[d][:], in0=S[d][:], in1=ps[:, 2:4, :].rearrange("p b f -> p (b f)"), op=ALU.add),
                 reads=[Sk, "ps2", "ps3"], writes=[Sk])
            return xk, xa

        def finalize(r0, slot, oc, xk, xa):
            zk, za = zt.next()
            P.dma(za[:], self.proj[r0:r0 + 128, 1536:2560], writes=[zk])
            P.op("dve", lambda e: e.tensor_tensor(out=f1[:], in0=xa[:, 0:1024], in1=dsk[:], op=ALU.mult), reads=[xk, "dsk"], writes=["f1"])
            P.op("dve", lambda e: e.tensor_tensor(out=f1[:], in0=f1[:], in1=yacc[:, slot, :], op=ALU.add), reads=["f1", ("yacc", slot)], writes=["f1"])
            P.op("act", lambda e: e.activation(out=f2[:], in_=za[:], func=AF.Exp, scale=-1.0), reads=[zk], writes=["f2"])
            P.op("act", lambda e: e.activation(out=f2[:], in_=f2[:], func=AF.Ln, bias=1.0), reads=["f2"], writes=["f2"])
            P.op("act", lambda e: e.activation(out=f2[:], in_=f2[:], func=AF.Exp, scale=-1.0), reads=["f2"], writes=["f2"])
            P.op("pool", lambda e: e.tensor_tensor(out=f2[:], in0=f2[:], in1=za[:], op=ALU.mult), reads=["f2", zk], writes=["f2"])
            P.op("dve", lambda e: e.tensor_tensor(out=f1[:], in0=f1[:], in1=f2[:], op=ALU.mult), reads=["f1", "f2"], writes=["f1"])
            fk, fs = fsm.next()
            for g in range(2):
                P.op("act", lambda e, g=g: e.activation(out=f2[:, g * 512:(g + 1) * 512], in_=f1[:, g * 512:(g + 1) * 512], func=AF.Square,
                                                      accum_out=fs[:, g:g + 1]), reads=["f1"], writes=["f2", (fk, g)])
            P.op("dve", lambda e: e.tensor_scalar(out=fs[:, 2:4], in0=fs[:, 0:2], scalar1=1.0 / 512, scalar2=EPS, op0=ALU.mult, op1=ALU.add),
                 reads=[(fk, 0), (fk, 1)], writes=[(fk, 2)])
            P.op("pool", lambda e: e.tensor_tensor(out=fs[:, 4:6], in0=fs[:, 2:4], in1=nh[:, 0:1].broadcast_to([128, 2]), op=ALU.pow),
                 reads=[(fk, 2), "nh"], writes=[(fk, 3)])
            ok_, oa = fo.next()
            for g in range(2):
                P.op("dve", lambda e, g=g: e.scalar_tensor_tensor(out=oa[:, g * 512:(g + 1) * 512], in0=f1[:, g * 512:(g + 1) * 512], scalar=fs[:, 4 + g:5 + g],
                                                                in1=gss[:, g * 512:(g + 1) * 512], op0=ALU.mult, op1=ALU.mult),
                     reads=["f1", (fk, 3), "gss"], writes=[(ok_, g)])
            pst = ps[:, 1, :].bitcast(BF16)
            for c in range(8):
                P.op("pe", lambda e, c=c: e.transpose(out=pst[:, c * 128:(c + 1) * 128], in_=oa[:, c * 128:(c + 1) * 128], identity=ident[:]),
                     reads=[(ok_, c // 4), "ident"], writes=["ps1"])
            sk_, sa = ost.next()
            P.op("act", lambda e: e.activation(out=sa[:].rearrange("p c t -> p (c t)"), in_=pst, func=AF.Copy), reads=["ps1"], writes=[sk_])
            P.op("pool", lambda e: [e.dma_start(out=self.oT[1024:2048, :].rearrange("(c p) t -> p c t", p=128)[:, :, oc:oc + 128], in_=sa[:])],
                 reads=[sk_], writes=[("oTs", oc), sk_], ndma=1)

        for (b0, nrows, oc0) in self.blocks:
            nchb = nrows // 128
            for d in range(2):
                P.op("pool", lambda e, d=d: e.memset(S[d][:], 0.0), reads=["S%d" % d], writes=["S%d" % d])
            others = list(range(nown, nchb))
            precompute(b0, nchb, 0)
            for ch in others:
                step(b0 + ch * 128, b0 // 128 + ch, ch, 0, False, None, True)
            for ch in range(nown):
                step(b0 + ch * 128, b0 // 128 + ch, ch, 0, True, ch, True)
            precompute(b0, nchb, 1)
            for ch in reversed(others):
                step(b0 + ch * 128, b0 // 128 + ch, ch, 1, False, None, False)
            for ch in reversed(range(nown)):
                xk, xa = step(b0 + ch * 128, b0 // 128 + ch, ch, 1, True, ch, False)
                finalize(b0 + ch * 128, ch, oc0 + ch * 128, xk, xa)

    def build(self):
        nc, P, cfg = self.nc, self.P, self.cfg
        self.declare()
        with contextlib.ExitStack() as top:
            self.psum = top.enter_context(nc.psum_tensor("psum", [128, 8, 512], F32))
            self.sb_words = 52992
            self.SB = top.enter_context(nc.sbuf_tensor("SB", [128, self.sb_words], F32))
            if cfg.do_w:
                with contextlib.ExitStack() as st:
                    self.sb_off = 0
                    self.phase_w(st, names=("g1", "u1", "d1", "win") if cfg.do_b else None)
                    P.barrier()
            if cfg.do_a:
                with contextlib.ExitStack() as st:
                    self.sb_off = 0
                    self.phase_a(st)
                    P.barrier()
            if cfg.do_b:
                for ph in cfg.b_phases:
                    self.sb_off = 0
                    getattr(self, "phase_" + ph)()
                    P.barrier()
            if cfg.do_c:
                with contextlib.ExitStack() as st:
                    self.sb_off = 0
                    self.phase_c(st)
                    P.barrier()
            P.emit()
        return nc


def _slot_heads():
    order = []
    for p in range(8):
        if p < 4:
            order += [p, 4 + p]
        else:
            order += [8 + (p - 4), 12 + (p - 4)]
    return order


def _partner():
    d = np.arange(64)
    a, b, i = d // 32, (d // 16) % 2, d % 16
    return a * 32 + (1 - b) * 16 + i


def _rope_rows(tok):
    f32 = np.float32
    inv = (f32(10000.0) ** (-(np.arange(16, dtype=f32) / f32(16)))).astype(f32)
    t = np.maximum(tok, 0)
    row = (t // 64).astype(f32)
    col = (t % 64).astype(f32)
    ang = np.stack([row[:, None] * inv, col[:, None] * inv], axis=1).astype(f32)
    ang = np.where((tok >= 0)[:, None, None], ang, f32(0))
    c = np.cos(ang).astype(f32)
    s = np.sin(ang).astype(f32)
    cos = np.broadcast_to(c[:, :, None, :], (len(tok), 2, 2, 16)).reshape(len(tok), 64)
    sgn = np.array([-1.0, 1.0], dtype=f32)[None, None, :, None]
    sin = (np.broadcast_to(s[:, :, None, :], (len(tok), 2, 2, 16)) * sgn).reshape(len(tok), 64)
    return np.concatenate([cos, sin], axis=1).astype(f32)


def prep_inputs(cfg, inp):
    f32 = np.float32
    NS, LS = cfg.ncores, cfg.ls
    xp = np.asarray(inp["x_prompt"], f32)[0]
    xs = np.asarray(inp["x_sample"], f32)
    meta = np.asarray(inp["meta_tokens"], f32)
    sq = lambda k: np.ascontiguousarray(np.asarray(inp[k], f32)[0])
    w_in = sq("w_in")
    heads = _slot_heads()
    qcols = np.concatenate([np.arange(h * 64, (h + 1) * 64) for h in heads])
    perm = np.concatenate([qcols, np.arange(1024, 1536), np.arange(1536, 2560), np.arange(4096, 4128), np.arange(2560, 4096)])
    w_out = sq("w_out")
    shared = {
        "ff1_wg": sq("ff1_w_gate"), "ff1_wu": sq("ff1_w_up"), "ff1_wd": sq("ff1_w_down"),
        "ff2_wg": sq("ff2_w_gate"), "ff2_wu": sq("ff2_w_up"), "ff2_wd": sq("ff2_w_down"),
        "w_in_p": np.ascontiguousarray(w_in[:, perm]),
        "w_out_p": np.ascontiguousarray(np.concatenate([w_out[qcols], w_out[1024:]], axis=0)),
        "ff1_pre": sq("ff1_norm_pre"), "ff1_post": sq("ff1_norm_post"), "mix_pre": sq("mix_norm_pre"),
        "mix_post": sq("mix_norm_post"), "ff2_pre": sq("ff2_norm_pre"), "ff2_post": sq("ff2_norm_post"),
        "ident": np.eye(128, dtype=f32),
        "conv_w": sq("conv_w"), "conv_b": sq("conv_b"),
        "a_log": sq("a_log").reshape(32), "dt_bias": sq("dt_bias").reshape(32),
        "dskip": np.repeat(sq("d_skip"), 64), "ssm_norm": sq("ssm_norm"),
    }
    pt = _partner()
    qn, kn = sq("q_norm"), sq("k_norm")
    shared["qkg"] = np.stack([qn, qn[pt], kn, kn[pt]]).astype(f32)
    i = np.arange(128)
    tri_f = (i[:, None] <= i[None, :]).astype(f32)
    tri_b = (i[:, None] >= i[None, :]).astype(f32)
    u_f = (i[:, None] > i[None, :]).astype(f32)
    u_b = (i[:, None] < i[None, :]).astype(f32)
    shared["cst"] = np.ascontiguousarray(np.stack([np.eye(128, dtype=f32), tri_f, tri_b, u_f, u_b], axis=1))
    seam_x = np.concatenate([np.zeros((112, D), f32), meta], axis=0)
    seam_tok = np.full(128, -1)
    seam_valid = np.concatenate([np.zeros(112), np.ones(16)])
    maps = []
    for c in range(NS):
        own = np.arange(c * LS, (c + 1) * LS)
        after = np.arange((c + 1) * LS, NS * LS)
        before = np.arange(0, c * LS)
        xin = np.concatenate([xp[own], xp[after], seam_x, xp[before], xs[c], seam_x], axis=0)
        tok = np.concatenate([own, after, seam_tok, before, np.arange(LS), seam_tok])
        valid = np.concatenate([np.ones(LS + len(after)), seam_valid, np.ones(len(before)), np.ones(LS), seam_valid])
        assert xin.shape[0] == cfg.rtot
        nch = cfg.rtot // 128
        lay = lambda v: np.ascontiguousarray(v.reshape(nch, 128).T.astype(f32))
        seam_ch = (LS + len(after)) // 128
        keepf = np.ones(nch, f32)
        keepb = np.ones(nch, f32)
        keepf[seam_ch] = 0
        keepb[seam_ch - 1] = 0
        s_seam = cfg.rs0 // 128 + LS // 128
        keepf[s_seam] = 0
        keepb[s_seam - 1] = 0
        m = dict(shared)
        m["xin"] = np.ascontiguousarray(xin)
        m["ropet"] = _rope_rows(tok)
        m["kbias"] = lay((1 - valid) * -30000.0)
        m["dtmask"] = lay(valid)
        m["keep"] = np.ascontiguousarray(np.stack([np.broadcast_to(keepf, (128, nch)), np.broadcast_to(keepb, (128, nch))], axis=1).astype(f32))
        maps.append(m)
    return maps


_CACHE = {}


def kernel(**inputs):
    cfg = Cfg()
    if "nc" not in _CACHE:
        _CACHE["nc"] = Builder(cfg).build()
    nc = _CACHE["nc"]
    maps = prep_inputs(cfg, inputs)
    res = run_bass_kernel_spmd(nc, maps, core_ids=list(range(cfg.ncores)))
    ys = [np.asarray(r["yout"]) for r in res.results]
    y_prompt = np.concatenate([y[0:cfg.ls] for y in ys], axis=0)[None]
    y_sample = np.stack([y[cfg.ls:2 * cfg.ls] for y in ys], axis=0)
    return (y_prompt.astype(np.float32), y_sample.astype(np.float32))
```
